# Optimizing a Trainium2 kernel written in Bass

```python
import math
import jax, jax.numpy as jnp
from jax import lax
import numpy as np

D_MODEL = 1024
BATCH = 8
SEQ = 2048
DEPTH = 4

HEAD_DIM = 64
MOBA_HEADS = 8
MOBA_BLOCK = 256
MOBA_TOPK = 3
MOBA_QCHUNK = 16
DIFF_HEADS = 4
DIFF_HEAD_DIM = 64
DIL_HEADS = 8
DIL_CONFIGS = ((128, 1), (512, 4), (2048, 16))
DIL_BLOCK = 128
MLA_HEADS = 4
MLA_Q_RANK = 256
MLA_KV_RANK = 128
MLA_NOPE_DIM = 128
MLA_ROPE_DIM = 64
MLA_V_DIM = 128
ROPE_THETA = 10000.0
Q_BLOCK = 128
REL_BUCKETS = 32
REL_MAX_DIST = 1024
N_BIAS_HEADS = MOBA_HEADS + DIFF_HEADS + DIL_HEADS
D_FF = 2816
CONV_WIDTH = 3
EPS = 1e-6
NEG = -1e30
N_EVEN = (DEPTH + 1) // 2
N_ODD = DEPTH // 2
MOBA_W = MOBA_HEADS * HEAD_DIM
DIFF_QK_W = DIFF_HEADS * 2 * DIFF_HEAD_DIM
DIFF_V_W = DIFF_HEADS * 2 * DIFF_HEAD_DIM
EVEN_IN_SPLIT = (MOBA_W, MOBA_W, MOBA_W, DIFF_QK_W, DIFF_QK_W, DIFF_V_W)
EVEN_IN = sum(EVEN_IN_SPLIT)
EVEN_MIX = MOBA_W + DIFF_V_W
DIL_W = DIL_HEADS * HEAD_DIM
ODD_IN_SPLIT = (DIL_W, DIL_W, DIL_W, MLA_Q_RANK, MLA_KV_RANK, MLA_ROPE_DIM)
ODD_IN = sum(ODD_IN_SPLIT)
ODD_MIX = DIL_W + MLA_HEADS * MLA_V_DIM

kernel_name = 'hybrid_moba_diff_dilated_mla_convffn'


def rmsnorm(x, g):
    xf = x.astype(jnp.float32)
    y = xf * lax.rsqrt(jnp.mean(xf * xf, axis=-1, keepdims=True) + EPS)
    return (y * g.astype(jnp.float32)).astype(x.dtype)


def split_cols(x, sizes):
    outs, off = [], 0
    for s in sizes:
        outs.append(x[..., off:off + s])
        off += s
    return outs


def heads(x, n, d):
    b, s, _ = x.shape
    return x.reshape(b, s, n, d).transpose(0, 2, 1, 3)


def merge_heads(o):
    b, h, s, d = o.shape
    return o.transpose(0, 2, 1, 3).reshape(b, s, h * d)


def rel_bucket(dist):
    max_exact = REL_BUCKETS // 2
    n = jnp.maximum(dist, 0)
    nf = jnp.maximum(n, 1).astype(jnp.float32)
    large = max_exact + (jnp.log(nf / max_exact) / math.log(REL_MAX_DIST / max_exact)
                         * (REL_BUCKETS - max_exact)).astype(jnp.int32)
    large = jnp.minimum(large, REL_BUCKETS - 1)
    return jnp.where(n < max_exact, n, large)


def rope(x, pos):
    half = x.shape[-1] // 2
    freq = ROPE_THETA ** (-jnp.arange(half, dtype=jnp.float32) / half)
    ang = pos.astype(jnp.float32)[:, None] * freq[None, :]
    cos, sin = jnp.cos(ang), jnp.sin(ang)
    xf = x.astype(jnp.float32)
    x1, x2 = xf[..., :half], xf[..., half:]
    return jnp.concatenate([x1 * cos - x2 * sin, x1 * sin + x2 * cos], axis=-1).astype(x.dtype)


def moba_attention(q, k, v, bias_table):
    B, H, S, dh = q.shape
    blk = MOBA_BLOCK
    sp = -(-S // blk) * blk
    if sp != S:
        padw = ((0, 0), (0, 0), (0, sp - S), (0, 0))
        q, k, v = jnp.pad(q, padw), jnp.pad(k, padw), jnp.pad(v, padw)
    nb = sp // blk
    n_sel = min(MOBA_TOPK, nb - 1)
    scale = dh ** -0.5
    kb = k.reshape(B, H, nb, blk, dh)
    vb = v.reshape(B, H, nb, blk, dh)
    qblk = jnp.arange(sp) // blk
    table_t = bias_table.T
    head_ix = jnp.arange(H)[None, :, None, None, None]
    if n_sel > 0:
        kmean = jnp.mean(kb.astype(jnp.float32), axis=3)
        gate = jnp.einsum('bhsd,bhnd->bhsn', q.astype(jnp.float32), kmean)
        fully_past = jnp.arange(nb)[None, :] < qblk[:, None]
        gate = jnp.where(fully_past, gate, -jnp.inf)
        _, sel = lax.top_k(gate, n_sel)
        sel_ok = sel < qblk[:, None]
    gather_blocks = jax.vmap(jax.vmap(lambda xb, ix: xb[ix]))

    def chunk(c):
        start = c * MOBA_QCHUNK
        qc = lax.dynamic_slice_in_dim(q, start, MOBA_QCHUNK, axis=2)
        qpos = start + jnp.arange(MOBA_QCHUNK)
        own = start // blk
        k_own = lax.dynamic_index_in_dim(kb, own, axis=2, keepdims=False)
        v_own = lax.dynamic_index_in_dim(vb, own, axis=2, keepdims=False)
        dist_own = qpos[:, None] - (own * blk + jnp.arange(blk))[None, :]
        b_own = jnp.moveaxis(jnp.take(bias_table, rel_bucket(dist_own), axis=0), -1, 0)
        s_own = jnp.einsum('bhqd,bhkd->bhqk', qc, k_own).astype(jnp.float32) * scale + b_own
        s_own = jnp.where(dist_own >= 0, s_own, NEG)
        if n_sel == 0:
            p = jax.nn.softmax(s_own, axis=-1).astype(v.dtype)
            return jnp.einsum('bhqk,bhkd->bhqd', p, v_own)
        sel_c = lax.dynamic_slice_in_dim(sel, start, MOBA_QCHUNK, axis=2)
        ok_c = lax.dynamic_slice_in_dim(sel_ok, start, MOBA_QCHUNK, axis=2)
        flat = sel_c.reshape(B, H, MOBA_QCHUNK * n_sel)
        k_sel = gather_blocks(kb, flat).reshape(B, H, MOBA_QCHUNK, n_sel, blk, dh)
        v_sel = gather_blocks(vb, flat).reshape(B, H, MOBA_QCHUNK, n_sel, blk, dh)
        s_sel = jnp.einsum('bhqd,bhqnkd->bhqnk', qc, k_sel).astype(jnp.float32) * scale
        dist_sel = qpos[None, None, :, None, None] - (sel_c[..., None] * blk + jnp.arange(blk))
        s_sel = s_sel + table_t[head_ix, rel_bucket(dist_sel)]
        s_sel = jnp.where(ok_c[..., None], s_sel, NEG)
        s = jnp.concatenate([s_own, s_sel.reshape(B, H, MOBA_QCHUNK, n_sel * blk)], axis=-1)
        p = jax.nn.softmax(s, axis=-1).astype(v.dtype)
        p_own = p[..., :blk]
        p_sel = p[..., blk:].reshape(B, H, MOBA_QCHUNK, n_sel, blk)
        return (jnp.einsum('bhqk,bhkd->bhqd', p_own, v_own)
                + jnp.einsum('bhqnk,bhqnkd->bhqd', p_sel, v_sel))

    o = lax.map(chunk, jnp.arange(sp // MOBA_QCHUNK))
    o = jnp.moveaxis(o, 0, 2).reshape(B, H, sp, dh)
    return o[:, :, :S]


def diff_attention(q, k, v, lam, subln_g, lam_init, bias_table):
    B, H, _, S, dh = q.shape
    scale = dh ** -0.5
    kpos = jnp.arange(S)

    def block(i):
        qb = lax.dynamic_slice_in_dim(q, i * Q_BLOCK, Q_BLOCK, axis=3)
        qpos = i * Q_BLOCK + jnp.arange(Q_BLOCK)
        dist = qpos[:, None] - kpos[None, :]
        bias = jnp.moveaxis(jnp.take(bias_table, rel_bucket(dist), axis=0), -1, 0)
        s = jnp.einsum('bhcqd,bhckd->bhcqk', qb, k).astype(jnp.float32) * scale
        s = jnp.where(dist >= 0, s + bias[None, :, None], NEG)
        p = jax.nn.softmax(s, axis=-1)
        a = (p[:, :, 0] - lam * p[:, :, 1]).astype(v.dtype)
        return jnp.einsum('bhqk,bhkd->bhqd', a, v)

    o = lax.map(block, jnp.arange(S // Q_BLOCK))
    o = jnp.moveaxis(o, 0, 2).reshape(B, H, S, 2 * dh)
    return rmsnorm(o, subln_g) * (1.0 - lam_init)


def dilated_attention(q, k, v, bias_table):
    B, H, S, dh = q.shape
    blk = DIL_BLOCK
    scale = dh ** -0.5
    qi = jnp.arange(blk)
    kj = jnp.arange(2 * blk) - blk
    step = qi[:, None] - kj[None, :]
    outs, lses = [], []
    for window, dil in DIL_CONFIGS:
        span = window // dil
        L = S // dil
        lp = -(-L // blk) * blk
        nblk = lp // blk

        def strided(x):
            x = x.reshape(B, H, L, dil, dh).transpose(0, 1, 3, 2, 4)
            x = jnp.pad(x, ((0, 0), (0, 0), (0, 0), (0, lp - L), (0, 0)))
            return x.reshape(B, H, dil, nblk, blk, dh)

        def band(x):
            prev = jnp.pad(x, ((0, 0), (0, 0), (0, 0), (1, 0), (0, 0), (0, 0)))[:, :, :, :-1]
            return jnp.concatenate([prev, x], axis=4)

        qb = strided(q)
        kband = band(strided(k))
        vband = band(strided(v))
        s = jnp.einsum('bhrnqd,bhrnkd->bhrnqk', qb, kband).astype(jnp.float32) * scale
        bias = jnp.moveaxis(jnp.take(bias_table, rel_bucket(step * dil), axis=0), -1, 0)
        key_exists = (jnp.arange(nblk)[:, None] * blk + kj[None, :]) >= 0
        mask = ((step >= 0) & (step <= span))[None] & key_exists[:, None, :]
        s = jnp.where(mask, s + bias[None, :, None, None], NEG)
        m = jnp.max(s, axis=-1, keepdims=True)
        e = jnp.exp(s - m)
        den = jnp.sum(e, axis=-1, keepdims=True)
        o = jnp.einsum('bhrnqk,bhrnkd->bhrnqd', (e / den).astype(v.dtype), vband)
        lse = (m + jnp.log(den))[..., 0]
        o = o.reshape(B, H, dil, lp, dh)[:, :, :, :L].transpose(0, 1, 3, 2, 4).reshape(B, H, S, dh)
        lse = lse.reshape(B, H, dil, lp)[:, :, :, :L].transpose(0, 1, 3, 2).reshape(B, H, S)
        outs.append(o)
        lses.append(lse)
    wts = jax.nn.softmax(jnp.stack(lses, axis=0), axis=0)
    return jnp.einsum('gbhs,gbhsd->bhsd', wts.astype(v.dtype), jnp.stack(outs, axis=0))


def mla_attention(q_nope, q_rope, k_nope, k_rope, v):
    B, H, S, _ = q_nope.shape
    scale = (MLA_NOPE_DIM + MLA_ROPE_DIM) ** -0.5
    kpos = jnp.arange(S)

    def block(i):
        qn = lax.dynamic_slice_in_dim(q_nope, i * Q_BLOCK, Q_BLOCK, axis=2)
        qr = lax.dynamic_slice_in_dim(q_rope, i * Q_BLOCK, Q_BLOCK, axis=2)
        qpos = i * Q_BLOCK + jnp.arange(Q_BLOCK)
        s = (jnp.einsum('bhqd,bhkd->bhqk', qn, k_nope)
             + jnp.einsum('bhqd,bkd->bhqk', qr, k_rope)).astype(jnp.float32) * scale
        s = jnp.where(qpos[:, None] >= kpos[None, :], s, NEG)
        p = jax.nn.softmax(s, axis=-1).astype(v.dtype)
        return jnp.einsum('bhqk,bhkd->bhqd', p, v)

    o = lax.map(block, jnp.arange(S // Q_BLOCK))
    return jnp.moveaxis(o, 0, 2).reshape(B, H, S, MLA_V_DIM)


def even_mixer(h, w_in, lam_params, subln_g, w_out, rel_table, lam_init):
    B, S, _ = h.shape
    qa, ka, va, qd, kd, vd = split_cols(h @ w_in, EVEN_IN_SPLIT)
    o_a = moba_attention(heads(qa, MOBA_HEADS, HEAD_DIM), heads(ka, MOBA_HEADS, HEAD_DIM),
                         heads(va, MOBA_HEADS, HEAD_DIM), rel_table[:, :MOBA_HEADS])
    qd = qd.reshape(B, S, DIFF_HEADS, 2, DIFF_HEAD_DIM).transpose(0, 2, 3, 1, 4)
    kd = kd.reshape(B, S, DIFF_HEADS, 2, DIFF_HEAD_DIM).transpose(0, 2, 3, 1, 4)
    vd = heads(vd, DIFF_HEADS, 2 * DIFF_HEAD_DIM)
    lp = lam_params.astype(jnp.float32)
    lam = jnp.exp(jnp.sum(lp[0] * lp[1])) - jnp.exp(jnp.sum(lp[2] * lp[3])) + lam_init
    o_b = diff_attention(qd, kd, vd, lam, subln_g, lam_init,
                         rel_table[:, MOBA_HEADS:MOBA_HEADS + DIFF_HEADS])
    return jnp.concatenate([merge_heads(o_a), merge_heads(o_b)], axis=-1) @ w_out


def odd_mixer(h, w_in, q_norm, w_uq, kv_norm, w_ukv, w_out, rel_table):
    B, S, _ = h.shape
    qc, kc, vc, cq, ckv, kr = split_cols(h @ w_in, ODD_IN_SPLIT)
    o_c = dilated_attention(heads(qc, DIL_HEADS, HEAD_DIM), heads(kc, DIL_HEADS, HEAD_DIM),
                            heads(vc, DIL_HEADS, HEAD_DIM), rel_table[:, MOBA_HEADS + DIFF_HEADS:])
    pos = jnp.arange(S)
    q = heads(rmsnorm(cq, q_norm) @ w_uq, MLA_HEADS, MLA_NOPE_DIM + MLA_ROPE_DIM)
    q_nope, q_rope = q[..., :MLA_NOPE_DIM], rope(q[..., MLA_NOPE_DIM:], pos)
    kv = heads(rmsnorm(ckv, kv_norm) @ w_ukv, MLA_HEADS, MLA_NOPE_DIM + MLA_V_DIM)
    k_nope, v = kv[..., :MLA_NOPE_DIM], kv[..., MLA_NOPE_DIM:]
    o_d = mla_attention(q_nope, q_rope, k_nope, rope(kr, pos), v)
    return jnp.concatenate([merge_heads(o_c), merge_heads(o_d)], axis=-1) @ w_out


def conv_ffn(h, w_in, conv_w, conv_b, w_out):
    u, g = jnp.split(h @ w_in, 2, axis=-1)
    g = lax.conv_general_dilated(g, conv_w[:, None, :], window_strides=(1,),
                                 padding=[(CONV_WIDTH - 1, 0)],
                                 dimension_numbers=('NWC', 'WIO', 'NWC'),
                                 feature_group_count=D_FF) + conv_b
    return (jax.nn.gelu(g, approximate=False) * u) @ w_out


def setup_inputs(seed: int = 0) -> dict:
    key = jax.random.key(seed)
    ks = jax.random.split(key, 20)
    f32 = jnp.float32

    def w(k, shape, fan_in):
        return jax.random.normal(k, shape, f32) * (fan_in ** -0.5)

    def gain(k, shape):
        return 1.0 + 0.02 * jax.random.normal(k, shape, f32)

    return {
        'x': jax.random.normal(ks[0], (BATCH, SEQ, D_MODEL), f32),
        'rel_bias': 0.2 * jax.random.normal(ks[1], (REL_BUCKETS, N_BIAS_HEADS), f32),
        'even_norm1': gain(ks[2], (N_EVEN, D_MODEL)),
        'even_w_in': w(ks[3], (N_EVEN, D_MODEL, EVEN_IN), D_MODEL),
        'diff_lambda': 0.1 * jax.random.normal(ks[4], (N_EVEN, 4, DIFF_HEAD_DIM), f32),
        'diff_subln': gain(ks[5], (N_EVEN, 2 * DIFF_HEAD_DIM)),
        'even_w_out': w(ks[6], (N_EVEN, EVEN_MIX, D_MODEL), EVEN_MIX),
        'odd_norm1': gain(ks[7], (N_ODD, D_MODEL)),
        'odd_w_in': w(ks[8], (N_ODD, D_MODEL, ODD_IN), D_MODEL),
        'mla_q_norm': gain(ks[9], (N_ODD, MLA_Q_RANK)),
        'mla_w_uq': w(ks[10], (N_ODD, MLA_Q_RANK, MLA_HEADS * (MLA_NOPE_DIM + MLA_ROPE_DIM)), MLA_Q_RANK),
        'mla_kv_norm': gain(ks[11], (N_ODD, MLA_KV_RANK)),
        'mla_w_ukv': w(ks[12], (N_ODD, MLA_KV_RANK, MLA_HEADS * (MLA_NOPE_DIM + MLA_V_DIM)), MLA_KV_RANK),
        'odd_w_out': w(ks[13], (N_ODD, ODD_MIX, D_MODEL), ODD_MIX),
        'ffn_norm': gain(ks[14], (DEPTH, D_MODEL)),
        'ffn_w_in': w(ks[15], (DEPTH, D_MODEL, 2 * D_FF), D_MODEL),
        'ffn_conv_w': w(ks[16], (DEPTH, CONV_WIDTH, D_FF), CONV_WIDTH),
        'ffn_conv_b': 0.02 * jax.random.normal(ks[17], (DEPTH, D_FF), f32),
        'ffn_w_out': w(ks[18], (DEPTH, D_FF, D_MODEL), D_FF),
        'final_norm': gain(ks[19], (D_MODEL,)),
    }


def reference(x, rel_bias, even_norm1, even_w_in, diff_lambda, diff_subln, even_w_out,
              odd_norm1, odd_w_in, mla_q_norm, mla_w_uq, mla_kv_norm, mla_w_ukv, odd_w_out,
              ffn_norm, ffn_w_in, ffn_conv_w, ffn_conv_b, ffn_w_out, final_norm):
    h = x
    for layer in range(DEPTH):
        i = layer // 2
        if layer % 2 == 0:
            lam_init = 0.8 - 0.6 * math.exp(-0.3 * layer)
            h = h + even_mixer(rmsnorm(h, even_norm1[i]), even_w_in[i], diff_lambda[i],
                               diff_subln[i], even_w_out[i], rel_bias, lam_init)
        else:
            h = h + odd_mixer(rmsnorm(h, odd_norm1[i]), odd_w_in[i], mla_q_norm[i], mla_w_uq[i],
                              mla_kv_norm[i], mla_w_ukv[i], odd_w_out[i], rel_bias)
        h = h + conv_ffn(rmsnorm(h, ffn_norm[layer]), ffn_w_in[layer], ffn_conv_w[layer],
                         ffn_conv_b[layer], ffn_w_out[layer])
    return rmsnorm(h, final_norm)
```

```python
import math
from contextlib import ExitStack

import numpy as np
import concourse.bass as bass
import concourse.mybir as mybir
from concourse.bass_utils import run_bass_kernel_spmd

F32 = mybir.dt.float32
BF16 = mybir.dt.bfloat16
ALU = mybir.AluOpType
AF = mybir.ActivationFunctionType
AX = mybir.AxisListType

S = 2048
D = 1024
NT = 16
DEPTH = 4
DFF = 2816
NCC = 22
EPS = 1e-6
LW = 2560
GW = 2432
NEGBIG = -30000.0


class Buf:
    __slots__ = ("name", "w", "rs")

    def __init__(self, name):
        self.name = name
        self.w = None
        self.rs = []


class T:
    __slots__ = ("t", "b")

    def __init__(self, t, name):
        self.t = t
        self.b = Buf(name)

    def __getitem__(self, k):
        return self.t[k]


class Prog:
    ENGS = ("pe", "act", "dve", "pool", "sp")

    def __init__(self, nc, n_dma_sems=8):
        self.nc = nc
        self.ops = {e: [] for e in self.ENGS}
        self.cnt = {e: 0 for e in self.ENGS}
        self.waited = {e: {} for e in self.ENGS}
        self.sems = {}
        self.n_dma_sems = n_dma_sems
        self.dma_i = {"sp": 0, "pool": 0, "act": 0}
        self.dma_last = {}
        self.stopped = False

    def alloc_sems(self, stack):
        for e in self.ENGS:
            self.sems[e] = stack.enter_context(self.nc.semaphore("s_" + e))
        for q in ("sp", "pool", "act"):
            for i in range(self.n_dma_sems):
                self.sems[("dma", q, i)] = stack.enter_context(self.nc.semaphore(f"d_{q}_{i}"))

    def _need(self, eng, waits, ev, kind):
        if ev is None:
            return
        semkey, val, src = ev
        if src == eng and semkey == eng:
            if eng == "pe" or kind == "war":
                return
        if self.waited[eng].get(semkey, 0) >= val:
            return
        if waits.get(semkey, 0) < val:
            waits[semkey] = val

    def _deps(self, eng, reads, writes):
        waits = {}
        for b in reads:
            self._need(eng, waits, b.w, "raw")
        for b in writes:
            self._need(eng, waits, b.w, "waw")
            for r in b.rs:
                self._need(eng, waits, r, "war")
        for k, v in waits.items():
            self.waited[eng][k] = v
        return list(waits.items())

    def op(self, eng, fn, reads=(), writes=()):
        if self.stopped:
            return None
        reads = [x.b if isinstance(x, T) else x for x in reads]
        writes = [x.b if isinstance(x, T) else x for x in writes]
        waits = self._deps(eng, reads, writes)
        self.cnt[eng] += 1
        ev = (eng, self.cnt[eng], eng)
        self.ops[eng].append((waits, fn, (eng, 1)))
        for b in reads:
            b.rs.append(ev)
            if len(b.rs) > 64:
                b.rs = self._prune(b.rs)
        for b in writes:
            b.w = ev
            b.rs = []
        return ev

    @staticmethod
    def _prune(rs):
        best = {}
        for (k, v, s) in rs:
            if k not in best or best[k][1] < v:
                best[k] = (k, v, s)
        return list(best.values())

    def dma(self, q, fn, reads=(), writes=()):
        if self.stopped:
            return None
        reads = [x.b if isinstance(x, T) else x for x in reads]
        writes = [x.b if isinstance(x, T) else x for x in writes]
        waits = self._deps(q, reads, writes)
        i = self.dma_i[q]
        self.dma_i[q] += 1
        slot = i % self.n_dma_sems
        semkey = ("dma", q, slot)
        val = 16 * (i // self.n_dma_sems + 1)
        if val > 16 and self.waited[q].get(semkey, 0) < val - 16:
            waits.append((semkey, val - 16))
            self.waited[q][semkey] = val - 16
        ev = (semkey, val, q + "_dma")
        self.dma_last[semkey] = val
        self.ops[q].append((waits, fn, (semkey, 16)))
        for b in reads:
            b.rs.append(ev)
        for b in writes:
            b.w = ev
            b.rs = []
        return ev

    def barrier(self):
        if self.stopped:
            return
        for e in self.ENGS:
            waits = []
            for o in self.ENGS:
                if o == e:
                    continue
                v = self.cnt[o]
                if v > 0 and self.waited[e].get(o, 0) < v:
                    waits.append((o, v))
                    self.waited[e][o] = v
            for k, v in self.dma_last.items():
                if self.waited[e].get(k, 0) < v:
                    waits.append((k, v))
                    self.waited[e][k] = v
            if waits:
                self.ops[e].append((waits, None, None))

    def wait_all(self, eng, bufs):
        bufs = [x.b if isinstance(x, T) else x for x in bufs]
        waits = self._deps(eng, bufs, ())
        self.ops[eng].append((waits, None, None))

    def emit(self, block):
        sems = self.sems

        def run(e, lst):
            for waits, fn, inc in lst:
                for k, v in waits:
                    e.wait_ge(sems[k], v)
                if fn is not None:
                    fn(e).then_inc(sems[inc[0]], inc[1])

        @block.tensor
        def _(e):
            run(e, self.ops["pe"])

        @block.scalar
        def _(e):
            run(e, self.ops["act"])

        @block.vector
        def _(e):
            run(e, self.ops["dve"])

        @block.gpsimd
        def _(e):
            run(e, self.ops["pool"])

        @block.sync
        def _(e):
            run(e, self.ops["sp"])


def _np_bucket(dist):
    n = np.maximum(dist, 0)
    nf = np.maximum(n, 1).astype(np.float32)
    large = 16 + (np.log(nf / np.float32(16)) / np.float32(math.log(64)) * np.float32(16)).astype(np.int32)
    large = np.minimum(large, 31)
    return np.where(n < 16, n, large)


def _host_consts():
    c = {}
    oh = np.zeros((33, LW), np.float32)
    d = np.arange(LW) - 511
    b = _np_bucket(d)
    for i in range(LW):
        if d[i] < 0:
            oh[32, i] = 1.0
        else:
            oh[b[i], i] = 1.0
    c["c_oh_main"] = oh
    ohd = np.zeros((33, 3 * 512), np.float32)
    for gi, dil in enumerate((1, 4, 16)):
        for i in range(512):
            s = i - 127
            if 0 <= s <= 128:
                ohd[_np_bucket(np.array([s * dil]))[0], gi * 512 + i] = 1.0
            else:
                ohd[32, gi * 512 + i] = 1.0
    c["c_oh_dil"] = ohd
    koh = np.zeros((8, S), np.float32)
    for n in range(8):
        koh[n, n * 256:(n + 1) * 256] = 1.0
    c["c_koh"] = koh
    gm = np.zeros((2, 16, 8), np.float32)
    om = np.ones((2, 16, 8), np.float32)
    for t in range(16):
        bq = t // 2
        gm[:, t, bq:] = -1e30
        om[:, t, bq] = 0.0
    c["c_gmask"] = np.broadcast_to(gm.reshape(1, 256), (128, 256)).copy()
    c["c_omask"] = np.broadcast_to(om.reshape(1, 256), (128, 256)).copy()
    half = 32
    freq = (np.float32(10000.0) ** (-np.arange(half, dtype=np.float32) / np.float32(half))).astype(np.float32)
    ang = np.arange(S, dtype=np.float32)[None, :] * freq[:, None]
    cs, sn = np.cos(ang).astype(np.float32), np.sin(ang).astype(np.float32)
    c["c_ropec"] = np.concatenate([cs, cs], 0)
    c["c_ropes"] = np.concatenate([-sn, sn], 0)
    tri = (np.arange(128)[None, :] >= np.arange(128)[:, None]).astype(np.float32)
    c["c_tri"] = tri
    return c


import os as _os
DIL_SPLIT = bool(_os.environ.get("DIL_SPLIT"))
FMA_ENG = "dve"


class _Stop(Exception):
    pass


def build_program(n_layers=DEPTH, debug_h=False, stop_at=None):
    nc = bass.Bass("TRN2", target_bir_lowering=False)

    stop_state = {"P": None}

    def ckpt(name):
        if stop_at == name:
            stop_state["P"].stopped = True

    dram_in = {}

    def din(name, shape):
        dram_in[name] = nc.dram_tensor(name, list(shape), F32, kind="ExternalInput").ap()
        return dram_in[name]

    x_d = din("x", [S, D])
    rel_d = din("rel_bias", [32, 20])
    en1_d = din("even_norm1", [2, D])
    ewin_d = din("even_w_in", [2, D, 3072])
    dlam_d = din("diff_lambda", [2, 4, 64])
    dsub_d = din("diff_subln", [2, 128])
    ewout_d = din("even_w_out", [2, D, D])
    on1_d = din("odd_norm1", [2, D])
    owin_d = din("odd_w_in", [2, D, 1984])
    qn_d = din("mla_q_norm", [2, 256])
    wuq_d = din("mla_w_uq", [2, 256, 768])
    kvn_d = din("mla_kv_norm", [2, 128])
    wukv_d = din("mla_w_ukv", [2, 128, 1024])
    owout_d = din("odd_w_out", [2, D, D])
    fn_d = din("ffn_norm", [4, D])
    fwin_d = din("ffn_w_in", [4, D, 2 * DFF])
    fcw_d = din("ffn_conv_w", [4, 3, DFF])
    fcb_d = din("ffn_conv_b", [4, DFF])
    fwout_d = din("ffn_w_out", [4, DFF, D])
    fin_d = din("final_norm", [D])
    c_oh_main = din("c_oh_main", [33, LW])
    c_oh_dil = din("c_oh_dil", [33, 1536])
    c_koh = din("c_koh", [8, S])
    c_gmask = din("c_gmask", [128, 256])
    c_omask = din("c_omask", [128, 256])
    c_ropec = din("c_ropec", [64, S])
    c_ropes = din("c_ropes", [64, S])
    c_tri = din("c_tri", [128, 128])
    out_d = nc.dram_tensor("out", [S, D], F32, kind="ExternalOutput").ap()
    m_main = nc.dram_tensor("m_main", [12 * 128, LW], BF16)
    m_dil = nc.dram_tensor("m_dil", [24 * 128, 512], BF16)
    B_mmain = [Buf("mmain%d" % i) for i in range(12)]
    B_mdil = [Buf("mdil%d" % i) for i in range(24)]
    B_out = Buf("out")

    with ExitStack() as st:
        P = Prog(nc)
        stop_state["P"] = P
        P.alloc_sems(st)
        st.enter_context(nc.allow_non_contiguous_dma("small strided parameter loads"))

        uniq = {"i": 0}

        def sbt(stack, name, shape, dt):
            uniq["i"] += 1
            name = "%s_%d" % (name, uniq["i"])
            return T(stack.enter_context(nc.sbuf_tensor(name, list(shape), dt)), name)

        def mm(out, lhsT, rhs, start, stop, r, w):
            P.op("pe", lambda e: e.matmul(out, lhsT=lhsT, rhs=rhs, start=start, stop=stop,
                                          skip_group_check=True), r, w)

        def trp(out, in_, ident, r, w):
            P.op("pe", lambda e: e.transpose(out=out, in_=in_, identity=ident), r, w)

        def act(out, in_, func, r, w, bias=None, scale=None, accum=None):
            kw = {}
            if bias is not None:
                kw["bias"] = bias
            if scale is not None:
                kw["scale"] = scale
            if accum is not None:
                kw["accum_out"] = accum
            P.op("act", lambda e: e.activation(out=out, in_=in_, func=func, **kw), r, w)

        def tt(out, in0, in1, op, r, w, eng="dve"):
            P.op(eng, lambda e: e.tensor_tensor(out=out, in0=in0, in1=in1, op=op), r, w)

        def ts(out, in0, s1, s2, op0, op1, r, w, eng="dve"):
            if s2 is None:
                P.op(eng, lambda e: e.tensor_scalar(out=out, in0=in0, scalar1=s1, scalar2=None, op0=op0), r, w)
            else:
                P.op(eng, lambda e: e.tensor_scalar(out=out, in0=in0, scalar1=s1, scalar2=s2, op0=op0, op1=op1), r, w)

        def stt(out, in0, scalar, in1, op0, op1, r, w, eng="dve"):
            P.op(eng, lambda e: e.scalar_tensor_tensor(out=out, in0=in0, scalar=scalar, in1=in1, op0=op0, op1=op1), r, w)

        def cp(out, in_, r, w, eng="dve"):
            P.op(eng, lambda e: e.tensor_copy(out=out, in_=in_), r, w)

        def recip(out, in_, r, w):
            act(out, in_, AF.Ln, r, w)
            act(out, out, AF.Exp, list(w), w, scale=-1.0)

        def memset(ap, val, w, eng="pool"):
            P.op(eng, lambda e: e.memset(ap, val), (), w)

        def dma(q, out, in_, r, w):
            P.dma(q, lambda e: e.dma_start(out=out, in_=in_), r, w)

        def run_pipeline(tasks, depth):
            n = len(tasks)
            for i in range(min(depth, n)):
                tasks[i][0]()
            for i in range(n):
                if i + depth < n:
                    tasks[i + depth][0]()
                tasks[i][1]()

        def rope2(dst_ap, dst_T, ps, col0, tok0, n, rope_fn):
            rope_fn(dst_ap, dst_T, ps, col0, tok0, n)

        h = sbt(st, "h", [128, NT, D], F32)
        hB = [Buf("h%d" % t) for t in range(NT)]
        hnT = sbt(st, "hnT", [128, 8, S], BF16)
        idf = sbt(st, "idf", [128, 128], F32)
        idb = sbt(st, "idb", [128, 128], BF16)
        ones_bf = sbt(st, "ones_bf", [128, 128], BF16)
        onesA = sbt(st, "onesA", [128, 128], BF16)
        onesB = sbt(st, "onesB", [128, 128], BF16)
        ones_f = sbt(st, "ones_f", [128, 128], F32)
        tri_bf = sbt(st, "tri_bf", [128, 128], BF16)
        gmask = sbt(st, "gmask", [128, 256], F32)
        omask = sbt(st, "omask", [128, 256], F32)
        convw = sbt(st, "convw", [128, 16 * NCC], F32)
        small = sbt(st, "small", [128, 64], F32)
        lamb = sbt(st, "lamb", [128, 2 * 256], F32)
        stat = sbt(st, "stat", [128, 64], F32)
        PS = [T(st.enter_context(nc.psum_tensor("ps%d" % i, [128, 512], F32)), "ps%d" % i) for i in range(8)]

        memset(idf[:], 0.0, [idf])
        P.op("pool", lambda e: e.affine_select(out=idf[:], in_=idf[:], pattern=[[-1, 128]],
                                               compare_op=ALU.not_equal, fill=1.0, base=0,
                                               channel_multiplier=1), [idf], [idf])
        cp(idb[:, :], idf[:, :], [idf], [idb])
        memset(ones_bf[:], 1.0, [ones_bf])
        memset(ones_f[:], 1.0, [ones_f])
        memset(onesA[:], 0.0, [onesA])
        memset(onesA[:, 0:64], 1.0, [onesA])
        memset(onesB[:], 0.0, [onesB])
        memset(onesB[:, 64:128], 1.0, [onesB])
        dma("pool", tri_bf[:], c_tri, [], [tri_bf])
        dma("sp", gmask[:], c_gmask, [], [gmask])
        dma("sp", omask[:], c_omask, [], [omask])
        for t in range(NT):
            dma("sp", h[:, t, :], x_d[t * 128:(t + 1) * 128, :], [], [hB[t]])
        with ExitStack() as sc:
            raws = [sbt(sc, "craw%d" % i, [128, 128], F32) for i in range(4)]
            for i_ in range(4):
                memset(raws[i_][:, :], 0.0, [raws[i_]], eng="dve")
            vecs = [fcw_d[li, k] for li in range(4) for k in range(3)] + [fcb_d[li] for li in range(4)]
            for v, vec in enumerate(vecs):
                tl, j = v // 5, v % 5
                dma("sp", raws[tl][j * NCC:(j + 1) * NCC, :], vec.rearrange("(c p) -> c p", p=128), [raws[tl]], [raws[tl]])
            for tl in range(4):
                nv = min(5, 16 - tl * 5)
                ps = PS[tl]
                trp(ps[:, 0:128], raws[tl][:, :], idf[:, :], [raws[tl], idf], [ps])
                cp(convw[:, tl * 5 * NCC:(tl * 5 + nv) * NCC], ps[:, 0:nv * NCC], [ps], [convw])
            P.barrier()
        for i in range(2):
            dma("sp", small[:, i:i + 1], dsub_d[i].rearrange("(c p) -> p c", p=128), [], [small])
            dma("sp", small[:, 2 + 2 * i:4 + 2 * i], qn_d[i].rearrange("(c p) -> p c", p=128), [], [small])
            dma("sp", small[:, 6 + i:7 + i], kvn_d[i].rearrange("(c p) -> p c", p=128), [], [small])
            dma("sp", lamb[:, i * 256:(i + 1) * 256],
                dlam_d[i].rearrange("a b -> (a b)").partition_broadcast(128), [], [lamb])
        for i in range(2):
            layer = 2 * i
            lam_init = 0.8 - 0.6 * math.exp(-0.3 * layer)
            lp = lamb[:, i * 256:(i + 1) * 256]
            tt(lamb[:, i * 256:i * 256 + 64], lamb[:, i * 256:i * 256 + 64], lamb[:, i * 256 + 64:i * 256 + 128], ALU.mult, [lamb], [lamb])
            tt(lamb[:, i * 256 + 128:i * 256 + 192], lamb[:, i * 256 + 128:i * 256 + 192], lamb[:, i * 256 + 192:i * 256 + 256], ALU.mult, [lamb], [lamb])
            P.op("dve", lambda e, i=i: e.reduce_sum(out=small[:, 16 + 2 * i:17 + 2 * i], in_=lamb[:, i * 256:i * 256 + 64], axis=AX.X), [lamb], [small])
            P.op("dve", lambda e, i=i: e.reduce_sum(out=small[:, 17 + 2 * i:18 + 2 * i], in_=lamb[:, i * 256 + 128:i * 256 + 192], axis=AX.X), [lamb], [small])
            act(small[:, 20 + 2 * i:22 + 2 * i], small[:, 16 + 2 * i:18 + 2 * i], AF.Exp, [small], [small])
            tt(small[:, 8 + i:9 + i], small[:, 21 + 2 * i:22 + 2 * i], small[:, 20 + 2 * i:21 + 2 * i], ALU.subtract, [small], [small])
            ts(small[:, 8 + i:9 + i], small[:, 8 + i:9 + i], -lam_init, None, ALU.add, None, [small], [small])
            ts(small[:, 12 + i:13 + i], small[:, i:i + 1], 1.0 - lam_init, None, ALU.mult, None, [small], [small])

        f_main = nc.dram_tensor("f_main", [20, LW], BF16)
        f_dil = nc.dram_tensor("f_dil", [20, 1536], BF16)
        B_fmain = Buf("fmain")
        B_fdil = Buf("fdil")
        with ExitStack() as s0:
            tab = sbt(s0, "tab", [33, 20], F32)
            ohm = sbt(s0, "ohm", [33, LW], F32)
            ohd = sbt(s0, "ohd", [33, 1536], F32)
            fsb = sbt(s0, "fsb", [20, LW], BF16)
            fsd = sbt(s0, "fsd", [20, 1536], BF16)
            memset(tab[:], NEGBIG, [tab])
            dma("sp", tab[0:32, :], rel_d, [tab], [tab])
            dma("sp", ohm[:], c_oh_main, [], [ohm])
            dma("sp", ohd[:], c_oh_dil, [], [ohd])
            k = 0
            for c in range(LW // 512):
                ps = PS[k % 8]
                k += 1
                mm(ps[0:20, :], tab[0:33, 0:20], ohm[0:33, c * 512:(c + 1) * 512], True, True, [tab, ohm], [ps])
                act(fsb[0:20, c * 512:(c + 1) * 512], ps[0:20, :], AF.Exp, [ps], [fsb])
            for gi in range(3):
                ps = PS[k % 8]
                k += 1
                mm(ps[0:20, :], tab[0:33, 0:20], ohd[0:33, gi * 512:(gi + 1) * 512], True, True, [tab, ohd], [ps])
                act(fsd[0:20, gi * 512:(gi + 1) * 512], ps[0:20, :], AF.Exp, [ps], [fsd])
            dma("sp", f_main.ap()[:, :], fsb[0:20, :], [fsb], [B_fmain])
            dma("sp", f_dil.ap()[:, :], fsd[0:20, :], [fsd], [B_fdil])
            P.barrier()
        for hh in range(12):
            src = bass.AP(f_main, hh * LW, [[0, 128], [1, LW]])
            dma("sp", m_main.ap()[hh * 128:(hh + 1) * 128, :], src, [B_fmain], [B_mmain[hh]])
        for hh in range(8):
            for gi in range(3):
                idx = hh * 3 + gi
                src = bass.AP(f_dil, (12 + hh) * 1536 + gi * 512, [[0, 128], [1, 512]])
                dma("sp", m_dil.ap()[idx * 128:(idx + 1) * 128, :], src, [B_fdil], [B_mdil[idx]])

        def load_G(dst, head):
            src = bass.AP(m_main, head * 128 * LW + 127, [[LW - 1, 128], [1, GW]])
            dma("sp", dst[:, 0:GW], src, [B_mmain[head]], [dst])

        def load_Gdil(dst, idx):
            src = bass.AP(m_dil, idx * 128 * 512 + 127, [[511, 128], [1, 256]])
            dma("sp", dst, src, [B_mdil[idx]], [])

        gain_src = [en1_d[0], en1_d[1], on1_d[0], on1_d[1], fn_d[0], fn_d[1], fn_d[2], fn_d[3]]

        def norm_transpose(gidx, tiles):
            with ExitStack() as sn:
                norm_body(sn, gidx, tiles)
            P.barrier()

        def norm_body(sn, gidx, tiles):
            if True:
                gb = sbt(sn, "gb", [128, D], F32)
                junk = sbt(sn, "junk", [128, D], BF16)
                xg = [sbt(sn, "xg%d" % i, [128, D], BF16) for i in range(4)]
                dma("sp", gb[:, :], gain_src[gidx].partition_broadcast(128), [], [gb])
                sB = [Buf("st%d" % t) for t in range(NT)]
                hnB = [Buf("hn%d" % t) for t in range(NT)]
                memset(stat[:, 0:16], 0.0, sB, eng="dve")
                for i_, t in enumerate(tiles):
                    x_ = xg[i_ % 4]
                    act(junk[:, :], h[:, t, :], AF.Square, [hB[t]], [junk, sB[t]], accum=stat[:, t:t + 1])
                    act(stat[:, 16 + t:17 + t], stat[:, t:t + 1], AF.Ln, [sB[t]], [sB[t]], bias=EPS, scale=1.0 / D)
                    act(stat[:, 32 + t:33 + t], stat[:, 16 + t:17 + t], AF.Exp, [sB[t]], [sB[t]], scale=-0.5)
                    stt(x_[:, :], h[:, t, :], stat[:, 32 + t:33 + t], gb[:, :], ALU.mult, ALU.mult, [hB[t], sB[t], gb], [x_])
                    ps = PS[i_ % 8]
                    psb = ps[:, :].bitcast(BF16)
                    for c in range(8):
                        trp(psb[:, c * 128:(c + 1) * 128], x_[:, c * 128:(c + 1) * 128], idb[:, :], [x_, idb], [ps])
                    src = psb[:, :].rearrange("p (c n) -> p c n", n=128)
                    if i_ % 2 == 0:
                        cp(hnT[:, :, t * 128:(t + 1) * 128], src, [ps], [hnB[t]])
                    else:
                        act(hnT[:, :, t * 128:(t + 1) * 128], src, AF.Copy, [ps], [hnB[t]])

        def load_w(dst, w2d, col0, ncol, nk):
            src = w2d[:, col0:col0 + ncol].rearrange("(kc p) n -> p kc n", p=128)
            dma("pool", dst[:, 0:nk, 0:ncol], src, [], [dst])

        def proj_fm(wslab, wc0, evac, kchunks=8, src=None, banks=(0, 1, 2), M=128):
            src = hnT if src is None else src
            for tg in range(4):
                ps = PS[banks[tg % len(banks)]]
                for kc in range(kchunks):
                    mm(ps[0:M, :], wslab[:, kc, wc0:wc0 + M], src[:, kc, tg * 512:(tg + 1) * 512],
                       kc == 0, kc == kchunks - 1, [wslab, src], [ps])
                evac(tg, ps)

        rot = {"s": 0, "p": 0}
        pending = []

        def flush_pending():
            while pending:
                pending.pop(0)()

        def attn_qgroup(g, k_of, q_of, extra_of, Gt, v_of, ones_ap, numb, denb, pexp, ptb, exp_scale, kq_reads, v_reads,
                        mla=False, extra2_of=None):
            nj = 4 * g + 4
            sbanks = {}

            def issue_scores(j):
                c0 = max(0, j - 4 * g) * 128
                N = 512 - c0
                sb = PS[rot["s"] % 4]
                rot["s"] += 1
                sbanks[j] = sb
                last = (extra_of is None and extra2_of is None)
                ex = extra_of(j, g, c0, N) if extra_of is not None else None
                if extra_of is not None and ex is None:
                    mm(sb[:, 0:N], k_of(j), q_of(g, c0, N), True, True, kq_reads, [sb])
                    return
                mm(sb[:, 0:N], k_of(j), q_of(g, c0, N), True, last, kq_reads, [sb])
                if ex is not None:
                    l2, r2 = ex
                    mm(sb[:, 0:N], l2, r2, False, True, kq_reads, [sb])
                if extra2_of is not None:
                    l2, r2, nr = extra2_of(j, g, c0, N)
                    mm(sb[nr[0]:nr[1], 0:N], l2, r2, False, True, kq_reads, [sb])

            for j0 in range(min(3, nj)):
                issue_scores(j0)
            for j in range(nj):
                if j + 3 < nj:
                    issue_scores(j + 3)
                c0 = max(0, j - 4 * g) * 128
                N = 512 - c0
                sb = sbanks.pop(j)
                pt = ptb[rot["p"] % len(ptb)]
                if mla:
                    act(pt[:, 0:N], sb[:, 0:N], AF.Exp, [sb], [pt], scale=exp_scale)
                    if j >= 4 * g:
                        tt(pt[:, 0:128], pt[:, 0:128], tri_bf[:, :], ALU.mult, [pt, tri_bf], [pt])
                else:
                    pe_ = pexp[rot["p"] % len(pexp)]
                    act(pe_[:, 0:N], sb[:, 0:N], AF.Exp, [sb], [pe_], scale=exp_scale)
                    gc = (4 * g - j + 3) * 128 + c0
                    tt(pt[:, 0:N], pe_[:, 0:N], Gt[:, gc:gc + N], ALU.mult, [pe_, Gt], [pt])
                rot["p"] += 1
                mm(numb[:, c0:512], v_of(j), pt[:, 0:N], j == 0, j == nj - 1, [pt] + v_reads, [numb])
                if denb is not None:
                    mm(denb[:, c0:512], ones_ap, pt[:, 0:N], j == 0, j == nj - 1, [pt, ones_bf], [denb])
                if j == min(2, nj - 1):
                    flush_pending()

        def w_out_residual(stack, mixT, w2d):
            wo = [sbt(stack, "wo%d" % i, [128, 8, 512], BF16) for i in range(2)]
            for ch in range(2):
                load_w(wo[ch], w2d, ch * 512, 512, 8)
            k = 0
            for t in range(NT):
                for ch in range(2):
                    ps = PS[k % 8]
                    k += 1
                    for c in range(8):
                        mm(ps[:, :], mixT[:, c, t * 128:(t + 1) * 128], wo[ch][:, c, :], c == 0, c == 7, [mixT, wo[ch]], [ps])
                    tt(h[:, t, ch * 512:(ch + 1) * 512], ps[:, :], h[:, t, ch * 512:(ch + 1) * 512], ALU.add, [ps, hB[t]], [hB[t]])

        def even_mixer(li):
            i2 = li // 2
            w_in = ewin_d[i2]
            with ExitStack() as sm:
                mixT = sbt(sm, "mixT", [128, 8, S], BF16)
                norm_transpose(i2, list(range(NT)))
                ckpt("norm")
                with ExitStack() as sa:
                    QTz = [sbt(sa, "QTz%d" % i, [128, S], BF16) for i in range(2)]
                    KTz = [sbt(sa, "KTz%d" % i, [128, S], BF16) for i in range(2)]
                    VP = sbt(sa, "VP", [128, NT, 2, 128], BF16)
                    QTf = [sbt(sa, "QTf%d" % i, [128, 512], F32) for i in range(2)]
                    ksum = sbt(sa, "ksum", [128, 8], F32)
                    gm = sbt(sa, "gm", [128, 256], F32)
                    top = sbt(sa, "top", [128, 256], F32)
                    pen = sbt(sa, "pen", [128, 256], F32)
                    Gt = [sbt(sa, "Gt%d" % i, [128, GW], BF16) for i in range(2)]
                    wq = [sbt(sa, "wq%d" % i, [128, 8, 128], BF16) for i in range(2)]
                    wk = [sbt(sa, "wk%d" % i, [128, 8, 128], BF16) for i in range(2)]
                    wv = [sbt(sa, "wv%d" % i, [128, 8, 128], BF16) for i in range(2)]
                    pexp = [sbt(sa, "pexp%d" % i, [128, 512], BF16) for i in range(3)]
                    ptb = [sbt(sa, "ptb%d" % i, [128, 512], BF16) for i in range(3)]
                    rd = [sbt(sa, "rd%d" % i, [128, 512], F32) for i in range(2)]
                    o0 = sbt(sa, "o0", [128, 512], F32)
                    o1 = sbt(sa, "o1", [128, 512], F32)
                    sq = sbt(sa, "sq", [128, 512], F32)

                    memset(VP[:, :, :, :].rearrange("p a b c -> p (a b c)"), 1.0, [VP], eng="dve")
                    for i_ in range(2):
                        memset(QTz[i_][:, :], 0.0, [QTz[i_]], eng="dve")
                        memset(KTz[i_][:, :], 0.0, [KTz[i_]], eng="dve")
                    dma("pool", KTz[0][64:72, :], c_koh, [KTz[0]], [KTz[0]])
                    dma("pool", KTz[1][0:8, :], c_koh, [KTz[1]], [KTz[1]])
                    units = [("moba", fc) for fc in range(4)] + [("diff", hd) for hd in range(4)]

                    def unit_cols(u):
                        kind, idx = u
                        if kind == "moba":
                            return idx * 128, 512 + idx * 128, 1024 + idx * 128
                        return 1536 + idx * 128, 2048 + idx * 128, 2560 + idx * 128

                    def load_unit_w(ui):
                        qc, kc_, vc = unit_cols(units[ui])
                        load_w(wq[ui % 2], w_in, qc, 128, 8)
                        load_w(wk[ui % 2], w_in, kc_, 128, 8)
                        load_w(wv[ui % 2], w_in, vc, 128, 8)

                    load_unit_w(0)
                    gslot = 0
                    moba_rot = {"i": 0}
                    for ui, u in enumerate(units):
                        kind, idx = u
                        if ui + 1 < len(units):
                            load_unit_w(ui + 1)
                        Wq, Wk, Wv = wq[ui % 2], wk[ui % 2], wv[ui % 2]

                        if kind == "diff" and idx == 0:
                            memset(QTz[0][64:128, :], 0.0, [QTz[0]], eng="dve")
                            memset(QTz[1][0:64, :], 0.0, [QTz[1]], eng="dve")

                        def evac_k(tg, ps):
                            cp(KTz[0][0:64, tg * 512:(tg + 1) * 512], ps[0:64, :], [ps], [KTz[0]])
                            cp(KTz[1][64:128, tg * 512:(tg + 1) * 512], ps[64:128, :], [ps], [KTz[1]])
                            if kind == "moba":
                                P.op("dve", lambda e: e.reduce_sum(
                                    out=ksum[:, 2 * tg:2 * tg + 2],
                                    in_=ps[:, :].rearrange("p (b k) -> p b k", k=256), axis=AX.X), [ps], [ksum])
                        proj_fm(Wk, 0, evac_k)
                        ckpt("u%dk" % ui)
                        for tq in range(4):
                            ps = PS[3 + tq % 2]
                            for tt_ in range(4):
                                t = tq * 4 + tt_
                                for kc in range(8):
                                    mm(ps[:, tt_ * 128:(tt_ + 1) * 128], hnT[:, kc, t * 128:(t + 1) * 128], Wv[:, kc, :],
                                       kc == 0, kc == 7, [hnT, Wv], [ps])
                            psv = ps[:, :].rearrange("p (a b) -> p a b", b=128)
                            if kind == "moba":
                                act(VP[:, tq * 4:(tq + 1) * 4, 0, 0:64], psv[:, :, 0:64], AF.Copy, [ps], [VP])
                                act(VP[:, tq * 4:(tq + 1) * 4, 1, 64:128], psv[:, :, 64:128], AF.Copy, [ps], [VP])
                            else:
                                act(VP[:, tq * 4:(tq + 1) * 4, 0, :], psv, AF.Copy, [ps], [VP])

                        ckpt("u%dv" % ui)

                        def evac_q(tg, ps):
                            ts(QTz[0][0:64, tg * 512:(tg + 1) * 512], ps[0:64, :], 0.125, None, ALU.mult, None, [ps], [QTz[0]])
                            ts(QTz[1][64:128, tg * 512:(tg + 1) * 512], ps[64:128, :], 0.125, None, ALU.mult, None, [ps], [QTz[1]])
                            if kind == "moba":
                                qf = QTf[tg % 2]
                                ts(qf[:, :], ps[:, :], 0.125, None, ALU.mult, None, [ps], [qf])
                                for hh in range(2):
                                    for tt_ in range(4):
                                        t = tg * 4 + tt_
                                        col = (hh * 16 + t) * 8
                                        mm(PS[7][:, col:col + 8], qf[hh * 64:(hh + 1) * 64, tt_ * 128:(tt_ + 1) * 128],
                                           ksum[hh * 64:(hh + 1) * 64, 0:8], True, True, [qf, ksum], [PS[7]])
                        proj_fm(Wq, 0, evac_q)
                        ckpt("u%dq" % ui)

                        if kind == "moba":
                            tt(gm[:, :], PS[7][:, 0:256], gmask[:, :], ALU.add, [PS[7], gmask], [gm])
                            for q_ in range(32):
                                P.op("dve", lambda e, q_=q_: e.max(out=top[:, q_ * 8:(q_ + 1) * 8], in_=gm[:, q_ * 8:(q_ + 1) * 8]),
                                     [gm], [top])
                            thr = top[:, :].rearrange("p (a b) -> p a b", b=8)[:, :, 2:3].to_broadcast([128, 32, 8])
                            tt(pen[:, :].rearrange("p (a b) -> p a b", b=8), gm[:, :].rearrange("p (a b) -> p a b", b=8), thr,
                               ALU.is_ge, [gm, top], [pen])
                            ckpt("u%dsel" % ui)
                            ts(pen[:, :], pen[:, :], -1.0, -NEGBIG, ALU.add, ALU.mult, [pen], [pen])
                            tt(pen[:, :], pen[:, :], omask[:, :], ALU.mult, [pen, omask], [pen])
                            for hh in range(2):
                                for grp in range(4):
                                    ps = PS[5 + grp % 2]
                                    for tt_ in range(4):
                                        t = grp * 4 + tt_
                                        col = (hh * 16 + t) * 8
                                        trp(ps[0:8, tt_ * 128:(tt_ + 1) * 128], pen[:, col:col + 8], idf[:, :], [pen, idf], [ps])
                                    prow = 64 if hh == 0 else 0
                                    cp(QTz[hh][prow:prow + 8, grp * 512:(grp + 1) * 512], ps[0:8, :], [ps], [QTz[hh]])

                        ckpt("u%dproj" % ui)
                        if kind == "moba":
                            for hh in range(2):
                                head = idx * 2 + hh
                                G = Gt[gslot % 2]
                                gslot += 1
                                load_G(G, head)
                                lo, hi = hh * 64, (hh + 1) * 64
                                dlo, dhi = (1 - hh) * 64, (2 - hh) * 64
                                for g in range(4):
                                    numb = PS[4 + (moba_rot["i"] % 4)]
                                    moba_rot["i"] += 1
                                    attn_qgroup(
                                        g,
                                        lambda j: KTz[hh][:, j * 128:(j + 1) * 128],
                                        lambda g_, c0, N: QTz[hh][:, g_ * 512 + c0:g_ * 512 + c0 + N],
                                        None,
                                        G, lambda j: VP[:, j, hh, :], None, numb, None, pexp, ptb, 1.0,
                                        [KTz[hh], QTz[hh]], [VP])
                                    def epi(g=g, numb=numb, lo=lo, hi=hi, dlo=dlo, dhi=dhi, idx=idx):
                                        r_ = rd[g % 2]
                                        recip(r_[lo:hi, :], numb[dlo:dhi, :], [numb], [r_])
                                        tt(mixT[lo:hi, idx, g * 512:(g + 1) * 512], numb[lo:hi, :], r_[lo:hi, :], ALU.mult,
                                           [numb, r_], [mixT])
                                    pending.append(epi)
                        else:
                            head = 8 + idx
                            G = Gt[gslot % 2]
                            gslot += 1
                            load_G(G, head)
                            for g in range(4):
                                for c in range(2):
                                    lo, hi = c * 64, (c + 1) * 64
                                    numb, denb = (PS[4], PS[5]) if c == 0 else (PS[6], PS[7])
                                    attn_qgroup(
                                        g,
                                        lambda j: KTz[c][:, j * 128:(j + 1) * 128],
                                        lambda g_, c0, N: QTz[c][:, g_ * 512 + c0:g_ * 512 + c0 + N],
                                        None, G, lambda j: VP[:, j, 0, :], ones_bf[:, :], numb, denb, pexp, ptb, 1.0,
                                        [KTz[c], QTz[c]], [VP])
                                act(rd[0][:, :], PS[5][:, :], AF.Ln, [PS[5]], [rd[0]])
                                cp(o0[:, :], PS[4][:, :], [PS[4]], [o0])
                                act(rd[1][:, :], PS[7][:, :], AF.Ln, [PS[7]], [rd[1]])
                                cp(o1[:, :], PS[6][:, :], [PS[6]], [o1])

                                def epi(g=g, idx=idx):
                                    act(rd[0][:, :], rd[0][:, :], AF.Exp, [rd[0]], [rd[0]], scale=-1.0)
                                    act(rd[1][:, :], rd[1][:, :], AF.Exp, [rd[1]], [rd[1]], scale=-1.0)
                                    tt(o0[:, :], o0[:, :], rd[0][:, :], ALU.mult, [o0, rd[0]], [o0])
                                    tt(o1[:, :], o1[:, :], rd[1][:, :], ALU.mult, [o1, rd[1]], [o1])
                                    stt(o0[:, :], o1[:, :], small[:, 8 + i2:9 + i2], o0[:, :], ALU.mult, ALU.add, [o1, o0, small], [o0])
                                    act(sq[:, :], o0[:, :], AF.Square, [o0], [sq])
                                    ssb = PS[7]
                                    mm(ssb[:, :], ones_f[:, :], sq[:, :], True, True, [ones_f, sq], [ssb])
                                    act(sq[:, :], ssb[:, :], AF.Ln, [ssb], [sq], bias=EPS, scale=1.0 / 128)
                                    act(sq[:, :], sq[:, :], AF.Exp, [sq], [sq], scale=-0.5)
                                    stt(mixT[:, 4 + idx, g * 512:(g + 1) * 512], o0[:, :], small[:, 12 + i2:13 + i2], sq[:, :],
                                        ALU.mult, ALU.mult, [o0, small, sq], [mixT])
                                pending.append(epi)
                        flush_pending()
                        ckpt("u%d" % ui)
                P.barrier()
                with ExitStack() as sw:
                    w_out_residual(sw, mixT, ewout_d[i2])
                    norm_body(sw, 4 + li, list(range(NT)))
            P.barrier()

        def conv_ffn(li):
            w_in = fwin_d[li]
            w_out = fwout_d[li]
            with ExitStack() as sf:
                actT = sbt(sf, "actT", [128, NCC, 1024], BF16)
                wug = [sbt(sf, "wug%d" % i, [128, 8, 256], BF16) for i in range(3)]
                wo2 = [sbt(sf, "wo2_%d" % i, [128, NCC, 256], BF16) for i in range(2)]
                graw = [sbt(sf, "graw%d" % i, [128, 1040], F32) for i in range(2)]
                A = [sbt(sf, "A%d" % i, [128, 512], F32) for i in range(2)]
                Ag = [sbt(sf, "Ag%d" % i, [128, 512], F32) for i in range(2)]
                gtail = sbt(sf, "gtail", [128, NCC, 2], F32)

                def load_pair(slot, cc):
                    src_u = w_in[:, cc * 128:(cc + 1) * 128].rearrange("(kc p) n -> p kc n", p=128)
                    src_g = w_in[:, DFF + cc * 128:DFF + (cc + 1) * 128].rearrange("(kc p) n -> p kc n", p=128)
                    dma("pool", wug[slot][:, :, 0:128], src_u, [], [wug[slot]])
                    dma("pool", wug[slot][:, :, 128:256], src_g, [], [wug[slot]])

                k = 0
                for half in range(2):
                    tiles = list(range(half * 8, half * 8 + 8))
                    load_pair(0, 0)
                    load_pair(1, 1)
                    for cc in range(NCC):
                        if cc + 2 < NCC:
                            load_pair((cc + 2) % 3, cc + 2)
                        W = wug[cc % 3]
                        gr = graw[cc % 2]
                        if half == 0:
                            memset(gr[:, 14:16], 0.0, [gr], eng="dve")
                        else:
                            cp(gr[:, 14:16], gtail[:, cc, :], [gtail], [gr])
                        w0 = convw[:, (li * 3 + 0) * NCC + cc:(li * 3 + 0) * NCC + cc + 1]
                        w1 = convw[:, (li * 3 + 1) * NCC + cc:(li * 3 + 1) * NCC + cc + 1]
                        w2 = convw[:, (li * 3 + 2) * NCC + cc:(li * 3 + 2) * NCC + cc + 1]
                        bb = convw[:, (12 + li) * NCC + cc:(12 + li) * NCC + cc + 1]
                        for tgi in range(2):
                            tok0 = half * 1024 + tgi * 512
                            pu = PS[(k * 2) % 8]
                            pg = PS[(k * 2 + 1) % 8]
                            k += 1
                            for kc in range(8):
                                mm(pg[:, :], W[:, kc, 128:256], hnT[:, kc, tok0:tok0 + 512], kc == 0, kc == 7, [W, hnT], [pg])
                            for kc in range(8):
                                mm(pu[:, :], W[:, kc, 0:128], hnT[:, kc, tok0:tok0 + 512], kc == 0, kc == 7, [W, hnT], [pu])
                            a = A[tgi]
                            ag = Ag[tgi]
                            act(gr[:, 16 + tgi * 512:16 + (tgi + 1) * 512], pg[:, :], AF.Copy, [pg], [gr])
                            ts(a[:, :], gr[:, 16 + tgi * 512:16 + (tgi + 1) * 512], w2, bb, ALU.mult, ALU.add, [gr, convw], [a])
                            stt(a[:, :], gr[:, 15 + tgi * 512:15 + (tgi + 1) * 512], w1, a[:, :], ALU.mult, ALU.add, [gr, a, convw], [a], eng=FMA_ENG)
                            stt(a[:, :], gr[:, 14 + tgi * 512:14 + (tgi + 1) * 512], w0, a[:, :], ALU.mult, ALU.add, [gr, a, convw], [a], eng=FMA_ENG)
                            act(ag[:, :], a[:, :], AF.Gelu, [a], [ag])
                            tt(actT[:, cc, tgi * 512:(tgi + 1) * 512], ag[:, :], pu[:, :], ALU.mult, [ag, pu], [actT])
                        if half == 0:
                            cp(gtail[:, cc, :], gr[:, 1038:1040], [gr], [gtail])
                    load_cols = lambda slot, cq: dma(
                        "pool", wo2[slot][:, :, :],
                        w_out[:, cq * 256:(cq + 1) * 256].rearrange("(cc p) n -> p cc n", p=128), [], [wo2[slot]])
                    load_cols(0, 0)
                    for cq in range(4):
                        if cq + 1 < 4:
                            load_cols((cq + 1) % 2, cq + 1)
                        Wo = wo2[cq % 2]
                        for tl in range(8):
                            t = half * 8 + tl
                            ps = PS[k % 8]
                            k += 1
                            for cc in range(NCC):
                                mm(ps[:, 0:256], actT[:, cc, tl * 128:(tl + 1) * 128], Wo[:, cc, :], cc == 0, cc == NCC - 1,
                                   [actT, Wo], [ps])
                            tt(h[:, t, cq * 256:(cq + 1) * 256], ps[:, 0:256], h[:, t, cq * 256:(cq + 1) * 256], ALU.add,
                               [ps, hB[t]], [hB[t]])
            P.barrier()

        def odd_mixer(li):
            i2 = li // 2
            w_in = owin_d[i2]
            with ExitStack() as sm:
                mixT = sbt(sm, "mixTo", [128, 8, S], BF16)
                norm_transpose(2 + i2, list(range(NT)))
                with ExitStack() as sa:
                    QTz = [sbt(sa, "oQTz%d" % i, [128, S], BF16) for i in range(2)]
                    KTz = [sbt(sa, "oKTz%d" % i, [128, S], BF16) for i in range(2)]
                    for i_ in range(2):
                        memset(QTz[i_][:, :], 0.0, [QTz[i_]], eng="dve")
                        memset(KTz[i_][:, :], 0.0, [KTz[i_]], eng="dve")
                    VS = sbt(sa, "VS", [128, NT, 2, 128], BF16)
                    accn = sbt(sa, "accn", [128, S], F32)
                    accd = sbt(sa, "accd", [128, S], F32)
                    Gd = sbt(sa, "Gd", [128, 3, 2, 256], BF16)
                    wq = [sbt(sa, "owq%d" % i, [128, 8, 128], BF16) for i in range(2)]
                    wk = [sbt(sa, "owk%d" % i, [128, 8, 128], BF16) for i in range(2)]
                    wv = [sbt(sa, "owv%d" % i, [128, 8, 128], BF16) for i in range(2)]
                    pexp = [sbt(sa, "opexp%d" % i, [128, 512], BF16) for i in range(3)]
                    ptb = [sbt(sa, "optb%d" % i, [128, 512], BF16) for i in range(3)]
                    rd = [sbt(sa, "ord%d" % i, [128, 512], F32) for i in range(2)]

                    memset(VS[:, :, :, :].rearrange("p a b c -> p (a b c)"), 0.0, [VS], eng="dve")
                    VT = sbt(sa, "VT", [128, S], BF16)

                    def load_unit_w(ui):
                        load_w(wq[ui % 2], w_in, ui * 128, 128, 8)
                        load_w(wk[ui % 2], w_in, 512 + ui * 128, 128, 8)
                        load_w(wv[ui % 2], w_in, 1024 + ui * 128, 128, 8)

                    load_unit_w(0)
                    sr = 0
                    vr = {"i": 0}
                    for ui in range(4):
                        if ui + 1 < 4:
                            load_unit_w(ui + 1)
                        Wq, Wk, Wv = wq[ui % 2], wk[ui % 2], wv[ui % 2]
                        def evac_k(tg, ps):
                            cp(KTz[0][0:64, tg * 512:(tg + 1) * 512], ps[0:64, :], [ps], [KTz[0]])
                            cp(KTz[1][64:128, tg * 512:(tg + 1) * 512], ps[64:128, :], [ps], [KTz[1]])

                        def evac_q(tg, ps):
                            ts(QTz[0][0:64, tg * 512:(tg + 1) * 512], ps[0:64, :], 0.125, None, ALU.mult, None, [ps], [QTz[0]])
                            ts(QTz[1][64:128, tg * 512:(tg + 1) * 512], ps[64:128, :], 0.125, None, ALU.mult, None, [ps], [QTz[1]])
                        proj_fm(Wk, 0, evac_k)
                        proj_fm(Wq, 0, evac_q)
                        proj_fm(Wv, 0, lambda tg, ps: cp(VT[:, tg * 512:(tg + 1) * 512], ps[:, :], [ps], [VT]))
                        for hh in range(2):
                            for gi in range(3):
                                idx = (ui * 2 + hh) * 3 + gi
                                src = bass.AP(m_dil, idx * 128 * 512 + 127, [[511, 128], [1, 256]])
                                dma("sp", Gd[:, gi, hh, :], src, [B_mdil[idx]], [Gd])
                        first = True
                        for gi, dil in enumerate((1, 4, 16)):
                            L = S // dil
                            nblk = max(1, L // 128)
                            for r in range(dil):
                                for n0 in range(0, nblk, 4):
                                    nn = min(4, nblk - n0)
                                    ps = PS[(3, 0, 1, 2)[vr["i"] % 4]]
                                    vr["i"] += 1
                                    psb = ps[:, :].bitcast(BF16)
                                    for n_ in range(nn):
                                        n = n0 + n_
                                        t0 = n * 128 * dil + r
                                        trp(psb[:, n_ * 128:(n_ + 1) * 128], VT[:, t0:t0 + 127 * dil + 1:dil], idb[:, :], [VT, idb], [ps])
                                    slot0 = r * nblk + n0
                                    psv = psb[:, 0:nn * 128].rearrange("p (a b) -> p a b", b=128)
                                    act(VS[:, slot0:slot0 + nn, 0, 0:64], psv[:, :, 0:64], AF.Copy, [ps], [VS])
                                    act(VS[:, slot0:slot0 + nn, 1, 64:128], psv[:, :, 64:128], AF.Copy, [ps], [VS])
                            ckpt("d%dv%d" % (ui, gi))
                            tasks = []
                            for r in range(dil):
                                for seg0 in range(0, nblk, 4):
                                    segn = min(4, nblk - seg0)
                                    W_ = segn * 128
                                    numb, denb = (PS[4], PS[5]) if sr % 2 == 0 else (PS[6], PS[7])
                                    sr += 1
                                    kbs = [kb for kb in range(seg0 - 1, seg0 + segn) if kb >= 0]
                                    for ki, kb in enumerate(kbs):
                                        q0 = max(kb, seg0)
                                        q1 = min(kb + 2, seg0 + segn)
                                        N = (q1 - q0) * 128
                                        xoff = (q0 - kb) * 128
                                        cpos = (q0 - seg0) * 128
                                        kt0 = kb * 128 * dil + r
                                        qt0 = q0 * 128 * dil + r
                                        for hh in range(2):
                                            st_ = {}

                                            def score_fn(N=N, kt0=kt0, qt0=qt0, st_=st_, hh=hh):
                                                sb = PS[rot["s"] % 4]
                                                rot["s"] += 1
                                                st_["sb"] = sb
                                                mm(sb[:, 0:N], KTz[hh][:, kt0:kt0 + 127 * dil + 1:dil],
                                                   QTz[hh][:, qt0:qt0 + (N - 1) * dil + 1:dil], True, True, [KTz[hh], QTz[hh]], [sb])

                                            def rest_fn(N=N, xoff=xoff, cpos=cpos, kb=kb, r=r, st_=st_, numb=numb, denb=denb, hh=hh,
                                                        first_k=(ki == 0 and hh == 0), last_k=(ki == len(kbs) - 1 and hh == 1),
                                                        W_=W_, seg0=seg0, first=first):
                                                sb = st_["sb"]
                                                pe_ = pexp[rot["p"] % 3]
                                                pt = ptb[rot["p"] % 3]
                                                rot["p"] += 1
                                                act(pe_[:, 0:N], sb[:, 0:N], AF.Exp, [sb], [pe_])
                                                tt(pt[:, 0:N], pe_[:, 0:N], Gd[:, gi, hh, xoff:xoff + N], ALU.mult, [pe_, Gd], [pt])
                                                mm(numb[:, cpos:cpos + N], VS[:, r * nblk + kb, hh, :], pt[:, 0:N], first_k, False,
                                                   [pt, VS], [numb])
                                                mm(denb[:, cpos:cpos + N], (onesA if hh == 0 else onesB)[:, :], pt[:, 0:N], first_k, False,
                                                   [pt, onesA, onesB], [denb])
                                                if last_k:
                                                    tq0 = seg0 * 128 * dil + r
                                                    dst_n = accn[:, tq0:tq0 + (W_ - 1) * dil + 1:dil]
                                                    dst_d = accd[:, tq0:tq0 + (W_ - 1) * dil + 1:dil]
                                                    if first:
                                                        cp(dst_n, numb[:, 0:W_], [numb], [accn])
                                                        cp(dst_d, denb[:, 0:W_], [denb], [accd])
                                                    else:
                                                        tt(dst_n, numb[:, 0:W_], dst_n, ALU.add, [numb, accn], [accn])
                                                        tt(dst_d, denb[:, 0:W_], dst_d, ALU.add, [denb, accd], [accd])
                                            tasks.append((score_fn, rest_fn))
                            run_pipeline(tasks, 3)
                            ckpt("d%db%d" % (ui, gi))
                            first = False
                        ckpt("d%dpre" % ui)
                        for g in range(4):
                            r_ = rd[g % 2]
                            recip(r_[:, :], accd[:, g * 512:(g + 1) * 512], [accd], [r_])
                            tt(mixT[:, ui, g * 512:(g + 1) * 512], accn[:, g * 512:(g + 1) * 512], r_[:, :], ALU.mult, [accn, r_], [mixT])
                P.barrier()
                with ExitStack() as sb_:
                    wuq = sbt(sb_, "wuq", [128, 2, 768], BF16)
                    wukv = sbt(sb_, "wukv", [128, 1, 1024], BF16)
                    cqT = sbt(sb_, "cqT", [128, 2, S], BF16)
                    ckvT = sbt(sb_, "ckvT", [128, 1, S], BF16)
                    krT = sbt(sb_, "krT", [128, S], BF16)
                    rcs = sbt(sb_, "rcs", [64, 2 * S], F32)
                    xs1 = [sbt(sb_, "xs1_%d" % i, [64, 512], F32) for i in range(2)]
                    xs2 = [sbt(sb_, "xs2_%d" % i, [64, 512], F32) for i in range(2)]
                    lst = sbt(sb_, "lst", [128, 96], F32)
                    rr = {"i": 0}

                    def rope(dst_ap, dst_T, ps, col0, tok0, n):
                        a = xs1[rr["i"] % 2]
                        b = xs2[rr["i"] % 2]
                        rr["i"] += 1
                        tt(b[0:32, 0:n], ps[32:64, col0:col0 + n], rcs[0:32, S + tok0:S + tok0 + n], ALU.mult, [ps, rcs], [b])
                        tt(b[32:64, 0:n], ps[0:32, col0:col0 + n], rcs[32:64, S + tok0:S + tok0 + n], ALU.mult, [ps, rcs], [b])
                        tt(a[0:64, 0:n], ps[0:64, col0:col0 + n], rcs[0:64, tok0:tok0 + n], ALU.mult, [ps, rcs], [a])
                        tt(dst_ap, a[0:64, 0:n], b[0:64, 0:n], ALU.add, [a, b], [dst_T])

                    memset(krT[64:128, :], 0.0, [krT], eng="dve")
                    load_w(wuq, wuq_d[i2], 0, 768, 2)
                    load_w(wukv, wukv_d[i2], 0, 1024, 1)
                    dma("sp", rcs[0:64, 0:S], c_ropec, [], [rcs])
                    dma("sp", rcs[0:64, S:2 * S], c_ropes, [], [rcs])
                    with ExitStack() as s1:
                        wl = sbt(s1, "wl", [128, 8, 448], BF16)
                        lat = [sbt(s1, "lat%d" % i, [128, 448], F32) for i in range(4)]
                        load_w(wl, w_in, 1536, 448, 8)
                        lB = [Buf("lst%d" % t) for t in range(NT)]
                        memset(lst[:, 0:32], 0.0, lB, eng="dve")
                        for t in range(NT):
                            ps = PS[t % 4]
                            la = lat[t % 4]
                            for kc in range(8):
                                mm(ps[:, 0:448], hnT[:, kc, t * 128:(t + 1) * 128], wl[:, kc, :], kc == 0, kc == 7, [hnT, wl], [ps])
                            act(la[:, 0:256], ps[:, 0:256], AF.Square, [ps], [la, lB[t]], accum=lst[:, t:t + 1])
                            act(la[:, 256:384], ps[:, 256:384], AF.Square, [ps], [la, lB[t]], accum=lst[:, 16 + t:17 + t])
                            act(lst[:, 32 + t:33 + t], lst[:, t:t + 1], AF.Ln, [lB[t]], [lB[t]], bias=EPS, scale=1.0 / 256)
                            act(lst[:, 32 + t:33 + t], lst[:, 32 + t:33 + t], AF.Exp, [lB[t]], [lB[t]], scale=-0.5)
                            act(lst[:, 48 + t:49 + t], lst[:, 16 + t:17 + t], AF.Ln, [lB[t]], [lB[t]], bias=EPS, scale=1.0 / 128)
                            act(lst[:, 48 + t:49 + t], lst[:, 48 + t:49 + t], AF.Exp, [lB[t]], [lB[t]], scale=-0.5)
                            act(la[:, 0:256], ps[:, 0:256], AF.Copy, [ps, lB[t]], [la], scale=lst[:, 32 + t:33 + t])
                            act(la[:, 256:384], ps[:, 256:384], AF.Copy, [ps, lB[t]], [la], scale=lst[:, 48 + t:49 + t])
                            act(la[:, 384:448], ps[:, 384:448], AF.Copy, [ps], [la])
                            pt_ = PS[4 + t % 4]
                            for c in range(3):
                                trp(pt_[:, c * 128:(c + 1) * 128], la[:, c * 128:(c + 1) * 128], idf[:, :], [la, idf], [pt_])
                            trp(pt_[0:64, 384:512], la[:, 384:448], idf[:, :], [la, idf], [pt_])
                            for c in range(2):
                                ts(cqT[:, c, t * 128:(t + 1) * 128], pt_[:, c * 128:(c + 1) * 128], small[:, 2 + 2 * i2 + c:3 + 2 * i2 + c],
                                   None, ALU.mult, None, [pt_, small], [cqT])
                            ts(ckvT[:, 0, t * 128:(t + 1) * 128], pt_[:, 256:384], small[:, 6 + i2:7 + i2], None, ALU.mult, None,
                               [pt_, small], [ckvT])
                            rope_dst = krT[0:64, t * 128:(t + 1) * 128]
                            rope2(rope_dst, krT, pt_, 384, t * 128, 128, rope)
                    P.barrier()
                    with ExitStack() as s2:
                        qnT = sbt(s2, "qnT", [128, S], BF16)
                        qrT = sbt(s2, "qrT", [128, S], BF16)
                        memset(qrT[64:128, :], 0.0, [qrT], eng="dve")
                        knT = sbt(s2, "knT", [128, S], BF16)
                        VM = sbt(s2, "VM", [128, NT, 128], BF16)
                        ptb = [sbt(s2, "mptb%d" % i, [128, 512], BF16) for i in range(3)]
                        rd = [sbt(s2, "mrd%d" % i, [128, 512], F32) for i in range(2)]
                        scale = 192.0 ** -0.5
                        for hd in range(4):
                            proj_fm(wuq, hd * 192, lambda tg, ps: cp(qnT[:, tg * 512:(tg + 1) * 512], ps[:, :], [ps], [qnT]),
                                    kchunks=2, src=cqT)
                            proj_fm(wuq, hd * 192 + 128,
                                    lambda tg, ps: rope2(qrT[0:64, tg * 512:(tg + 1) * 512], qrT, ps, 0, tg * 512, 512, rope),
                                    kchunks=2, src=cqT, M=64)
                            proj_fm(wukv, hd * 256, lambda tg, ps: cp(knT[:, tg * 512:(tg + 1) * 512], ps[:, :], [ps], [knT]),
                                    kchunks=1, src=ckvT)
                            for tq in range(4):
                                ps = PS[3 + tq % 2]
                                for tt_ in range(4):
                                    t = tq * 4 + tt_
                                    mm(ps[:, tt_ * 128:(tt_ + 1) * 128], ckvT[:, 0, t * 128:(t + 1) * 128],
                                       wukv[:, 0, hd * 256 + 128:hd * 256 + 256], True, True, [ckvT, wukv], [ps])
                                act(VM[:, tq * 4:(tq + 1) * 4, :], ps[:, :].rearrange("p (a b) -> p a b", b=128), AF.Copy, [ps], [VM])
                            for g in range(4):
                                numb, denb = (PS[4], PS[5]) if g % 2 == 0 else (PS[6], PS[7])
                                attn_qgroup(
                                    g,
                                    lambda j: knT[:, j * 128:(j + 1) * 128],
                                    lambda g_, c0, N: qnT[:, g_ * 512 + c0:g_ * 512 + c0 + N],
                                    lambda j, g_, c0, N: (krT[:, j * 128:(j + 1) * 128],
                                                          qrT[:, g_ * 512 + c0:g_ * 512 + c0 + N]),
                                    None, lambda j: VM[:, j, :], ones_bf[:, :], numb, denb, None, ptb, scale,
                                    [knT, qnT, krT, qrT], [VM], mla=True)
                                def epi(g=g, numb=numb, denb=denb, hd=hd):
                                    r_ = rd[g % 2]
                                    recip(r_[:, :], denb[:, :], [denb], [r_])
                                    tt(mixT[:, 4 + hd, g * 512:(g + 1) * 512], numb[:, :], r_[:, :], ALU.mult, [numb, r_], [mixT])
                                pending.append(epi)
                            flush_pending()
                P.barrier()
                with ExitStack() as sw:
                    w_out_residual(sw, mixT, owout_d[i2])
                    norm_body(sw, 4 + li, list(range(NT)))
            P.barrier()

        try:
            ckpt("setup")
            for li in range(n_layers):
                if li % 2 == 0:
                    even_mixer(li)
                else:
                    odd_mixer(li)
                ckpt("mixer%d" % li)
                conv_ffn(li)
        except _Stop:
            pass
        P.stopped = False
        P.barrier()

        with ExitStack() as sfin:
            gfin = sbt(sfin, "gfin", [128, D], F32)
            yb = [sbt(sfin, "yb%d" % i, [128, D], F32) for i in range(2)]
            dma("sp", gfin[:, :], fin_d.partition_broadcast(128), [], [gfin])
            for t in range(NT):
                y = yb[t % 2]
                if debug_h:
                    cp(y[:, :], h[:, t, :], [hB[t]], [y])
                else:
                    act(y[:, :], h[:, t, :], AF.Square, [hB[t]], [y, stat], accum=stat[:, t:t + 1])
                    act(stat[:, 16 + t:17 + t], stat[:, t:t + 1], AF.Ln, [stat], [stat], bias=EPS, scale=1.0 / D)
                    act(stat[:, 32 + t:33 + t], stat[:, 16 + t:17 + t], AF.Exp, [stat], [stat], scale=-0.5)
                    stt(y[:, :], h[:, t, :], stat[:, 32 + t:33 + t], gfin[:, :], ALU.mult, ALU.mult, [hB[t], stat, gfin], [y])
                dma("sp", out_d[t * 128:(t + 1) * 128, :], y[:, :], [y], [B_out])
            P.wait_all("sp", [B_out])
            P.barrier()

        with nc.Block() as block:
            P.emit(block)
    return nc


_CACHE = {}


def kernel(**inputs):
    n = 8
    consts = _host_consts()
    if "nc" not in _CACHE:
        _CACHE["nc"] = build_program()
    nc = _CACHE["nc"]
    x = np.ascontiguousarray(np.asarray(inputs["x"], dtype=np.float32))
    shared = {k: np.ascontiguousarray(np.asarray(v, dtype=np.float32)) for k, v in inputs.items() if k != "x"}
    shared.update(consts)
    in_maps = []
    for c in range(n):
        m = dict(shared)
        m["x"] = x[c]
        in_maps.append(m)
    res = run_bass_kernel_spmd(nc, in_maps, core_ids=list(range(n)))
    out = np.stack([np.asarray(r["out"], dtype=np.float32) for r in res.results], axis=0)
    return out
```

```python
import math
from contextlib import ExitStack

import numpy as np
import concourse.bass as bass
import concourse.mybir as mybir
from concourse.bass_utils import run_bass_kernel_spmd

F32 = mybir.dt.float32
BF16 = mybir.dt.bfloat16
ALU = mybir.AluOpType
AF = mybir.ActivationFunctionType
AX = mybir.AxisListType

S = 2048
D = 1024
NT = 16
DEPTH = 4
DFF = 2816
NCC = 22
EPS = 1e-6
LW = 2560
GW = 2432
NEGBIG = -30000.0


class Buf:
    __slots__ = ("name", "w", "rs")

    def __init__(self, name):
        self.name = name
        self.w = None
        self.rs = []


class T:
    __slots__ = ("t", "b")

    def __init__(self, t, name):
        self.t = t
        self.b = Buf(name)

    def __getitem__(self, k):
        return self.t[k]


class Prog:
    ENGS = ("pe", "act", "dve", "pool", "sp")

    def __init__(self, nc, n_dma_sems=8):
        self.nc = nc
        self.ops = {e: [] for e in self.ENGS}
        self.cnt = {e: 0 for e in self.ENGS}
        self.waited = {e: {} for e in self.ENGS}
        self.sems = {}
        self.n_dma_sems = n_dma_sems
        self.dma_i = {"sp": 0, "pool": 0, "act": 0}
        self.dma_last = {}
        self.stopped = False

    def alloc_sems(self, stack):
        for e in self.ENGS:
            self.sems[e] = stack.enter_context(self.nc.semaphore("s_" + e))
        for q in ("sp", "pool", "act"):
            for i in range(self.n_dma_sems):
                self.sems[("dma", q, i)] = stack.enter_context(self.nc.semaphore(f"d_{q}_{i}"))

    def _need(self, eng, waits, ev, kind):
        if ev is None:
            return
        semkey, val, src = ev
        if src == eng and semkey == eng:
            if eng == "pe" or kind == "war":
                return
        if self.waited[eng].get(semkey, 0) >= val:
            return
        if waits.get(semkey, 0) < val:
            waits[semkey] = val

    def _deps(self, eng, reads, writes):
        waits = {}
        for b in reads:
            self._need(eng, waits, b.w, "raw")
        for b in writes:
            self._need(eng, waits, b.w, "waw")
            for r in b.rs:
                self._need(eng, waits, r, "war")
        for k, v in waits.items():
            self.waited[eng][k] = v
        return list(waits.items())

    def op(self, eng, fn, reads=(), writes=()):
        if self.stopped:
            return None
        reads = [x.b if isinstance(x, T) else x for x in reads]
        writes = [x.b if isinstance(x, T) else x for x in writes]
        waits = self._deps(eng, reads, writes)
        self.cnt[eng] += 1
        ev = (eng, self.cnt[eng], eng)
        self.ops[eng].append((waits, fn, (eng, 1)))
        for b in reads:
            b.rs.append(ev)
            if len(b.rs) > 64:
                b.rs = self._prune(b.rs)
        for b in writes:
            b.w = ev
            b.rs = []
        return ev

    @staticmethod
    def _prune(rs):
        best = {}
        for (k, v, s) in rs:
            if k not in best or best[k][1] < v:
                best[k] = (k, v, s)
        return list(best.values())

    def dma(self, q, fn, reads=(), writes=()):
        if self.stopped:
            return None
        reads = [x.b if isinstance(x, T) else x for x in reads]
        writes = [x.b if isinstance(x, T) else x for x in writes]
        waits = self._deps(q, reads, writes)
        i = self.dma_i[q]
        self.dma_i[q] += 1
        slot = i % self.n_dma_sems
        semkey = ("dma", q, slot)
        val = 16 * (i // self.n_dma_sems + 1)
        if val > 16 and self.waited[q].get(semkey, 0) < val - 16:
            waits.append((semkey, val - 16))
            self.waited[q][semkey] = val - 16
        ev = (semkey, val, q + "_dma")
        self.dma_last[semkey] = val
        self.ops[q].append((waits, fn, (semkey, 16)))
        for b in reads:
            b.rs.append(ev)
        for b in writes:
            b.w = ev
            b.rs = []
        return ev

    def barrier(self):
        if self.stopped:
            return
        for e in self.ENGS:
            waits = []
            for o in self.ENGS:
                if o == e:
                    continue
                v = self.cnt[o]
                if v > 0 and self.waited[e].get(o, 0) < v:
                    waits.append((o, v))
                    self.waited[e][o] = v
            for k, v in self.dma_last.items():
                if self.waited[e].get(k, 0) < v:
                    waits.append((k, v))
                    self.waited[e][k] = v
            if waits:
                self.ops[e].append((waits, None, None))

    def wait_all(self, eng, bufs):
        bufs = [x.b if isinstance(x, T) else x for x in bufs]
        waits = self._deps(eng, bufs, ())
        self.ops[eng].append((waits, None, None))

    def emit(self, block):
        sems = self.sems

        def run(e, lst):
            for waits, fn, inc in lst:
                for k, v in waits:
                    e.wait_ge(sems[k], v)
                if fn is not None:
                    fn(e).then_inc(sems[inc[0]], inc[1])

        @block.tensor
        def _(e):
            run(e, self.ops["pe"])

        @block.scalar
        def _(e):
            run(e, self.ops["act"])

        @block.vector
        def _(e):
            run(e, self.ops["dve"])

        @block.gpsimd
        def _(e):
            run(e, self.ops["pool"])

        @block.sync
        def _(e):
            run(e, self.ops["sp"])


def _np_bucket(dist):
    n = np.maximum(dist, 0)
    nf = np.maximum(n, 1).astype(np.float32)
    large = 16 + (np.log(nf / np.float32(16)) / np.float32(math.log(64)) * np.float32(16)).astype(np.int32)
    large = np.minimum(large, 31)
    return np.where(n < 16, n, large)


def _host_consts():
    c = {}
    oh = np.zeros((33, LW), np.float32)
    d = np.arange(LW) - 511
    b = _np_bucket(d)
    for i in range(LW):
        if d[i] < 0:
            oh[32, i] = 1.0
        else:
            oh[b[i], i] = 1.0
    c["c_oh_main"] = oh
    ohd = np.zeros((33, 3 * 512), np.float32)
    for gi, dil in enumerate((1, 4, 16)):
        for i in range(512):
            s = i - 127
            if 0 <= s <= 128:
                ohd[_np_bucket(np.array([s * dil]))[0], gi * 512 + i] = 1.0
            else:
                ohd[32, gi * 512 + i] = 1.0
    c["c_oh_dil"] = ohd
    koh = np.zeros((8, S), np.float32)
    for n in range(8):
        koh[n, n * 256:(n + 1) * 256] = 1.0
    c["c_koh"] = koh
    gm = np.zeros((2, 16, 8), np.float32)
    om = np.ones((2, 16, 8), np.float32)
    for t in range(16):
        bq = t // 2
        gm[:, t, bq:] = -1e30
        om[:, t, bq] = 0.0
    c["c_gmask"] = np.broadcast_to(gm.reshape(1, 256), (128, 256)).copy()
    c["c_omask"] = np.broadcast_to(om.reshape(1, 256), (128, 256)).copy()
    half = 32
    freq = (np.float32(10000.0) ** (-np.arange(half, dtype=np.float32) / np.float32(half))).astype(np.float32)
    ang = np.arange(S, dtype=np.float32)[None, :] * freq[:, None]
    cs, sn = np.cos(ang).astype(np.float32), np.sin(ang).astype(np.float32)
    c["c_ropec"] = np.concatenate([cs, cs], 0)
    c["c_ropes"] = np.concatenate([-sn, sn], 0)
    tri = (np.arange(128)[None, :] >= np.arange(128)[:, None]).astype(np.float32)
    c["c_tri"] = tri
    return c


import os as _os
DIL_SPLIT = bool(_os.environ.get("DIL_SPLIT"))
FMA_ENG = "dve"


class _Stop(Exception):
    pass


def build_program(n_layers=DEPTH, debug_h=False, stop_at=None):
    nc = bass.Bass("TRN2", target_bir_lowering=False)

    stop_state = {"P": None}

    def ckpt(name):
        if stop_at == name:
            stop_state["P"].stopped = True

    dram_in = {}

    def din(name, shape):
        dram_in[name] = nc.dram_tensor(name, list(shape), F32, kind="ExternalInput").ap()
        return dram_in[name]

    x_d = din("x", [S, D])
    rel_d = din("rel_bias", [32, 20])
    en1_d = din("even_norm1", [2, D])
    ewin_d = din("even_w_in", [2, D, 3072])
    dlam_d = din("diff_lambda", [2, 4, 64])
    dsub_d = din("diff_subln", [2, 128])
    ewout_d = din("even_w_out", [2, D, D])
    on1_d = din("odd_norm1", [2, D])
    owin_d = din("odd_w_in", [2, D, 1984])
    qn_d = din("mla_q_norm", [2, 256])
    wuq_d = din("mla_w_uq", [2, 256, 768])
    kvn_d = din("mla_kv_norm", [2, 128])
    wukv_d = din("mla_w_ukv", [2, 128, 1024])
    owout_d = din("odd_w_out", [2, D, D])
    fn_d = din("ffn_norm", [4, D])
    fwin_d = din("ffn_w_in", [4, D, 2 * DFF])
    fcw_d = din("ffn_conv_w", [4, 3, DFF])
    fcb_d = din("ffn_conv_b", [4, DFF])
    fwout_d = din("ffn_w_out", [4, DFF, D])
    fin_d = din("final_norm", [D])
    c_oh_main = din("c_oh_main", [33, LW])
    c_oh_dil = din("c_oh_dil", [33, 1536])
    c_koh = din("c_koh", [8, S])
    c_gmask = din("c_gmask", [128, 256])
    c_omask = din("c_omask", [128, 256])
    c_ropec = din("c_ropec", [64, S])
    c_ropes = din("c_ropes", [64, S])
    c_tri = din("c_tri", [128, 128])
    out_d = nc.dram_tensor("out", [S, D], F32, kind="ExternalOutput").ap()
    m_main = nc.dram_tensor("m_main", [12 * 128, LW], BF16)
    m_dil = nc.dram_tensor("m_dil", [24 * 128, 512], BF16)
    B_mmain = [Buf("mmain%d" % i) for i in range(12)]
    B_mdil = [Buf("mdil%d" % i) for i in range(24)]
    B_out = Buf("out")

    with ExitStack() as st:
        P = Prog(nc)
        stop_state["P"] = P
        P.alloc_sems(st)
        st.enter_context(nc.allow_non_contiguous_dma("small strided parameter loads"))

        uniq = {"i": 0}

        def sbt(stack, name, shape, dt):
            uniq["i"] += 1
            name = "%s_%d" % (name, uniq["i"])
            return T(stack.enter_context(nc.sbuf_tensor(name, list(shape), dt)), name)

        def mm(out, lhsT, rhs, start, stop, r, w):
            P.op("pe", lambda e: e.matmul(out, lhsT=lhsT, rhs=rhs, start=start, stop=stop,
                                          skip_group_check=True), r, w)

        def trp(out, in_, ident, r, w):
            P.op("pe", lambda e: e.transpose(out=out, in_=in_, identity=ident), r, w)

        def act(out, in_, func, r, w, bias=None, scale=None, accum=None):
            kw = {}
            if bias is not None:
                kw["bias"] = bias
            if scale is not None:
                kw["scale"] = scale
            if accum is not None:
                kw["accum_out"] = accum
            P.op("act", lambda e: e.activation(out=out, in_=in_, func=func, **kw), r, w)

        def tt(out, in0, in1, op, r, w, eng="dve"):
            P.op(eng, lambda e: e.tensor_tensor(out=out, in0=in0, in1=in1, op=op), r, w)

        def ts(out, in0, s1, s2, op0, op1, r, w, eng="dve"):
            if s2 is None:
                P.op(eng, lambda e: e.tensor_scalar(out=out, in0=in0, scalar1=s1, scalar2=None, op0=op0), r, w)
            else:
                P.op(eng, lambda e: e.tensor_scalar(out=out, in0=in0, scalar1=s1, scalar2=s2, op0=op0, op1=op1), r, w)

        def stt(out, in0, scalar, in1, op0, op1, r, w, eng="dve"):
            P.op(eng, lambda e: e.scalar_tensor_tensor(out=out, in0=in0, scalar=scalar, in1=in1, op0=op0, op1=op1), r, w)

        def cp(out, in_, r, w, eng="dve"):
            P.op(eng, lambda e: e.tensor_copy(out=out, in_=in_), r, w)

        def recip(out, in_, r, w):
            act(out, in_, AF.Ln, r, w)
            act(out, out, AF.Exp, list(w), w, scale=-1.0)

        def memset(ap, val, w, eng="pool"):
            P.op(eng, lambda e: e.memset(ap, val), (), w)

        def dma(q, out, in_, r, w):
            P.dma(q, lambda e: e.dma_start(out=out, in_=in_), r, w)

        def run_pipeline(tasks, depth):
            n = len(tasks)
            for i in range(min(depth, n)):
                tasks[i][0]()
            for i in range(n):
                if i + depth < n:
                    tasks[i + depth][0]()
                tasks[i][1]()

        def rope2(dst_ap, dst_T, ps, col0, tok0, n, rope_fn):
            rope_fn(dst_ap, dst_T, ps, col0, tok0, n)

        h = sbt(st, "h", [128, NT, D], F32)
        hB = [Buf("h%d" % t) for t in range(NT)]
        hnT = sbt(st, "hnT", [128, 8, S], BF16)
        idf = sbt(st, "idf", [128, 128], F32)
        idb = sbt(st, "idb", [128, 128], BF16)
        ones_bf = sbt(st, "ones_bf", [128, 128], BF16)
        onesA = sbt(st, "onesA", [128, 128], BF16)
        onesB = sbt(st, "onesB", [128, 128], BF16)
        ones_f = sbt(st, "ones_f", [128, 128], F32)
        tri_bf = sbt(st, "tri_bf", [128, 128], BF16)
        gmask = sbt(st, "gmask", [128, 256], F32)
        omask = sbt(st, "omask", [128, 256], F32)
        convw = sbt(st, "convw", [128, 16 * NCC], F32)
        small = sbt(st, "small", [128, 64], F32)
        lamb = sbt(st, "lamb", [128, 2 * 256], F32)
        stat = sbt(st, "stat", [128, 64], F32)
        PS = [T(st.enter_context(nc.psum_tensor("ps%d" % i, [128, 512], F32)), "ps%d" % i) for i in range(8)]

        memset(idf[:], 0.0, [idf])
        P.op("pool", lambda e: e.affine_select(out=idf[:], in_=idf[:], pattern=[[-1, 128]],
                                               compare_op=ALU.not_equal, fill=1.0, base=0,
                                               channel_multiplier=1), [idf], [idf])
        cp(idb[:, :], idf[:, :], [idf], [idb])
        memset(ones_bf[:], 1.0, [ones_bf])
        memset(ones_f[:], 1.0, [ones_f])
        memset(onesA[:], 0.0, [onesA])
        memset(onesA[:, 0:64], 1.0, [onesA])
        memset(onesB[:], 0.0, [onesB])
        memset(onesB[:, 64:128], 1.0, [onesB])
        dma("pool", tri_bf[:], c_tri, [], [tri_bf])
        dma("sp", gmask[:], c_gmask, [], [gmask])
        dma("sp", omask[:], c_omask, [], [omask])
        for t in range(NT):
            dma("sp", h[:, t, :], x_d[t * 128:(t + 1) * 128, :], [], [hB[t]])
        with ExitStack() as sc:
            raws = [sbt(sc, "craw%d" % i, [128, 128], F32) for i in range(4)]
            for i_ in range(4):
                memset(raws[i_][:, :], 0.0, [raws[i_]], eng="dve")
            vecs = [fcw_d[li, k] for li in range(4) for k in range(3)] + [fcb_d[li] for li in range(4)]
            for v, vec in enumerate(vecs):
                tl, j = v // 5, v % 5
                dma("sp", raws[tl][j * NCC:(j + 1) * NCC, :], vec.rearrange("(c p) -> c p", p=128), [raws[tl]], [raws[tl]])
            for tl in range(4):
                nv = min(5, 16 - tl * 5)
                ps = PS[tl]
                trp(ps[:, 0:128], raws[tl][:, :], idf[:, :], [raws[tl], idf], [ps])
                cp(convw[:, tl * 5 * NCC:(tl * 5 + nv) * NCC], ps[:, 0:nv * NCC], [ps], [convw])
            P.barrier()
        for i in range(2):
            dma("sp", small[:, i:i + 1], dsub_d[i].rearrange("(c p) -> p c", p=128), [], [small])
            dma("sp", small[:, 2 + 2 * i:4 + 2 * i], qn_d[i].rearrange("(c p) -> p c", p=128), [], [small])
            dma("sp", small[:, 6 + i:7 + i], kvn_d[i].rearrange("(c p) -> p c", p=128), [], [small])
            dma("sp", lamb[:, i * 256:(i + 1) * 256],
                dlam_d[i].rearrange("a b -> (a b)").partition_broadcast(128), [], [lamb])
        for i in range(2):
            layer = 2 * i
            lam_init = 0.8 - 0.6 * math.exp(-0.3 * layer)
            lp = lamb[:, i * 256:(i + 1) * 256]
            tt(lamb[:, i * 256:i * 256 + 64], lamb[:, i * 256:i * 256 + 64], lamb[:, i * 256 + 64:i * 256 + 128], ALU.mult, [lamb], [lamb])
            tt(lamb[:, i * 256 + 128:i * 256 + 192], lamb[:, i * 256 + 128:i * 256 + 192], lamb[:, i * 256 + 192:i * 256 + 256], ALU.mult, [lamb], [lamb])
            P.op("dve", lambda e, i=i: e.reduce_sum(out=small[:, 16 + 2 * i:17 + 2 * i], in_=lamb[:, i * 256:i * 256 + 64], axis=AX.X), [lamb], [small])
            P.op("dve", lambda e, i=i: e.reduce_sum(out=small[:, 17 + 2 * i:18 + 2 * i], in_=lamb[:, i * 256 + 128:i * 256 + 192], axis=AX.X), [lamb], [small])
            act(small[:, 20 + 2 * i:22 + 2 * i], small[:, 16 + 2 * i:18 + 2 * i], AF.Exp, [small], [small])
            tt(small[:, 8 + i:9 + i], small[:, 21 + 2 * i:22 + 2 * i], small[:, 20 + 2 * i:21 + 2 * i], ALU.subtract, [small], [small])
            ts(small[:, 8 + i:9 + i], small[:, 8 + i:9 + i], -lam_init, None, ALU.add, None, [small], [small])
            ts(small[:, 12 + i:13 + i], small[:, i:i + 1], 1.0 - lam_init, None, ALU.mult, None, [small], [small])

        f_main = nc.dram_tensor("f_main", [20, LW], BF16)
        f_dil = nc.dram_tensor("f_dil", [20, 1536], BF16)
        B_fmain = Buf("fmain")
        B_fdil = Buf("fdil")
        with ExitStack() as s0:
            tab = sbt(s0, "tab", [33, 20], F32)
            ohm = sbt(s0, "ohm", [33, LW], F32)
            ohd = sbt(s0, "ohd", [33, 1536], F32)
            fsb = sbt(s0, "fsb", [20, LW], BF16)
            fsd = sbt(s0, "fsd", [20, 1536], BF16)
            memset(tab[:], NEGBIG, [tab])
            dma("sp", tab[0:32, :], rel_d, [tab], [tab])
            dma("sp", ohm[:], c_oh_main, [], [ohm])
            dma("sp", ohd[:], c_oh_dil, [], [ohd])
            k = 0
            for c in range(LW // 512):
                ps = PS[k % 8]
                k += 1
                mm(ps[0:20, :], tab[0:33, 0:20], ohm[0:33, c * 512:(c + 1) * 512], True, True, [tab, ohm], [ps])
                act(fsb[0:20, c * 512:(c + 1) * 512], ps[0:20, :], AF.Exp, [ps], [fsb])
            for gi in range(3):
                ps = PS[k % 8]
                k += 1
                mm(ps[0:20, :], tab[0:33, 0:20], ohd[0:33, gi * 512:(gi + 1) * 512], True, True, [tab, ohd], [ps])
                act(fsd[0:20, gi * 512:(gi + 1) * 512], ps[0:20, :], AF.Exp, [ps], [fsd])
            dma("sp", f_main.ap()[:, :], fsb[0:20, :], [fsb], [B_fmain])
            dma("sp", f_dil.ap()[:, :], fsd[0:20, :], [fsd], [B_fdil])
            P.barrier()
        for hh in range(12):
            src = bass.AP(f_main, hh * LW, [[0, 128], [1, LW]])
            dma("sp", m_main.ap()[hh * 128:(hh + 1) * 128, :], src, [B_fmain], [B_mmain[hh]])
        for hh in range(8):
            for gi in range(3):
                idx = hh * 3 + gi
                src = bass.AP(f_dil, (12 + hh) * 1536 + gi * 512, [[0, 128], [1, 512]])
                dma("sp", m_dil.ap()[idx * 128:(idx + 1) * 128, :], src, [B_fdil], [B_mdil[idx]])

        def load_G(dst, head):
            src = bass.AP(m_main, head * 128 * LW + 127, [[LW - 1, 128], [1, GW]])
            dma("sp", dst[:, 0:GW], src, [B_mmain[head]], [dst])

        def load_Gdil(dst, idx):
            src = bass.AP(m_dil, idx * 128 * 512 + 127, [[511, 128], [1, 256]])
            dma("sp", dst, src, [B_mdil[idx]], [])

        gain_src = [en1_d[0], en1_d[1], on1_d[0], on1_d[1], fn_d[0], fn_d[1], fn_d[2], fn_d[3]]

        def norm_transpose(gidx, tiles):
            with ExitStack() as sn:
                norm_body(sn, gidx, tiles)
            P.barrier()

        def norm_body(sn, gidx, tiles):
            if True:
                gb = sbt(sn, "gb", [128, D], F32)
                junk = sbt(sn, "junk", [128, D], BF16)
                xg = [sbt(sn, "xg%d" % i, [128, D], BF16) for i in range(4)]
                dma("sp", gb[:, :], gain_src[gidx].partition_broadcast(128), [], [gb])
                sB = [Buf("st%d" % t) for t in range(NT)]
                hnB = [Buf("hn%d" % t) for t in range(NT)]
                memset(stat[:, 0:16], 0.0, sB, eng="dve")
                for i_, t in enumerate(tiles):
                    x_ = xg[i_ % 4]
                    act(junk[:, :], h[:, t, :], AF.Square, [hB[t]], [junk, sB[t]], accum=stat[:, t:t + 1])
                    act(stat[:, 16 + t:17 + t], stat[:, t:t + 1], AF.Ln, [sB[t]], [sB[t]], bias=EPS, scale=1.0 / D)
                    act(stat[:, 32 + t:33 + t], stat[:, 16 + t:17 + t], AF.Exp, [sB[t]], [sB[t]], scale=-0.5)
                    stt(x_[:, :], h[:, t, :], stat[:, 32 + t:33 + t], gb[:, :], ALU.mult, ALU.mult, [hB[t], sB[t], gb], [x_])
                    ps = PS[i_ % 8]
                    psb = ps[:, :].bitcast(BF16)
                    for c in range(8):
                        trp(psb[:, c * 128:(c + 1) * 128], x_[:, c * 128:(c + 1) * 128], idb[:, :], [x_, idb], [ps])
                    src = psb[:, :].rearrange("p (c n) -> p c n", n=128)
                    if i_ % 2 == 0:
                        cp(hnT[:, :, t * 128:(t + 1) * 128], src, [ps], [hnB[t]])
                    else:
                        act(hnT[:, :, t * 128:(t + 1) * 128], src, AF.Copy, [ps], [hnB[t]])

        def load_w(dst, w2d, col0, ncol, nk):
            src = w2d[:, col0:col0 + ncol].rearrange("(kc p) n -> p kc n", p=128)
            dma("pool", dst[:, 0:nk, 0:ncol], src, [], [dst])

        def proj_fm(wslab, wc0, evac, kchunks=8, src=None, banks=(0, 1, 2), M=128):
            src = hnT if src is None else src
            for tg in range(4):
                ps = PS[banks[tg % len(banks)]]
                for kc in range(kchunks):
                    mm(ps[0:M, :], wslab[:, kc, wc0:wc0 + M], src[:, kc, tg * 512:(tg + 1) * 512],
                       kc == 0, kc == kchunks - 1, [wslab, src], [ps])
                evac(tg, ps)

        rot = {"s": 0, "p": 0}
        pending = []

        def flush_pending():
            while pending:
                pending.pop(0)()

        def attn_qgroup(g, k_of, q_of, extra_of, Gt, v_of, ones_ap, numb, denb, pexp, ptb, exp_scale, kq_reads, v_reads,
                        mla=False, extra2_of=None):
            nj = 4 * g + 4
            sbanks = {}

            def issue_scores(j):
                c0 = max(0, j - 4 * g) * 128
                N = 512 - c0
                sb = PS[rot["s"] % 4]
                rot["s"] += 1
                sbanks[j] = sb
                last = (extra_of is None and extra2_of is None)
                ex = extra_of(j, g, c0, N) if extra_of is not None else None
                if extra_of is not None and ex is None:
                    mm(sb[:, 0:N], k_of(j), q_of(g, c0, N), True, True, kq_reads, [sb])
                    return
                mm(sb[:, 0:N], k_of(j), q_of(g, c0, N), True, last, kq_reads, [sb])
                if ex is not None:
                    l2, r2 = ex
                    mm(sb[:, 0:N], l2, r2, False, True, kq_reads, [sb])
                if extra2_of is not None:
                    l2, r2, nr = extra2_of(j, g, c0, N)
                    mm(sb[nr[0]:nr[1], 0:N], l2, r2, False, True, kq_reads, [sb])

            for j0 in range(min(3, nj)):
                issue_scores(j0)
            for j in range(nj):
                if j + 3 < nj:
                    issue_scores(j + 3)
                c0 = max(0, j - 4 * g) * 128
                N = 512 - c0
                sb = sbanks.pop(j)
                if mla:
                    pt = ptb[rot["p"] % len(ptb)]
                else:
                    allb = pexp + ptb
                    pt = allb[rot["p"] % len(allb)]
                if mla:
                    act(pt[:, 0:N], sb[:, 0:N], AF.Exp, [sb], [pt], scale=exp_scale)
                    if j >= 4 * g:
                        tt(pt[:, 0:128], pt[:, 0:128], tri_bf[:, :], ALU.mult, [pt, tri_bf], [pt])
                else:
                    act(pt[:, 0:N], sb[:, 0:N], AF.Exp, [sb], [pt], scale=exp_scale)
                    gc = (4 * g - j + 3) * 128 + c0
                    tt(pt[:, 0:N], pt[:, 0:N], Gt[:, gc:gc + N], ALU.mult, [pt, Gt], [pt])
                rot["p"] += 1
                mm(numb[:, c0:512], v_of(j), pt[:, 0:N], j == 0, j == nj - 1, [pt] + v_reads, [numb])
                if denb is not None:
                    mm(denb[:, c0:512], ones_ap, pt[:, 0:N], j == 0, j == nj - 1, [pt, ones_bf], [denb])
                if j == min(2, nj - 1):
                    flush_pending()

        def w_out_residual(stack, mixT, w2d):
            wo = [sbt(stack, "wo%d" % i, [128, 8, 512], BF16) for i in range(2)]
            for ch in range(2):
                load_w(wo[ch], w2d, ch * 512, 512, 8)
            k = 0
            for t in range(NT):
                for ch in range(2):
                    ps = PS[k % 8]
                    k += 1
                    for c in range(8):
                        mm(ps[:, :], mixT[:, c, t * 128:(t + 1) * 128], wo[ch][:, c, :], c == 0, c == 7, [mixT, wo[ch]], [ps])
                    tt(h[:, t, ch * 512:(ch + 1) * 512], ps[:, :], h[:, t, ch * 512:(ch + 1) * 512], ALU.add, [ps, hB[t]], [hB[t]])

        def even_mixer(li):
            i2 = li // 2
            w_in = ewin_d[i2]
            with ExitStack() as sm:
                mixT = sbt(sm, "mixT", [128, 8, S], BF16)
                norm_transpose(i2, list(range(NT)))
                ckpt("norm")
                with ExitStack() as sa:
                    QTz = [sbt(sa, "QTz%d" % i, [128, S], BF16) for i in range(2)]
                    KTz = [sbt(sa, "KTz%d" % i, [128, S], BF16) for i in range(2)]
                    VP = sbt(sa, "VP", [128, NT, 2, 128], BF16)
                    QTf = [sbt(sa, "QTf%d" % i, [128, 512], F32) for i in range(2)]
                    ksum = sbt(sa, "ksum", [128, 8], F32)
                    gm = sbt(sa, "gm", [128, 256], F32)
                    top = sbt(sa, "top", [128, 256], F32)
                    pen = sbt(sa, "pen", [128, 256], F32)
                    Gt = [sbt(sa, "Gt%d" % i, [128, GW], BF16) for i in range(2)]
                    wq = [sbt(sa, "wq%d" % i, [128, 8, 128], BF16) for i in range(2)]
                    wk = [sbt(sa, "wk%d" % i, [128, 8, 128], BF16) for i in range(2)]
                    wv = [sbt(sa, "wv%d" % i, [128, 8, 128], BF16) for i in range(2)]
                    pexp = [sbt(sa, "pexp%d" % i, [128, 512], BF16) for i in range(3)]
                    ptb = [sbt(sa, "ptb%d" % i, [128, 512], BF16) for i in range(3)]
                    rd = [sbt(sa, "rd%d" % i, [128, 512], F32) for i in range(2)]
                    o0 = sbt(sa, "o0", [128, 512], F32)
                    o1 = sbt(sa, "o1", [128, 512], F32)
                    sq = sbt(sa, "sq", [128, 512], F32)

                    memset(VP[:, :, :, :].rearrange("p a b c -> p (a b c)"), 1.0, [VP], eng="dve")
                    for i_ in range(2):
                        memset(QTz[i_][:, :], 0.0, [QTz[i_]], eng="dve")
                        memset(KTz[i_][:, :], 0.0, [KTz[i_]], eng="dve")
                    dma("pool", KTz[0][64:72, :], c_koh, [KTz[0]], [KTz[0]])
                    dma("pool", KTz[1][0:8, :], c_koh, [KTz[1]], [KTz[1]])
                    units = [("moba", fc) for fc in range(4)] + [("diff", hd) for hd in range(4)]

                    def unit_cols(u):
                        kind, idx = u
                        if kind == "moba":
                            return idx * 128, 512 + idx * 128, 1024 + idx * 128
                        return 1536 + idx * 128, 2048 + idx * 128, 2560 + idx * 128

                    def load_unit_w(ui):
                        qc, kc_, vc = unit_cols(units[ui])
                        load_w(wq[ui % 2], w_in, qc, 128, 8)
                        load_w(wk[ui % 2], w_in, kc_, 128, 8)
                        load_w(wv[ui % 2], w_in, vc, 128, 8)

                    load_unit_w(0)
                    gslot = 0
                    moba_rot = {"i": 0}
                    for ui, u in enumerate(units):
                        kind, idx = u
                        if ui + 1 < len(units):
                            load_unit_w(ui + 1)
                        Wq, Wk, Wv = wq[ui % 2], wk[ui % 2], wv[ui % 2]

                        if kind == "diff" and idx == 0:
                            memset(QTz[0][64:128, :], 0.0, [QTz[0]], eng="dve")
                            memset(QTz[1][0:64, :], 0.0, [QTz[1]], eng="dve")

                        def evac_k(tg, ps):
                            cp(KTz[0][0:64, tg * 512:(tg + 1) * 512], ps[0:64, :], [ps], [KTz[0]])
                            cp(KTz[1][64:128, tg * 512:(tg + 1) * 512], ps[64:128, :], [ps], [KTz[1]])
                            if kind == "moba":
                                P.op("dve", lambda e: e.reduce_sum(
                                    out=ksum[:, 2 * tg:2 * tg + 2],
                                    in_=ps[:, :].rearrange("p (b k) -> p b k", k=256), axis=AX.X), [ps], [ksum])
                        proj_fm(Wk, 0, evac_k)
                        ckpt("u%dk" % ui)
                        for tq in range(4):
                            ps = PS[3 + tq % 2]
                            for tt_ in range(4):
                                t = tq * 4 + tt_
                                for kc in range(8):
                                    mm(ps[:, tt_ * 128:(tt_ + 1) * 128], hnT[:, kc, t * 128:(t + 1) * 128], Wv[:, kc, :],
                                       kc == 0, kc == 7, [hnT, Wv], [ps])
                            psv = ps[:, :].rearrange("p (a b) -> p a b", b=128)
                            if kind == "moba":
                                act(VP[:, tq * 4:(tq + 1) * 4, 0, 0:64], psv[:, :, 0:64], AF.Copy, [ps], [VP])
                                act(VP[:, tq * 4:(tq + 1) * 4, 1, 64:128], psv[:, :, 64:128], AF.Copy, [ps], [VP])
                            else:
                                act(VP[:, tq * 4:(tq + 1) * 4, 0, :], psv, AF.Copy, [ps], [VP])

                        ckpt("u%dv" % ui)

                        def evac_q(tg, ps):
                            ts(QTz[0][0:64, tg * 512:(tg + 1) * 512], ps[0:64, :], 0.125, None, ALU.mult, None, [ps], [QTz[0]])
                            ts(QTz[1][64:128, tg * 512:(tg + 1) * 512], ps[64:128, :], 0.125, None, ALU.mult, None, [ps], [QTz[1]])
                            if kind == "moba":
                                qf = QTf[tg % 2]
                                ts(qf[:, :], ps[:, :], 0.125, None, ALU.mult, None, [ps], [qf])
                                for hh in range(2):
                                    for tt_ in range(4):
                                        t = tg * 4 + tt_
                                        col = (hh * 16 + t) * 8
                                        mm(PS[7][:, col:col + 8], qf[hh * 64:(hh + 1) * 64, tt_ * 128:(tt_ + 1) * 128],
                                           ksum[hh * 64:(hh + 1) * 64, 0:8], True, True, [qf, ksum], [PS[7]])
                        proj_fm(Wq, 0, evac_q)
                        ckpt("u%dq" % ui)

                        if kind == "moba":
                            tt(gm[:, :], PS[7][:, 0:256], gmask[:, :], ALU.add, [PS[7], gmask], [gm])
                            for q_ in range(32):
                                P.op("dve", lambda e, q_=q_: e.max(out=top[:, q_ * 8:(q_ + 1) * 8], in_=gm[:, q_ * 8:(q_ + 1) * 8]),
                                     [gm], [top])
                            thr = top[:, :].rearrange("p (a b) -> p a b", b=8)[:, :, 2:3].to_broadcast([128, 32, 8])
                            tt(pen[:, :].rearrange("p (a b) -> p a b", b=8), gm[:, :].rearrange("p (a b) -> p a b", b=8), thr,
                               ALU.is_ge, [gm, top], [pen])
                            ckpt("u%dsel" % ui)
                            ts(pen[:, :], pen[:, :], -1.0, -NEGBIG, ALU.add, ALU.mult, [pen], [pen])
                            tt(pen[:, :], pen[:, :], omask[:, :], ALU.mult, [pen, omask], [pen])
                            for hh in range(2):
                                for grp in range(4):
                                    ps = PS[5 + grp % 2]
                                    for tt_ in range(4):
                                        t = grp * 4 + tt_
                                        col = (hh * 16 + t) * 8
                                        trp(ps[0:8, tt_ * 128:(tt_ + 1) * 128], pen[:, col:col + 8], idf[:, :], [pen, idf], [ps])
                                    prow = 64 if hh == 0 else 0
                                    cp(QTz[hh][prow:prow + 8, grp * 512:(grp + 1) * 512], ps[0:8, :], [ps], [QTz[hh]])

                        ckpt("u%dproj" % ui)
                        if kind == "moba":
                            for hh in range(2):
                                head = idx * 2 + hh
                                G = Gt[gslot % 2]
                                gslot += 1
                                load_G(G, head)
                                lo, hi = hh * 64, (hh + 1) * 64
                                dlo, dhi = (1 - hh) * 64, (2 - hh) * 64
                                for g in range(4):
                                    numb = PS[4 + (moba_rot["i"] % 4)]
                                    moba_rot["i"] += 1
                                    attn_qgroup(
                                        g,
                                        lambda j: KTz[hh][:, j * 128:(j + 1) * 128],
                                        lambda g_, c0, N: QTz[hh][:, g_ * 512 + c0:g_ * 512 + c0 + N],
                                        None,
                                        G, lambda j: VP[:, j, hh, :], None, numb, None, pexp, ptb, 1.0,
                                        [KTz[hh], QTz[hh]], [VP])
                                    def epi(g=g, numb=numb, lo=lo, hi=hi, dlo=dlo, dhi=dhi, idx=idx):
                                        r_ = rd[g % 2]
                                        recip(r_[lo:hi, :], numb[dlo:dhi, :], [numb], [r_])
                                        tt(mixT[lo:hi, idx, g * 512:(g + 1) * 512], numb[lo:hi, :], r_[lo:hi, :], ALU.mult,
                                           [numb, r_], [mixT])
                                    pending.append(epi)
                        else:
                            head = 8 + idx
                            G = Gt[gslot % 2]
                            gslot += 1
                            load_G(G, head)
                            for g in range(4):
                                for c in range(2):
                                    lo, hi = c * 64, (c + 1) * 64
                                    numb, denb = (PS[4], PS[5]) if c == 0 else (PS[6], PS[7])
                                    attn_qgroup(
                                        g,
                                        lambda j: KTz[c][:, j * 128:(j + 1) * 128],
                                        lambda g_, c0, N: QTz[c][:, g_ * 512 + c0:g_ * 512 + c0 + N],
                                        None, G, lambda j: VP[:, j, 0, :], ones_bf[:, :], numb, denb, pexp, ptb, 1.0,
                                        [KTz[c], QTz[c]], [VP])
                                act(rd[0][:, :], PS[5][:, :], AF.Ln, [PS[5]], [rd[0]])
                                cp(o0[:, :], PS[4][:, :], [PS[4]], [o0])
                                act(rd[1][:, :], PS[7][:, :], AF.Ln, [PS[7]], [rd[1]])
                                cp(o1[:, :], PS[6][:, :], [PS[6]], [o1])

                                def epi(g=g, idx=idx):
                                    act(rd[0][:, :], rd[0][:, :], AF.Exp, [rd[0]], [rd[0]], scale=-1.0)
                                    act(rd[1][:, :], rd[1][:, :], AF.Exp, [rd[1]], [rd[1]], scale=-1.0)
                                    tt(o0[:, :], o0[:, :], rd[0][:, :], ALU.mult, [o0, rd[0]], [o0])
                                    tt(o1[:, :], o1[:, :], rd[1][:, :], ALU.mult, [o1, rd[1]], [o1])
                                    stt(o0[:, :], o1[:, :], small[:, 8 + i2:9 + i2], o0[:, :], ALU.mult, ALU.add, [o1, o0, small], [o0])
                                    act(sq[:, :], o0[:, :], AF.Square, [o0], [sq])
                                    ssb = PS[7]
                                    mm(ssb[:, :], ones_f[:, :], sq[:, :], True, True, [ones_f, sq], [ssb])
                                    act(sq[:, :], ssb[:, :], AF.Ln, [ssb], [sq], bias=EPS, scale=1.0 / 128)
                                    act(sq[:, :], sq[:, :], AF.Exp, [sq], [sq], scale=-0.5)
                                    stt(mixT[:, 4 + idx, g * 512:(g + 1) * 512], o0[:, :], small[:, 12 + i2:13 + i2], sq[:, :],
                                        ALU.mult, ALU.mult, [o0, small, sq], [mixT])
                                pending.append(epi)
                        flush_pending()
                        ckpt("u%d" % ui)
                P.barrier()
                with ExitStack() as sw:
                    w_out_residual(sw, mixT, ewout_d[i2])
                    norm_body(sw, 4 + li, list(range(NT)))
            P.barrier()

        def conv_ffn(li):
            w_in = fwin_d[li]
            w_out = fwout_d[li]
            with ExitStack() as sf:
                actT = sbt(sf, "actT", [128, NCC, 1024], BF16)
                wug = [sbt(sf, "wug%d" % i, [128, 8, 256], BF16) for i in range(3)]
                wo2 = [sbt(sf, "wo2_%d" % i, [128, NCC, 256], BF16) for i in range(2)]
                graw = [sbt(sf, "graw%d" % i, [128, 1040], F32) for i in range(2)]
                A = [sbt(sf, "A%d" % i, [128, 512], F32) for i in range(2)]
                Ag = [sbt(sf, "Ag%d" % i, [128, 512], F32) for i in range(2)]
                gtail = sbt(sf, "gtail", [128, NCC, 2], F32)

                def load_pair(slot, cc):
                    src_u = w_in[:, cc * 128:(cc + 1) * 128].rearrange("(kc p) n -> p kc n", p=128)
                    src_g = w_in[:, DFF + cc * 128:DFF + (cc + 1) * 128].rearrange("(kc p) n -> p kc n", p=128)
                    dma("pool", wug[slot][:, :, 0:128], src_u, [], [wug[slot]])
                    dma("pool", wug[slot][:, :, 128:256], src_g, [], [wug[slot]])

                k = 0
                for half in range(2):
                    tiles = list(range(half * 8, half * 8 + 8))
                    load_pair(0, 0)
                    load_pair(1, 1)
                    for cc in range(NCC):
                        if cc + 2 < NCC:
                            load_pair((cc + 2) % 3, cc + 2)
                        W = wug[cc % 3]
                        gr = graw[cc % 2]
                        if half == 0:
                            memset(gr[:, 14:16], 0.0, [gr], eng="dve")
                        else:
                            cp(gr[:, 14:16], gtail[:, cc, :], [gtail], [gr])
                        w0 = convw[:, (li * 3 + 0) * NCC + cc:(li * 3 + 0) * NCC + cc + 1]
                        w1 = convw[:, (li * 3 + 1) * NCC + cc:(li * 3 + 1) * NCC + cc + 1]
                        w2 = convw[:, (li * 3 + 2) * NCC + cc:(li * 3 + 2) * NCC + cc + 1]
                        bb = convw[:, (12 + li) * NCC + cc:(12 + li) * NCC + cc + 1]
                        for tgi in range(2):
                            tok0 = half * 1024 + tgi * 512
                            pu = PS[(k * 2) % 8]
                            pg = PS[(k * 2 + 1) % 8]
                            k += 1
                            for kc in range(8):
                                mm(pg[:, :], W[:, kc, 128:256], hnT[:, kc, tok0:tok0 + 512], kc == 0, kc == 7, [W, hnT], [pg])
                            for kc in range(8):
                                mm(pu[:, :], W[:, kc, 0:128], hnT[:, kc, tok0:tok0 + 512], kc == 0, kc == 7, [W, hnT], [pu])
                            a = A[tgi]
                            ag = Ag[tgi]
                            act(gr[:, 16 + tgi * 512:16 + (tgi + 1) * 512], pg[:, :], AF.Copy, [pg], [gr])
                            ts(a[:, :], gr[:, 16 + tgi * 512:16 + (tgi + 1) * 512], w2, bb, ALU.mult, ALU.add, [gr, convw], [a])
                            stt(a[:, :], gr[:, 15 + tgi * 512:15 + (tgi + 1) * 512], w1, a[:, :], ALU.mult, ALU.add, [gr, a, convw], [a], eng=FMA_ENG)
                            stt(a[:, :], gr[:, 14 + tgi * 512:14 + (tgi + 1) * 512], w0, a[:, :], ALU.mult, ALU.add, [gr, a, convw], [a], eng=FMA_ENG)
                            act(ag[:, :], a[:, :], AF.Gelu, [a], [ag])
                            tt(actT[:, cc, tgi * 512:(tgi + 1) * 512], ag[:, :], pu[:, :], ALU.mult, [ag, pu], [actT])
                        if half == 0:
                            cp(gtail[:, cc, :], gr[:, 1038:1040], [gr], [gtail])
                    load_cols = lambda slot, cq: dma(
                        "pool", wo2[slot][:, :, :],
                        w_out[:, cq * 256:(cq + 1) * 256].rearrange("(cc p) n -> p cc n", p=128), [], [wo2[slot]])
                    load_cols(0, 0)
                    for cq in range(4):
                        if cq + 1 < 4:
                            load_cols((cq + 1) % 2, cq + 1)
                        Wo = wo2[cq % 2]
                        for tl in range(8):
                            t = half * 8 + tl
                            ps = PS[k % 8]
                            k += 1
                            for cc in range(NCC):
                                mm(ps[:, 0:256], actT[:, cc, tl * 128:(tl + 1) * 128], Wo[:, cc, :], cc == 0, cc == NCC - 1,
                                   [actT, Wo], [ps])
                            tt(h[:, t, cq * 256:(cq + 1) * 256], ps[:, 0:256], h[:, t, cq * 256:(cq + 1) * 256], ALU.add,
                               [ps, hB[t]], [hB[t]])
            P.barrier()

        def odd_mixer(li):
            i2 = li // 2
            w_in = owin_d[i2]
            with ExitStack() as sm:
                mixT = sbt(sm, "mixTo", [128, 8, S], BF16)
                norm_transpose(2 + i2, list(range(NT)))
                with ExitStack() as sa:
                    QTz = [sbt(sa, "oQTz%d" % i, [128, S], BF16) for i in range(2)]
                    KTz = [sbt(sa, "oKTz%d" % i, [128, S], BF16) for i in range(2)]
                    for i_ in range(2):
                        memset(QTz[i_][:, :], 0.0, [QTz[i_]], eng="dve")
                        memset(KTz[i_][:, :], 0.0, [KTz[i_]], eng="dve")
                    VS = sbt(sa, "VS", [128, NT, 2, 128], BF16)
                    accn = sbt(sa, "accn", [128, S], F32)
                    accd = sbt(sa, "accd", [128, S], F32)
                    Gd = sbt(sa, "Gd", [128, 3, 2, 256], BF16)
                    wq = [sbt(sa, "owq%d" % i, [128, 8, 128], BF16) for i in range(2)]
                    wk = [sbt(sa, "owk%d" % i, [128, 8, 128], BF16) for i in range(2)]
                    wv = [sbt(sa, "owv%d" % i, [128, 8, 128], BF16) for i in range(2)]
                    pexp = [sbt(sa, "opexp%d" % i, [128, 512], BF16) for i in range(4)]
                    ptb = [sbt(sa, "optb%d" % i, [128, 512], BF16) for i in range(4)]
                    rd = [sbt(sa, "ord%d" % i, [128, 512], F32) for i in range(2)]

                    memset(VS[:, :, :, :].rearrange("p a b c -> p (a b c)"), 0.0, [VS], eng="dve")

                    def load_unit_w(ui):
                        load_w(wq[ui % 2], w_in, ui * 128, 128, 8)
                        load_w(wk[ui % 2], w_in, 512 + ui * 128, 128, 8)
                        load_w(wv[ui % 2], w_in, 1024 + ui * 128, 128, 8)

                    load_unit_w(0)
                    sr = 0
                    vr = {"i": 0}
                    for ui in range(4):
                        if ui + 1 < 4:
                            load_unit_w(ui + 1)
                        Wq, Wk, Wv = wq[ui % 2], wk[ui % 2], wv[ui % 2]
                        def evac_k(tg, ps):
                            cp(KTz[0][0:64, tg * 512:(tg + 1) * 512], ps[0:64, :], [ps], [KTz[0]])
                            cp(KTz[1][64:128, tg * 512:(tg + 1) * 512], ps[64:128, :], [ps], [KTz[1]])

                        def evac_q(tg, ps):
                            ts(QTz[0][0:64, tg * 512:(tg + 1) * 512], ps[0:64, :], 0.125, None, ALU.mult, None, [ps], [QTz[0]])
                            ts(QTz[1][64:128, tg * 512:(tg + 1) * 512], ps[64:128, :], 0.125, None, ALU.mult, None, [ps], [QTz[1]])
                        proj_fm(Wk, 0, evac_k)
                        proj_fm(Wq, 0, evac_q)
                        for hh in range(2):
                            for gi in range(3):
                                idx = (ui * 2 + hh) * 3 + gi
                                src = bass.AP(m_dil, idx * 128 * 512 + 127, [[511, 128], [1, 256]])
                                dma("sp", Gd[:, gi, hh, :], src, [B_mdil[idx]], [Gd])
                        first = True
                        for gi, dil in enumerate((1, 4, 16)):
                            L = S // dil
                            nblk = max(1, L // 128)
                            for r in range(dil):
                                for n0 in range(0, nblk, 4):
                                    nn = min(4, nblk - n0)
                                    ps = PS[(3, 0, 1, 2)[vr["i"] % 4]]
                                    vr["i"] += 1
                                    for n_ in range(nn):
                                        n = n0 + n_
                                        t0 = n * 128 * dil + r
                                        for kc in range(8):
                                            mm(ps[:, n_ * 128:(n_ + 1) * 128],
                                               hnT[:, kc, t0:t0 + 127 * dil + 1:dil], Wv[:, kc, :],
                                               kc == 0, kc == 7, [hnT, Wv], [ps])
                                    slot0 = r * nblk + n0
                                    psv = ps[:, 0:nn * 128].rearrange("p (a b) -> p a b", b=128)
                                    act(VS[:, slot0:slot0 + nn, 0, 0:64], psv[:, :, 0:64], AF.Copy, [ps], [VS])
                                    act(VS[:, slot0:slot0 + nn, 1, 64:128], psv[:, :, 64:128], AF.Copy, [ps], [VS])
                            ckpt("d%dv%d" % (ui, gi))
                            tasks = []
                            for r in range(dil):
                                for seg0 in range(0, nblk, 4):
                                    segn = min(4, nblk - seg0)
                                    W_ = segn * 128
                                    numb, denb = (PS[4], PS[5]) if sr % 2 == 0 else (PS[6], PS[7])
                                    sr += 1
                                    kbs = [kb for kb in range(seg0 - 1, seg0 + segn) if kb >= 0]
                                    for ki, kb in enumerate(kbs):
                                        q0 = max(kb, seg0)
                                        q1 = min(kb + 2, seg0 + segn)
                                        N = (q1 - q0) * 128
                                        xoff = (q0 - kb) * 128
                                        cpos = (q0 - seg0) * 128
                                        kt0 = kb * 128 * dil + r
                                        qt0 = q0 * 128 * dil + r
                                        for hh in range(2):
                                            st_ = {}

                                            def score_fn(N=N, kt0=kt0, qt0=qt0, st_=st_, hh=hh):
                                                sb = PS[rot["s"] % 4]
                                                rot["s"] += 1
                                                st_["sb"] = sb
                                                mm(sb[:, 0:N], KTz[hh][:, kt0:kt0 + 127 * dil + 1:dil],
                                                   QTz[hh][:, qt0:qt0 + (N - 1) * dil + 1:dil], True, True, [KTz[hh], QTz[hh]], [sb])

                                            def rest_fn(N=N, xoff=xoff, cpos=cpos, kb=kb, r=r, st_=st_, numb=numb, denb=denb, hh=hh,
                                                        first_k=(ki == 0 and hh == 0), last_k=(ki == len(kbs) - 1 and hh == 1),
                                                        W_=W_, seg0=seg0, first=first):
                                                sb = st_["sb"]
                                                pe_ = pexp[rot["p"] % 4]
                                                pt = ptb[rot["p"] % 4]
                                                rot["p"] += 1
                                                act(pe_[:, 0:N], sb[:, 0:N], AF.Exp, [sb], [pe_])
                                                tt(pt[:, 0:N], pe_[:, 0:N], Gd[:, gi, hh, xoff:xoff + N], ALU.mult, [pe_, Gd], [pt])
                                                mm(numb[:, cpos:cpos + N], VS[:, r * nblk + kb, hh, :], pt[:, 0:N], first_k, False,
                                                   [pt, VS], [numb])
                                                mm(denb[:, cpos:cpos + N], (onesA if hh == 0 else onesB)[:, :], pt[:, 0:N], first_k, False,
                                                   [pt, onesA, onesB], [denb])
                                                if last_k:
                                                    tq0 = seg0 * 128 * dil + r
                                                    dst_n = accn[:, tq0:tq0 + (W_ - 1) * dil + 1:dil]
                                                    dst_d = accd[:, tq0:tq0 + (W_ - 1) * dil + 1:dil]
                                                    if first:
                                                        cp(dst_n, numb[:, 0:W_], [numb], [accn])
                                                        cp(dst_d, denb[:, 0:W_], [denb], [accd])
                                                    else:
                                                        tt(dst_n, numb[:, 0:W_], dst_n, ALU.add, [numb, accn], [accn])
                                                        tt(dst_d, denb[:, 0:W_], dst_d, ALU.add, [denb, accd], [accd])
                                            tasks.append((score_fn, rest_fn))
                            run_pipeline(tasks, 3)
                            ckpt("d%db%d" % (ui, gi))
                            first = False
                        ckpt("d%dpre" % ui)
                        for g in range(4):
                            r_ = rd[g % 2]
                            recip(r_[:, :], accd[:, g * 512:(g + 1) * 512], [accd], [r_])
                            tt(mixT[:, ui, g * 512:(g + 1) * 512], accn[:, g * 512:(g + 1) * 512], r_[:, :], ALU.mult, [accn, r_], [mixT])
                P.barrier()
                with ExitStack() as sb_:
                    wuq = sbt(sb_, "wuq", [128, 2, 768], BF16)
                    wukv = sbt(sb_, "wukv", [128, 1, 1024], BF16)
                    cqT = sbt(sb_, "cqT", [128, 2, S], BF16)
                    ckvT = sbt(sb_, "ckvT", [128, 1, S], BF16)
                    krT = sbt(sb_, "krT", [128, S], BF16)
                    rcs = sbt(sb_, "rcs", [64, 2 * S], F32)
                    xs1 = [sbt(sb_, "xs1_%d" % i, [64, 512], F32) for i in range(2)]
                    xs2 = [sbt(sb_, "xs2_%d" % i, [64, 512], F32) for i in range(2)]
                    lst = sbt(sb_, "lst", [128, 96], F32)
                    rr = {"i": 0}

                    def rope(dst_ap, dst_T, ps, col0, tok0, n):
                        a = xs1[rr["i"] % 2]
                        b = xs2[rr["i"] % 2]
                        rr["i"] += 1
                        tt(b[0:32, 0:n], ps[32:64, col0:col0 + n], rcs[0:32, S + tok0:S + tok0 + n], ALU.mult, [ps, rcs], [b])
                        tt(b[32:64, 0:n], ps[0:32, col0:col0 + n], rcs[32:64, S + tok0:S + tok0 + n], ALU.mult, [ps, rcs], [b])
                        tt(a[0:64, 0:n], ps[0:64, col0:col0 + n], rcs[0:64, tok0:tok0 + n], ALU.mult, [ps, rcs], [a])
                        tt(dst_ap, a[0:64, 0:n], b[0:64, 0:n], ALU.add, [a, b], [dst_T])

                    memset(krT[64:128, :], 0.0, [krT], eng="dve")
                    load_w(wuq, wuq_d[i2], 0, 768, 2)
                    load_w(wukv, wukv_d[i2], 0, 1024, 1)
                    dma("sp", rcs[0:64, 0:S], c_ropec, [], [rcs])
                    dma("sp", rcs[0:64, S:2 * S], c_ropes, [], [rcs])
                    with ExitStack() as s1:
                        wl = sbt(s1, "wl", [128, 8, 448], BF16)
                        lat = [sbt(s1, "lat%d" % i, [128, 448], F32) for i in range(4)]
                        load_w(wl, w_in, 1536, 448, 8)
                        lB = [Buf("lst%d" % t) for t in range(NT)]
                        memset(lst[:, 0:32], 0.0, lB, eng="dve")
                        for t in range(NT):
                            ps = PS[t % 4]
                            la = lat[t % 4]
                            for kc in range(8):
                                mm(ps[:, 0:448], hnT[:, kc, t * 128:(t + 1) * 128], wl[:, kc, :], kc == 0, kc == 7, [hnT, wl], [ps])
                            act(la[:, 0:256], ps[:, 0:256], AF.Square, [ps], [la, lB[t]], accum=lst[:, t:t + 1])
                            act(la[:, 256:384], ps[:, 256:384], AF.Square, [ps], [la, lB[t]], accum=lst[:, 16 + t:17 + t])
                            act(lst[:, 32 + t:33 + t], lst[:, t:t + 1], AF.Ln, [lB[t]], [lB[t]], bias=EPS, scale=1.0 / 256)
                            act(lst[:, 32 + t:33 + t], lst[:, 32 + t:33 + t], AF.Exp, [lB[t]], [lB[t]], scale=-0.5)
                            act(lst[:, 48 + t:49 + t], lst[:, 16 + t:17 + t], AF.Ln, [lB[t]], [lB[t]], bias=EPS, scale=1.0 / 128)
                            act(lst[:, 48 + t:49 + t], lst[:, 48 + t:49 + t], AF.Exp, [lB[t]], [lB[t]], scale=-0.5)
                            act(la[:, 0:256], ps[:, 0:256], AF.Copy, [ps, lB[t]], [la], scale=lst[:, 32 + t:33 + t])
                            act(la[:, 256:384], ps[:, 256:384], AF.Copy, [ps, lB[t]], [la], scale=lst[:, 48 + t:49 + t])
                            act(la[:, 384:448], ps[:, 384:448], AF.Copy, [ps], [la])
                            pt_ = PS[4 + t % 4]
                            for c in range(3):
                                trp(pt_[:, c * 128:(c + 1) * 128], la[:, c * 128:(c + 1) * 128], idf[:, :], [la, idf], [pt_])
                            trp(pt_[0:64, 384:512], la[:, 384:448], idf[:, :], [la, idf], [pt_])
                            for c in range(2):
                                ts(cqT[:, c, t * 128:(t + 1) * 128], pt_[:, c * 128:(c + 1) * 128], small[:, 2 + 2 * i2 + c:3 + 2 * i2 + c],
                                   None, ALU.mult, None, [pt_, small], [cqT])
                            ts(ckvT[:, 0, t * 128:(t + 1) * 128], pt_[:, 256:384], small[:, 6 + i2:7 + i2], None, ALU.mult, None,
                               [pt_, small], [ckvT])
                            rope_dst = krT[0:64, t * 128:(t + 1) * 128]
                            rope2(rope_dst, krT, pt_, 384, t * 128, 128, rope)
                    P.barrier()
                    with ExitStack() as s2:
                        qnT = sbt(s2, "qnT", [128, S], BF16)
                        qrT = sbt(s2, "qrT", [128, S], BF16)
                        memset(qrT[64:128, :], 0.0, [qrT], eng="dve")
                        knT = sbt(s2, "knT", [128, S], BF16)
                        VM = sbt(s2, "VM", [128, NT, 128], BF16)
                        ptb = [sbt(s2, "mptb%d" % i, [128, 512], BF16) for i in range(4)]
                        rd = [sbt(s2, "mrd%d" % i, [128, 512], F32) for i in range(2)]
                        scale = 192.0 ** -0.5
                        for hd in range(4):
                            proj_fm(wuq, hd * 192, lambda tg, ps: cp(qnT[:, tg * 512:(tg + 1) * 512], ps[:, :], [ps], [qnT]),
                                    kchunks=2, src=cqT)
                            proj_fm(wuq, hd * 192 + 128,
                                    lambda tg, ps: rope2(qrT[0:64, tg * 512:(tg + 1) * 512], qrT, ps, 0, tg * 512, 512, rope),
                                    kchunks=2, src=cqT, M=64)
                            proj_fm(wukv, hd * 256, lambda tg, ps: cp(knT[:, tg * 512:(tg + 1) * 512], ps[:, :], [ps], [knT]),
                                    kchunks=1, src=ckvT)
                            for tq in range(4):
                                ps = PS[3 + tq % 2]
                                for tt_ in range(4):
                                    t = tq * 4 + tt_
                                    mm(ps[:, tt_ * 128:(tt_ + 1) * 128], ckvT[:, 0, t * 128:(t + 1) * 128],
                                       wukv[:, 0, hd * 256 + 128:hd * 256 + 256], True, True, [ckvT, wukv], [ps])
                                act(VM[:, tq * 4:(tq + 1) * 4, :], ps[:, :].rearrange("p (a b) -> p a b", b=128), AF.Copy, [ps], [VM])
                            for g in range(4):
                                numb, denb = (PS[4], PS[5]) if g % 2 == 0 else (PS[6], PS[7])
                                attn_qgroup(
                                    g,
                                    lambda j: knT[:, j * 128:(j + 1) * 128],
                                    lambda g_, c0, N: qnT[:, g_ * 512 + c0:g_ * 512 + c0 + N],
                                    lambda j, g_, c0, N: (krT[:, j * 128:(j + 1) * 128],
                                                          qrT[:, g_ * 512 + c0:g_ * 512 + c0 + N]),
                                    None, lambda j: VM[:, j, :], ones_bf[:, :], numb, denb, None, ptb, scale,
                                    [knT, qnT, krT, qrT], [VM], mla=True)
                                def epi(g=g, numb=numb, denb=denb, hd=hd):
                                    r_ = rd[g % 2]
                                    recip(r_[:, :], denb[:, :], [denb], [r_])
                                    tt(mixT[:, 4 + hd, g * 512:(g + 1) * 512], numb[:, :], r_[:, :], ALU.mult, [numb, r_], [mixT])
                                pending.append(epi)
                            flush_pending()
                P.barrier()
                with ExitStack() as sw:
                    w_out_residual(sw, mixT, owout_d[i2])
                    norm_body(sw, 4 + li, list(range(NT)))
            P.barrier()

        try:
            ckpt("setup")
            for li in range(n_layers):
                if li % 2 == 0:
                    even_mixer(li)
                else:
                    odd_mixer(li)
                ckpt("mixer%d" % li)
                conv_ffn(li)
        except _Stop:
            pass
        P.stopped = False
        P.barrier()

        with ExitStack() as sfin:
            gfin = sbt(sfin, "gfin", [128, D], F32)
            yb = [sbt(sfin, "yb%d" % i, [128, D], F32) for i in range(2)]
            dma("sp", gfin[:, :], fin_d.partition_broadcast(128), [], [gfin])
            for t in range(NT):
                y = yb[t % 2]
                if debug_h:
                    cp(y[:, :], h[:, t, :], [hB[t]], [y])
                else:
                    act(y[:, :], h[:, t, :], AF.Square, [hB[t]], [y, stat], accum=stat[:, t:t + 1])
                    act(stat[:, 16 + t:17 + t], stat[:, t:t + 1], AF.Ln, [stat], [stat], bias=EPS, scale=1.0 / D)
                    act(stat[:, 32 + t:33 + t], stat[:, 16 + t:17 + t], AF.Exp, [stat], [stat], scale=-0.5)
                    stt(y[:, :], h[:, t, :], stat[:, 32 + t:33 + t], gfin[:, :], ALU.mult, ALU.mult, [hB[t], stat, gfin], [y])
                dma("sp", out_d[t * 128:(t + 1) * 128, :], y[:, :], [y], [B_out])
            P.wait_all("sp", [B_out])
            P.barrier()

        with nc.Block() as block:
            P.emit(block)
    return nc


_CACHE = {}


def kernel(**inputs):
    n = 8
    consts = _host_consts()
    if "nc" not in _CACHE:
        _CACHE["nc"] = build_program()
    nc = _CACHE["nc"]
    x = np.ascontiguousarray(np.asarray(inputs["x"], dtype=np.float32))
    shared = {k: np.ascontiguousarray(np.asarray(v, dtype=np.float32)) for k, v in inputs.items() if k != "x"}
    shared.update(consts)
    in_maps = []
    for c in range(n):
        m = dict(shared)
        m["x"] = x[c]
        in_maps.append(m)
    res = run_bass_kernel_spmd(nc, in_maps, core_ids=list(range(n)))
    out = np.stack([np.asarray(r["out"], dtype=np.float32) for r in res.results], axis=0)
    return out
```

```python
import math
from contextlib import ExitStack

import numpy as np
import concourse.bass as bass
import concourse.mybir as mybir
from concourse.bass_utils import run_bass_kernel_spmd

F32 = mybir.dt.float32
BF16 = mybir.dt.bfloat16
ALU = mybir.AluOpType
AF = mybir.ActivationFunctionType
AX = mybir.AxisListType

S = 2048
D = 1024
NT = 16
DEPTH = 4
DFF = 2816
NCC = 22
EPS = 1e-6
LW = 2560
GW = 2432
NEGBIG = -30000.0


class Buf:
    __slots__ = ("name", "w", "rs")

    def __init__(self, name):
        self.name = name
        self.w = None
        self.rs = []


class T:
    __slots__ = ("t", "b")

    def __init__(self, t, name):
        self.t = t
        self.b = Buf(name)

    def __getitem__(self, k):
        return self.t[k]


class Prog:
    ENGS = ("pe", "act", "dve", "pool", "sp")

    def __init__(self, nc, n_dma_sems=8):
        self.nc = nc
        self.ops = {e: [] for e in self.ENGS}
        self.cnt = {e: 0 for e in self.ENGS}
        self.waited = {e: {} for e in self.ENGS}
        self.sems = {}
        self.n_dma_sems = n_dma_sems
        self.dma_i = {"sp": 0, "pool": 0, "act": 0}
        self.dma_last = {}
        self.stopped = False

    def alloc_sems(self, stack):
        for e in self.ENGS:
            self.sems[e] = stack.enter_context(self.nc.semaphore("s_" + e))
        for q in ("sp", "pool", "act"):
            for i in range(self.n_dma_sems):
                self.sems[("dma", q, i)] = stack.enter_context(self.nc.semaphore(f"d_{q}_{i}"))

    def _need(self, eng, waits, ev, kind):
        if ev is None:
            return
        semkey, val, src = ev
        if src == eng and semkey == eng:
            if eng == "pe" or kind == "war":
                return
        if self.waited[eng].get(semkey, 0) >= val:
            return
        if waits.get(semkey, 0) < val:
            waits[semkey] = val

    def _deps(self, eng, reads, writes):
        waits = {}
        for b in reads:
            self._need(eng, waits, b.w, "raw")
        for b in writes:
            self._need(eng, waits, b.w, "waw")
            for r in b.rs:
                self._need(eng, waits, r, "war")
        for k, v in waits.items():
            self.waited[eng][k] = v
        return list(waits.items())

    def op(self, eng, fn, reads=(), writes=()):
        if self.stopped:
            return None
        reads = [x.b if isinstance(x, T) else x for x in reads]
        writes = [x.b if isinstance(x, T) else x for x in writes]
        waits = self._deps(eng, reads, writes)
        self.cnt[eng] += 1
        ev = (eng, self.cnt[eng], eng)
        self.ops[eng].append((waits, fn, (eng, 1)))
        for b in reads:
            b.rs.append(ev)
            if len(b.rs) > 64:
                b.rs = self._prune(b.rs)
        for b in writes:
            b.w = ev
            b.rs = []
        return ev

    @staticmethod
    def _prune(rs):
        best = {}
        for (k, v, s) in rs:
            if k not in best or best[k][1] < v:
                best[k] = (k, v, s)
        return list(best.values())

    def dma(self, q, fn, reads=(), writes=()):
        if self.stopped:
            return None
        reads = [x.b if isinstance(x, T) else x for x in reads]
        writes = [x.b if isinstance(x, T) else x for x in writes]
        waits = self._deps(q, reads, writes)
        i = self.dma_i[q]
        self.dma_i[q] += 1
        slot = i % self.n_dma_sems
        semkey = ("dma", q, slot)
        val = 16 * (i // self.n_dma_sems + 1)
        if val > 16 and self.waited[q].get(semkey, 0) < val - 16:
            waits.append((semkey, val - 16))
            self.waited[q][semkey] = val - 16
        ev = (semkey, val, q + "_dma")
        self.dma_last[semkey] = val
        self.ops[q].append((waits, fn, (semkey, 16)))
        for b in reads:
            b.rs.append(ev)
        for b in writes:
            b.w = ev
            b.rs = []
        return ev

    def barrier(self):
        if self.stopped:
            return
        for e in self.ENGS:
            waits = []
            for o in self.ENGS:
                if o == e:
                    continue
                v = self.cnt[o]
                if v > 0 and self.waited[e].get(o, 0) < v:
                    waits.append((o, v))
                    self.waited[e][o] = v
            for k, v in self.dma_last.items():
                if self.waited[e].get(k, 0) < v:
                    waits.append((k, v))
                    self.waited[e][k] = v
            if waits:
                self.ops[e].append((waits, None, None))

    def wait_all(self, eng, bufs):
        bufs = [x.b if isinstance(x, T) else x for x in bufs]
        waits = self._deps(eng, bufs, ())
        self.ops[eng].append((waits, None, None))

    def emit(self, block):
        sems = self.sems

        def run(e, lst):
            for waits, fn, inc in lst:
                for k, v in waits:
                    e.wait_ge(sems[k], v)
                if fn is not None:
                    fn(e).then_inc(sems[inc[0]], inc[1])

        @block.tensor
        def _(e):
            run(e, self.ops["pe"])

        @block.scalar
        def _(e):
            run(e, self.ops["act"])

        @block.vector
        def _(e):
            run(e, self.ops["dve"])

        @block.gpsimd
        def _(e):
            run(e, self.ops["pool"])

        @block.sync
        def _(e):
            run(e, self.ops["sp"])


def _np_bucket(dist):
    n = np.maximum(dist, 0)
    nf = np.maximum(n, 1).astype(np.float32)
    large = 16 + (np.log(nf / np.float32(16)) / np.float32(math.log(64)) * np.float32(16)).astype(np.int32)
    large = np.minimum(large, 31)
    return np.where(n < 16, n, large)


def _host_consts():
    c = {}
    oh = np.zeros((33, LW), np.float32)
    d = np.arange(LW) - 511
    b = _np_bucket(d)
    for i in range(LW):
        if d[i] < 0:
            oh[32, i] = 1.0
        else:
            oh[b[i], i] = 1.0
    c["c_oh_main"] = oh
    ohd = np.zeros((33, 3 * 512), np.float32)
    for gi, dil in enumerate((1, 4, 16)):
        for i in range(512):
            s = i - 127
            if 0 <= s <= 128:
                ohd[_np_bucket(np.array([s * dil]))[0], gi * 512 + i] = 1.0
            else:
                ohd[32, gi * 512 + i] = 1.0
    c["c_oh_dil"] = ohd
    koh = np.zeros((8, S), np.float32)
    for n in range(8):
        koh[n, n * 256:(n + 1) * 256] = 1.0
    c["c_koh"] = koh
    gm = np.zeros((2, 16, 8), np.float32)
    om = np.ones((2, 16, 8), np.float32)
    for t in range(16):
        bq = t // 2
        gm[:, t, bq:] = -1e30
        om[:, t, bq] = 0.0
    c["c_gmask"] = np.broadcast_to(gm.reshape(1, 256), (128, 256)).copy()
    c["c_omask"] = np.broadcast_to(om.reshape(1, 256), (128, 256)).copy()
    half = 32
    freq = (np.float32(10000.0) ** (-np.arange(half, dtype=np.float32) / np.float32(half))).astype(np.float32)
    ang = np.arange(S, dtype=np.float32)[None, :] * freq[:, None]
    cs, sn = np.cos(ang).astype(np.float32), np.sin(ang).astype(np.float32)
    c["c_ropec"] = np.concatenate([cs, cs], 0)
    c["c_ropes"] = np.concatenate([-sn, sn], 0)
    tri = (np.arange(128)[None, :] >= np.arange(128)[:, None]).astype(np.float32)
    c["c_tri"] = tri
    return c


import os as _os
DIL_SPLIT = bool(_os.environ.get("DIL_SPLIT"))
FMA_ENG = "dve"


class _Stop(Exception):
    pass


def build_program(n_layers=DEPTH, debug_h=False, stop_at=None):
    nc = bass.Bass("TRN2", target_bir_lowering=False)

    stop_state = {"P": None}

    def ckpt(name):
        if stop_at == name:
            stop_state["P"].stopped = True

    dram_in = {}

    def din(name, shape):
        dram_in[name] = nc.dram_tensor(name, list(shape), F32, kind="ExternalInput").ap()
        return dram_in[name]

    x_d = din("x", [S, D])
    rel_d = din("rel_bias", [32, 20])
    en1_d = din("even_norm1", [2, D])
    ewin_d = din("even_w_in", [2, D, 3072])
    dlam_d = din("diff_lambda", [2, 4, 64])
    dsub_d = din("diff_subln", [2, 128])
    ewout_d = din("even_w_out", [2, D, D])
    on1_d = din("odd_norm1", [2, D])
    owin_d = din("odd_w_in", [2, D, 1984])
    qn_d = din("mla_q_norm", [2, 256])
    wuq_d = din("mla_w_uq", [2, 256, 768])
    kvn_d = din("mla_kv_norm", [2, 128])
    wukv_d = din("mla_w_ukv", [2, 128, 1024])
    owout_d = din("odd_w_out", [2, D, D])
    fn_d = din("ffn_norm", [4, D])
    fwin_d = din("ffn_w_in", [4, D, 2 * DFF])
    fcw_d = din("ffn_conv_w", [4, 3, DFF])
    fcb_d = din("ffn_conv_b", [4, DFF])
    fwout_d = din("ffn_w_out", [4, DFF, D])
    fin_d = din("final_norm", [D])
    c_oh_main = din("c_oh_main", [33, LW])
    c_oh_dil = din("c_oh_dil", [33, 1536])
    c_koh = din("c_koh", [8, S])
    c_gmask = din("c_gmask", [128, 256])
    c_omask = din("c_omask", [128, 256])
    c_ropec = din("c_ropec", [64, S])
    c_ropes = din("c_ropes", [64, S])
    c_tri = din("c_tri", [128, 128])
    out_d = nc.dram_tensor("out", [S, D], F32, kind="ExternalOutput").ap()
    m_main = nc.dram_tensor("m_main", [12 * 128, LW], BF16)
    m_dil = nc.dram_tensor("m_dil", [24 * 128, 512], BF16)
    B_mmain = [Buf("mmain%d" % i) for i in range(12)]
    B_mdil = [Buf("mdil%d" % i) for i in range(24)]
    B_out = Buf("out")

    with ExitStack() as st:
        P = Prog(nc)
        stop_state["P"] = P
        P.alloc_sems(st)
        st.enter_context(nc.allow_non_contiguous_dma("small strided parameter loads"))

        uniq = {"i": 0}

        def sbt(stack, name, shape, dt):
            uniq["i"] += 1
            name = "%s_%d" % (name, uniq["i"])
            return T(stack.enter_context(nc.sbuf_tensor(name, list(shape), dt)), name)

        def mm(out, lhsT, rhs, start, stop, r, w):
            P.op("pe", lambda e: e.matmul(out, lhsT=lhsT, rhs=rhs, start=start, stop=stop,
                                          skip_group_check=True), r, w)

        def trp(out, in_, ident, r, w):
            P.op("pe", lambda e: e.transpose(out=out, in_=in_, identity=ident), r, w)

        def act(out, in_, func, r, w, bias=None, scale=None, accum=None):
            kw = {}
            if bias is not None:
                kw["bias"] = bias
            if scale is not None:
                kw["scale"] = scale
            if accum is not None:
                kw["accum_out"] = accum
            P.op("act", lambda e: e.activation(out=out, in_=in_, func=func, **kw), r, w)

        def tt(out, in0, in1, op, r, w, eng="dve"):
            P.op(eng, lambda e: e.tensor_tensor(out=out, in0=in0, in1=in1, op=op), r, w)

        def ts(out, in0, s1, s2, op0, op1, r, w, eng="dve"):
            if s2 is None:
                P.op(eng, lambda e: e.tensor_scalar(out=out, in0=in0, scalar1=s1, scalar2=None, op0=op0), r, w)
            else:
                P.op(eng, lambda e: e.tensor_scalar(out=out, in0=in0, scalar1=s1, scalar2=s2, op0=op0, op1=op1), r, w)

        def stt(out, in0, scalar, in1, op0, op1, r, w, eng="dve"):
            P.op(eng, lambda e: e.scalar_tensor_tensor(out=out, in0=in0, scalar=scalar, in1=in1, op0=op0, op1=op1), r, w)

        def cp(out, in_, r, w, eng="dve"):
            P.op(eng, lambda e: e.tensor_copy(out=out, in_=in_), r, w)

        def recip(out, in_, r, w):
            act(out, in_, AF.Ln, r, w)
            act(out, out, AF.Exp, list(w), w, scale=-1.0)

        def memset(ap, val, w, eng="pool"):
            P.op(eng, lambda e: e.memset(ap, val), (), w)

        def dma(q, out, in_, r, w):
            P.dma(q, lambda e: e.dma_start(out=out, in_=in_), r, w)

        def run_pipeline(tasks, depth):
            n = len(tasks)
            for i in range(min(depth, n)):
                tasks[i][0]()
            for i in range(n):
                if i + depth < n:
                    tasks[i + depth][0]()
                tasks[i][1]()

        def rope2(dst_ap, dst_T, ps, col0, tok0, n, rope_fn):
            rope_fn(dst_ap, dst_T, ps, col0, tok0, n)

        h = sbt(st, "h", [128, NT, D], F32)
        hB = [Buf("h%d" % t) for t in range(NT)]
        hnT = sbt(st, "hnT", [128, 8, S], BF16)
        idf = sbt(st, "idf", [128, 128], F32)
        idb = sbt(st, "idb", [128, 128], BF16)
        ones_bf = sbt(st, "ones_bf", [128, 128], BF16)
        onesA = sbt(st, "onesA", [128, 128], BF16)
        onesB = sbt(st, "onesB", [128, 128], BF16)
        ones_f = sbt(st, "ones_f", [128, 128], F32)
        tri_bf = sbt(st, "tri_bf", [128, 128], BF16)
        gmask = sbt(st, "gmask", [128, 256], F32)
        omask = sbt(st, "omask", [128, 256], F32)
        convw = sbt(st, "convw", [128, 16 * NCC], F32)
        small = sbt(st, "small", [128, 64], F32)
        lamb = sbt(st, "lamb", [128, 2 * 256], F32)
        stat = sbt(st, "stat", [128, 64], F32)
        PS = [T(st.enter_context(nc.psum_tensor("ps%d" % i, [128, 512], F32)), "ps%d" % i) for i in range(8)]

        memset(idf[:], 0.0, [idf])
        P.op("pool", lambda e: e.affine_select(out=idf[:], in_=idf[:], pattern=[[-1, 128]],
                                               compare_op=ALU.not_equal, fill=1.0, base=0,
                                               channel_multiplier=1), [idf], [idf])
        cp(idb[:, :], idf[:, :], [idf], [idb])
        memset(ones_bf[:], 1.0, [ones_bf])
        memset(ones_f[:], 1.0, [ones_f])
        memset(onesA[:], 0.0, [onesA])
        memset(onesA[:, 0:64], 1.0, [onesA])
        memset(onesB[:], 0.0, [onesB])
        memset(onesB[:, 64:128], 1.0, [onesB])
        dma("pool", tri_bf[:], c_tri, [], [tri_bf])
        dma("sp", gmask[:], c_gmask, [], [gmask])
        dma("sp", omask[:], c_omask, [], [omask])
        for t in range(NT):
            dma("act", h[:, t, :], x_d[t * 128:(t + 1) * 128, :], [], [hB[t]])
        with ExitStack() as sc:
            raws = [sbt(sc, "craw%d" % i, [128, 128], F32) for i in range(4)]
            for i_ in range(4):
                memset(raws[i_][:, :], 0.0, [raws[i_]], eng="dve")
            vecs = [fcw_d[li, k] for li in range(4) for k in range(3)] + [fcb_d[li] for li in range(4)]
            for v, vec in enumerate(vecs):
                tl, j = v // 5, v % 5
                dma("sp", raws[tl][j * NCC:(j + 1) * NCC, :], vec.rearrange("(c p) -> c p", p=128), [raws[tl]], [raws[tl]])
            for tl in range(4):
                nv = min(5, 16 - tl * 5)
                ps = PS[tl]
                trp(ps[:, 0:128], raws[tl][:, :], idf[:, :], [raws[tl], idf], [ps])
                cp(convw[:, tl * 5 * NCC:(tl * 5 + nv) * NCC], ps[:, 0:nv * NCC], [ps], [convw])
            P.barrier()
        for i in range(2):
            dma("sp", small[:, i:i + 1], dsub_d[i].rearrange("(c p) -> p c", p=128), [], [small])
            dma("sp", small[:, 2 + 2 * i:4 + 2 * i], qn_d[i].rearrange("(c p) -> p c", p=128), [], [small])
            dma("sp", small[:, 6 + i:7 + i], kvn_d[i].rearrange("(c p) -> p c", p=128), [], [small])
            dma("sp", lamb[:, i * 256:(i + 1) * 256],
                dlam_d[i].rearrange("a b -> (a b)").partition_broadcast(128), [], [lamb])
        for i in range(2):
            layer = 2 * i
            lam_init = 0.8 - 0.6 * math.exp(-0.3 * layer)
            lp = lamb[:, i * 256:(i + 1) * 256]
            tt(lamb[:, i * 256:i * 256 + 64], lamb[:, i * 256:i * 256 + 64], lamb[:, i * 256 + 64:i * 256 + 128], ALU.mult, [lamb], [lamb])
            tt(lamb[:, i * 256 + 128:i * 256 + 192], lamb[:, i * 256 + 128:i * 256 + 192], lamb[:, i * 256 + 192:i * 256 + 256], ALU.mult, [lamb], [lamb])
            P.op("dve", lambda e, i=i: e.reduce_sum(out=small[:, 16 + 2 * i:17 + 2 * i], in_=lamb[:, i * 256:i * 256 + 64], axis=AX.X), [lamb], [small])
            P.op("dve", lambda e, i=i: e.reduce_sum(out=small[:, 17 + 2 * i:18 + 2 * i], in_=lamb[:, i * 256 + 128:i * 256 + 192], axis=AX.X), [lamb], [small])
            act(small[:, 20 + 2 * i:22 + 2 * i], small[:, 16 + 2 * i:18 + 2 * i], AF.Exp, [small], [small])
            tt(small[:, 8 + i:9 + i], small[:, 21 + 2 * i:22 + 2 * i], small[:, 20 + 2 * i:21 + 2 * i], ALU.subtract, [small], [small])
            ts(small[:, 8 + i:9 + i], small[:, 8 + i:9 + i], -lam_init, None, ALU.add, None, [small], [small])
            ts(small[:, 12 + i:13 + i], small[:, i:i + 1], 1.0 - lam_init, None, ALU.mult, None, [small], [small])

        f_main = nc.dram_tensor("f_main", [20, LW], BF16)
        f_dil = nc.dram_tensor("f_dil", [20, 1536], BF16)
        B_fmain = Buf("fmain")
        B_fdil = Buf("fdil")
        with ExitStack() as s0:
            tab = sbt(s0, "tab", [33, 20], F32)
            ohm = sbt(s0, "ohm", [33, LW], F32)
            ohd = sbt(s0, "ohd", [33, 1536], F32)
            fsb = sbt(s0, "fsb", [20, LW], BF16)
            fsd = sbt(s0, "fsd", [20, 1536], BF16)
            memset(tab[:], NEGBIG, [tab])
            dma("sp", tab[0:32, :], rel_d, [tab], [tab])
            dma("sp", ohm[:], c_oh_main, [], [ohm])
            dma("sp", ohd[:], c_oh_dil, [], [ohd])
            k = 0
            for c in range(LW // 512):
                ps = PS[k % 8]
                k += 1
                mm(ps[0:20, :], tab[0:33, 0:20], ohm[0:33, c * 512:(c + 1) * 512], True, True, [tab, ohm], [ps])
                act(fsb[0:20, c * 512:(c + 1) * 512], ps[0:20, :], AF.Exp, [ps], [fsb])
            for gi in range(3):
                ps = PS[k % 8]
                k += 1
                mm(ps[0:20, :], tab[0:33, 0:20], ohd[0:33, gi * 512:(gi + 1) * 512], True, True, [tab, ohd], [ps])
                act(fsd[0:20, gi * 512:(gi + 1) * 512], ps[0:20, :], AF.Exp, [ps], [fsd])
            dma("sp", f_main.ap()[:, :], fsb[0:20, :], [fsb], [B_fmain])
            dma("sp", f_dil.ap()[:, :], fsd[0:20, :], [fsd], [B_fdil])
            P.barrier()
        for hh in range(12):
            src = bass.AP(f_main, hh * LW, [[0, 128], [1, LW]])
            dma("sp", m_main.ap()[hh * 128:(hh + 1) * 128, :], src, [B_fmain], [B_mmain[hh]])
        for hh in range(8):
            for gi in range(3):
                idx = hh * 3 + gi
                src = bass.AP(f_dil, (12 + hh) * 1536 + gi * 512, [[0, 128], [1, 512]])
                dma("sp", m_dil.ap()[idx * 128:(idx + 1) * 128, :], src, [B_fdil], [B_mdil[idx]])

        def load_G(dst, head):
            src = bass.AP(m_main, head * 128 * LW + 127, [[LW - 1, 128], [1, GW]])
            dma("sp", dst[:, 0:GW], src, [B_mmain[head]], [dst])

        def load_Gdil(dst, idx):
            src = bass.AP(m_dil, idx * 128 * 512 + 127, [[511, 128], [1, 256]])
            dma("sp", dst, src, [B_mdil[idx]], [])

        gain_src = [en1_d[0], en1_d[1], on1_d[0], on1_d[1], fn_d[0], fn_d[1], fn_d[2], fn_d[3]]

        def norm_transpose(gidx, tiles):
            with ExitStack() as sn:
                norm_body(sn, gidx, tiles)
            P.barrier()

        def norm_body(sn, gidx, tiles):
            if True:
                gb = sbt(sn, "gb", [128, D], F32)
                junk = sbt(sn, "junk", [128, D], BF16)
                xg = [sbt(sn, "xg%d" % i, [128, D], BF16) for i in range(4)]
                dma("sp", gb[:, :], gain_src[gidx].partition_broadcast(128), [], [gb])
                sB = [Buf("st%d" % t) for t in range(NT)]
                hnB = [Buf("hn%d" % t) for t in range(NT)]
                memset(stat[:, 0:16], 0.0, sB, eng="dve")
                for i_, t in enumerate(tiles):
                    x_ = xg[i_ % 4]
                    act(junk[:, :], h[:, t, :], AF.Square, [hB[t]], [junk, sB[t]], accum=stat[:, t:t + 1])
                    act(stat[:, 16 + t:17 + t], stat[:, t:t + 1], AF.Ln, [sB[t]], [sB[t]], bias=EPS, scale=1.0 / D)
                    act(stat[:, 32 + t:33 + t], stat[:, 16 + t:17 + t], AF.Exp, [sB[t]], [sB[t]], scale=-0.5)
                    stt(x_[:, :], h[:, t, :], stat[:, 32 + t:33 + t], gb[:, :], ALU.mult, ALU.mult, [hB[t], sB[t], gb], [x_])
                    ps = PS[i_ % 8]
                    psb = ps[:, :].bitcast(BF16)
                    for c in range(8):
                        trp(psb[:, c * 128:(c + 1) * 128], x_[:, c * 128:(c + 1) * 128], idb[:, :], [x_, idb], [ps])
                    src = psb[:, :].rearrange("p (c n) -> p c n", n=128)
                    if i_ % 2 == 0:
                        cp(hnT[:, :, t * 128:(t + 1) * 128], src, [ps], [hnB[t]])
                    else:
                        act(hnT[:, :, t * 128:(t + 1) * 128], src, AF.Copy, [ps], [hnB[t]])

        def load_w(dst, w2d, col0, ncol, nk):
            src = w2d[:, col0:col0 + ncol].rearrange("(kc p) n -> p kc n", p=128)
            dma("pool", dst[:, 0:nk, 0:ncol], src, [], [dst])

        def proj_fm(wslab, wc0, evac, kchunks=8, src=None, banks=(0, 1, 2), M=128):
            src = hnT if src is None else src
            for tg in range(4):
                ps = PS[banks[tg % len(banks)]]
                for kc in range(kchunks):
                    mm(ps[0:M, :], wslab[:, kc, wc0:wc0 + M], src[:, kc, tg * 512:(tg + 1) * 512],
                       kc == 0, kc == kchunks - 1, [wslab, src], [ps])
                evac(tg, ps)

        rot = {"s": 0, "p": 0}
        pending = []

        def flush_pending():
            while pending:
                pending.pop(0)()

        def attn_tasks(g, k_of, q_of, extra_of, Gt, v_of, ones_ap, numb, denb, pexp, ptb, exp_scale, kq_reads, v_reads,
                       mla=False, on_done=None):
            nj = 4 * g + 4
            tasks = []
            for j in range(nj):
                c0 = max(0, j - 4 * g) * 128
                N = 512 - c0
                kj = k_of(j)
                qj = q_of(g, c0, N)
                ex = extra_of(j, g, c0, N) if extra_of is not None else None
                vj = v_of(j)
                gc = (4 * g - j + 3) * 128 + c0
                st_ = {}

                def score_fn(N=N, kj=kj, qj=qj, ex=ex, st_=st_):
                    sb = PS[rot["s"] % 4]
                    rot["s"] += 1
                    st_["sb"] = sb
                    mm(sb[:, 0:N], kj, qj, True, ex is None, kq_reads, [sb])
                    if ex is not None:
                        mm(sb[:, 0:N], ex[0], ex[1], False, True, kq_reads, [sb])

                def rest_fn(j=j, c0=c0, N=N, vj=vj, gc=gc, st_=st_):
                    sb = st_["sb"]
                    if mla:
                        pt = ptb[rot["p"] % len(ptb)]
                    else:
                        allb = pexp + ptb
                        pt = allb[rot["p"] % len(allb)]
                    rot["p"] += 1
                    act(pt[:, 0:N], sb[:, 0:N], AF.Exp, [sb], [pt], scale=exp_scale)
                    if mla:
                        if j >= 4 * g:
                            tt(pt[:, 0:128], pt[:, 0:128], tri_bf[:, :], ALU.mult, [pt, tri_bf], [pt])
                    else:
                        tt(pt[:, 0:N], pt[:, 0:N], Gt[:, gc:gc + N], ALU.mult, [pt, Gt], [pt])
                    mm(numb[:, c0:512], vj, pt[:, 0:N], j == 0, j == nj - 1, [pt] + v_reads, [numb])
                    if denb is not None:
                        mm(denb[:, c0:512], ones_ap, pt[:, 0:N], j == 0, j == nj - 1, [pt, ones_bf], [denb])
                    if j == min(2, nj - 1):
                        flush_pending()
                    if j == nj - 1 and on_done is not None:
                        on_done()
                tasks.append((score_fn, rest_fn))
            return tasks

        def w_out_residual(stack, mixT, w2d):
            wo = [sbt(stack, "wo%d" % i, [128, 8, 512], BF16) for i in range(2)]
            for ch in range(2):
                load_w(wo[ch], w2d, ch * 512, 512, 8)
            k = 0
            for t in range(NT):
                for ch in range(2):
                    ps = PS[k % 8]
                    k += 1
                    for c in range(8):
                        mm(ps[:, :], mixT[:, c, t * 128:(t + 1) * 128], wo[ch][:, c, :], c == 0, c == 7, [mixT, wo[ch]], [ps])
                    tt(h[:, t, ch * 512:(ch + 1) * 512], ps[:, :], h[:, t, ch * 512:(ch + 1) * 512], ALU.add, [ps, hB[t]], [hB[t]])

        def even_mixer(li):
            i2 = li // 2
            w_in = ewin_d[i2]
            with ExitStack() as sm:
                mixT = sbt(sm, "mixT", [128, 8, S], BF16)
                norm_transpose(i2, list(range(NT)))
                ckpt("norm")
                with ExitStack() as sa:
                    QTz = [sbt(sa, "QTz%d" % i, [128, S], BF16) for i in range(2)]
                    KTz = [sbt(sa, "KTz%d" % i, [128, S], BF16) for i in range(2)]
                    VP = sbt(sa, "VP", [128, NT, 2, 128], BF16)
                    QTf = [sbt(sa, "QTf%d" % i, [128, 512], F32) for i in range(2)]
                    ksum = sbt(sa, "ksum", [128, 8], F32)
                    gm = sbt(sa, "gm", [128, 256], F32)
                    top = sbt(sa, "top", [128, 256], F32)
                    pen = sbt(sa, "pen", [128, 256], F32)
                    Gt = [sbt(sa, "Gt%d" % i, [128, GW], BF16) for i in range(2)]
                    wq = [sbt(sa, "wq%d" % i, [128, 8, 128], BF16) for i in range(2)]
                    wk = [sbt(sa, "wk%d" % i, [128, 8, 128], BF16) for i in range(2)]
                    wv = [sbt(sa, "wv%d" % i, [128, 8, 128], BF16) for i in range(2)]
                    pexp = [sbt(sa, "pexp%d" % i, [128, 512], BF16) for i in range(3)]
                    ptb = [sbt(sa, "ptb%d" % i, [128, 512], BF16) for i in range(3)]
                    rd = [sbt(sa, "rd%d" % i, [128, 512], F32) for i in range(2)]
                    o0 = sbt(sa, "o0", [128, 512], F32)
                    o1 = sbt(sa, "o1", [128, 512], F32)
                    sq = sbt(sa, "sq", [128, 512], F32)

                    memset(VP[:, :, :, :].rearrange("p a b c -> p (a b c)"), 1.0, [VP], eng="dve")
                    for i_ in range(2):
                        memset(QTz[i_][:, :], 0.0, [QTz[i_]], eng="dve")
                        memset(KTz[i_][:, :], 0.0, [KTz[i_]], eng="dve")
                    dma("pool", KTz[0][64:72, :], c_koh, [KTz[0]], [KTz[0]])
                    dma("pool", KTz[1][0:8, :], c_koh, [KTz[1]], [KTz[1]])
                    units = [("moba", fc) for fc in range(4)] + [("diff", hd) for hd in range(4)]

                    def unit_cols(u):
                        kind, idx = u
                        if kind == "moba":
                            return idx * 128, 512 + idx * 128, 1024 + idx * 128
                        return 1536 + idx * 128, 2048 + idx * 128, 2560 + idx * 128

                    def load_unit_w(ui):
                        qc, kc_, vc = unit_cols(units[ui])
                        load_w(wq[ui % 2], w_in, qc, 128, 8)
                        load_w(wk[ui % 2], w_in, kc_, 128, 8)
                        load_w(wv[ui % 2], w_in, vc, 128, 8)

                    load_unit_w(0)
                    gslot = 0
                    moba_rot = {"i": 0}
                    for ui, u in enumerate(units):
                        kind, idx = u
                        if ui + 1 < len(units):
                            load_unit_w(ui + 1)
                        Wq, Wk, Wv = wq[ui % 2], wk[ui % 2], wv[ui % 2]

                        if kind == "diff" and idx == 0:
                            memset(QTz[0][64:128, :], 0.0, [QTz[0]], eng="dve")
                            memset(QTz[1][0:64, :], 0.0, [QTz[1]], eng="dve")

                        def evac_k(tg, ps):
                            cp(KTz[0][0:64, tg * 512:(tg + 1) * 512], ps[0:64, :], [ps], [KTz[0]])
                            cp(KTz[1][64:128, tg * 512:(tg + 1) * 512], ps[64:128, :], [ps], [KTz[1]])
                            if kind == "moba":
                                P.op("dve", lambda e: e.reduce_sum(
                                    out=ksum[:, 2 * tg:2 * tg + 2],
                                    in_=ps[:, :].rearrange("p (b k) -> p b k", k=256), axis=AX.X), [ps], [ksum])
                        proj_fm(Wk, 0, evac_k)
                        ckpt("u%dk" % ui)
                        ckpt("u%dv" % ui)

                        def evac_q(tg, ps):
                            ts(QTz[0][0:64, tg * 512:(tg + 1) * 512], ps[0:64, :], 0.125, None, ALU.mult, None, [ps], [QTz[0]])
                            ts(QTz[1][64:128, tg * 512:(tg + 1) * 512], ps[64:128, :], 0.125, None, ALU.mult, None, [ps], [QTz[1]])
                            if kind == "moba":
                                qf = QTf[tg % 2]
                                ts(qf[:, :], ps[:, :], 0.125, None, ALU.mult, None, [ps], [qf])
                                for hh in range(2):
                                    for tt_ in range(4):
                                        t = tg * 4 + tt_
                                        col = (hh * 16 + t) * 8
                                        mm(PS[7][:, col:col + 8], qf[hh * 64:(hh + 1) * 64, tt_ * 128:(tt_ + 1) * 128],
                                           ksum[hh * 64:(hh + 1) * 64, 0:8], True, True, [qf, ksum], [PS[7]])
                        proj_fm(Wq, 0, evac_q)
                        ckpt("u%dq" % ui)
                        for tq in range(4):
                            ps = PS[3 + tq % 2]
                            for tt_ in range(4):
                                t = tq * 4 + tt_
                                for kc in range(8):
                                    mm(ps[:, tt_ * 128:(tt_ + 1) * 128], hnT[:, kc, t * 128:(t + 1) * 128], Wv[:, kc, :],
                                       kc == 0, kc == 7, [hnT, Wv], [ps])
                            psv = ps[:, :].rearrange("p (a b) -> p a b", b=128)
                            if kind == "moba":
                                act(VP[:, tq * 4:(tq + 1) * 4, 0, 0:64], psv[:, :, 0:64], AF.Copy, [ps], [VP])
                                act(VP[:, tq * 4:(tq + 1) * 4, 1, 64:128], psv[:, :, 64:128], AF.Copy, [ps], [VP])
                            else:
                                act(VP[:, tq * 4:(tq + 1) * 4, 0, :], psv, AF.Copy, [ps], [VP])


                        if kind == "moba":
                            tt(gm[:, :], PS[7][:, 0:256], gmask[:, :], ALU.add, [PS[7], gmask], [gm])
                            for q_ in range(32):
                                P.op("dve", lambda e, q_=q_: e.max(out=top[:, q_ * 8:(q_ + 1) * 8], in_=gm[:, q_ * 8:(q_ + 1) * 8]),
                                     [gm], [top])
                            thr = top[:, :].rearrange("p (a b) -> p a b", b=8)[:, :, 2:3].to_broadcast([128, 32, 8])
                            tt(pen[:, :].rearrange("p (a b) -> p a b", b=8), gm[:, :].rearrange("p (a b) -> p a b", b=8), thr,
                               ALU.is_ge, [gm, top], [pen])
                            ckpt("u%dsel" % ui)
                            ts(pen[:, :], pen[:, :], -1.0, -NEGBIG, ALU.add, ALU.mult, [pen], [pen])
                            tt(pen[:, :], pen[:, :], omask[:, :], ALU.mult, [pen, omask], [pen])
                            for hh in range(2):
                                for grp in range(4):
                                    ps = PS[5 + grp % 2]
                                    for tt_ in range(4):
                                        t = grp * 4 + tt_
                                        col = (hh * 16 + t) * 8
                                        trp(ps[0:8, tt_ * 128:(tt_ + 1) * 128], pen[:, col:col + 8], idf[:, :], [pen, idf], [ps])
                                    prow = 64 if hh == 0 else 0
                                    cp(QTz[hh][prow:prow + 8, grp * 512:(grp + 1) * 512], ps[0:8, :], [ps], [QTz[hh]])

                        ckpt("u%dproj" % ui)
                        utasks = []
                        if kind == "moba":
                            for hh in range(2):
                                head = idx * 2 + hh
                                G = Gt[gslot % 2]
                                gslot += 1
                                load_G(G, head)
                                lo, hi = hh * 64, (hh + 1) * 64
                                dlo, dhi = (1 - hh) * 64, (2 - hh) * 64
                                for g in range(4):
                                    numb = PS[4 + (moba_rot["i"] % 4)]
                                    moba_rot["i"] += 1

                                    def epi(g=g, numb=numb, lo=lo, hi=hi, dlo=dlo, dhi=dhi, idx=idx):
                                        r_ = rd[g % 2]
                                        recip(r_[lo:hi, :], numb[dlo:dhi, :], [numb], [r_])
                                        tt(mixT[lo:hi, idx, g * 512:(g + 1) * 512], numb[lo:hi, :], r_[lo:hi, :], ALU.mult,
                                           [numb, r_], [mixT])
                                    utasks += attn_tasks(
                                        g,
                                        lambda j, hh=hh: KTz[hh][:, j * 128:(j + 1) * 128],
                                        lambda g_, c0, N, hh=hh: QTz[hh][:, g_ * 512 + c0:g_ * 512 + c0 + N],
                                        None,
                                        G, lambda j, hh=hh: VP[:, j, hh, :], None, numb, None, pexp, ptb, 1.0,
                                        [KTz[hh], QTz[hh]], [VP], on_done=lambda epi=epi: pending.append(epi))
                        else:
                            head = 8 + idx
                            G = Gt[gslot % 2]
                            gslot += 1
                            load_G(G, head)
                            for g in range(4):
                                for c in range(2):
                                    numb, denb = (PS[4], PS[5]) if c == 0 else (PS[6], PS[7])

                                    def done(g=g, idx=idx):
                                        act(rd[0][:, :], PS[5][:, :], AF.Ln, [PS[5]], [rd[0]])
                                        cp(o0[:, :], PS[4][:, :], [PS[4]], [o0])
                                        act(rd[1][:, :], PS[7][:, :], AF.Ln, [PS[7]], [rd[1]])
                                        cp(o1[:, :], PS[6][:, :], [PS[6]], [o1])

                                        def epi(g=g, idx=idx):
                                            act(rd[0][:, :], rd[0][:, :], AF.Exp, [rd[0]], [rd[0]], scale=-1.0)
                                            act(rd[1][:, :], rd[1][:, :], AF.Exp, [rd[1]], [rd[1]], scale=-1.0)
                                            tt(o0[:, :], o0[:, :], rd[0][:, :], ALU.mult, [o0, rd[0]], [o0])
                                            tt(o1[:, :], o1[:, :], rd[1][:, :], ALU.mult, [o1, rd[1]], [o1])
                                            stt(o0[:, :], o1[:, :], small[:, 8 + i2:9 + i2], o0[:, :], ALU.mult, ALU.add, [o1, o0, small], [o0])
                                            act(sq[:, :], o0[:, :], AF.Square, [o0], [sq])
                                            ssb = PS[7]
                                            mm(ssb[:, :], ones_f[:, :], sq[:, :], True, True, [ones_f, sq], [ssb])
                                            act(sq[:, :], ssb[:, :], AF.Ln, [ssb], [sq], bias=EPS, scale=1.0 / 128)
                                            act(sq[:, :], sq[:, :], AF.Exp, [sq], [sq], scale=-0.5)
                                            stt(mixT[:, 4 + idx, g * 512:(g + 1) * 512], o0[:, :], small[:, 12 + i2:13 + i2], sq[:, :],
                                                ALU.mult, ALU.mult, [o0, small, sq], [mixT])
                                        pending.append(epi)
                                    utasks += attn_tasks(
                                        g,
                                        lambda j, c=c: KTz[c][:, j * 128:(j + 1) * 128],
                                        lambda g_, c0, N, c=c: QTz[c][:, g_ * 512 + c0:g_ * 512 + c0 + N],
                                        None, G, lambda j: VP[:, j, 0, :], ones_bf[:, :], numb, denb, pexp, ptb, 1.0,
                                        [KTz[c], QTz[c]], [VP], on_done=(done if c == 1 else None))
                        run_pipeline(utasks, 3)
                        flush_pending()
                        ckpt("u%d" % ui)
                P.barrier()
                with ExitStack() as sw:
                    w_out_residual(sw, mixT, ewout_d[i2])
                    norm_body(sw, 4 + li, list(range(NT)))
            P.barrier()

        def conv_ffn(li):
            w_in = fwin_d[li]
            w_out = fwout_d[li]
            with ExitStack() as sf:
                actT = sbt(sf, "actT", [128, NCC, 1024], BF16)
                wug = [sbt(sf, "wug%d" % i, [128, 8, 256], BF16) for i in range(3)]
                wo2 = [sbt(sf, "wo2_%d" % i, [128, NCC, 256], BF16) for i in range(2)]
                graw = [sbt(sf, "graw%d" % i, [128, 1040], F32) for i in range(2)]
                A = [sbt(sf, "A%d" % i, [128, 512], F32) for i in range(2)]
                Ag = [sbt(sf, "Ag%d" % i, [128, 512], F32) for i in range(2)]
                gtail = sbt(sf, "gtail", [128, NCC, 2], F32)

                def load_pair(slot, cc):
                    src_u = w_in[:, cc * 128:(cc + 1) * 128].rearrange("(kc p) n -> p kc n", p=128)
                    src_g = w_in[:, DFF + cc * 128:DFF + (cc + 1) * 128].rearrange("(kc p) n -> p kc n", p=128)
                    dma("pool", wug[slot][:, :, 0:128], src_u, [], [wug[slot]])
                    dma("pool", wug[slot][:, :, 128:256], src_g, [], [wug[slot]])

                k = 0
                for half in range(2):
                    tiles = list(range(half * 8, half * 8 + 8))
                    load_pair(0, 0)
                    load_pair(1, 1)
                    for cc in range(NCC):
                        if cc + 2 < NCC:
                            load_pair((cc + 2) % 3, cc + 2)
                        W = wug[cc % 3]
                        gr = graw[cc % 2]
                        if half == 0:
                            memset(gr[:, 14:16], 0.0, [gr], eng="dve")
                        else:
                            cp(gr[:, 14:16], gtail[:, cc, :], [gtail], [gr])
                        w0 = convw[:, (li * 3 + 0) * NCC + cc:(li * 3 + 0) * NCC + cc + 1]
                        w1 = convw[:, (li * 3 + 1) * NCC + cc:(li * 3 + 1) * NCC + cc + 1]
                        w2 = convw[:, (li * 3 + 2) * NCC + cc:(li * 3 + 2) * NCC + cc + 1]
                        bb = convw[:, (12 + li) * NCC + cc:(12 + li) * NCC + cc + 1]
                        for tgi in range(2):
                            tok0 = half * 1024 + tgi * 512
                            pu = PS[(k * 2) % 8]
                            pg = PS[(k * 2 + 1) % 8]
                            k += 1
                            for kc in range(8):
                                mm(pg[:, :], W[:, kc, 128:256], hnT[:, kc, tok0:tok0 + 512], kc == 0, kc == 7, [W, hnT], [pg])
                            for kc in range(8):
                                mm(pu[:, :], W[:, kc, 0:128], hnT[:, kc, tok0:tok0 + 512], kc == 0, kc == 7, [W, hnT], [pu])
                            a = A[tgi]
                            ag = Ag[tgi]
                            act(gr[:, 16 + tgi * 512:16 + (tgi + 1) * 512], pg[:, :], AF.Copy, [pg], [gr])
                            ts(a[:, :], gr[:, 16 + tgi * 512:16 + (tgi + 1) * 512], w2, bb, ALU.mult, ALU.add, [gr, convw], [a])
                            stt(a[:, :], gr[:, 15 + tgi * 512:15 + (tgi + 1) * 512], w1, a[:, :], ALU.mult, ALU.add, [gr, a, convw], [a], eng=FMA_ENG)
                            stt(a[:, :], gr[:, 14 + tgi * 512:14 + (tgi + 1) * 512], w0, a[:, :], ALU.mult, ALU.add, [gr, a, convw], [a], eng=FMA_ENG)
                            act(ag[:, :], a[:, :], AF.Gelu, [a], [ag])
                            tt(actT[:, cc, tgi * 512:(tgi + 1) * 512], ag[:, :], pu[:, :], ALU.mult, [ag, pu], [actT])
                        if half == 0:
                            cp(gtail[:, cc, :], gr[:, 1038:1040], [gr], [gtail])
                    load_cols = lambda slot, cq: dma(
                        "pool", wo2[slot][:, :, :],
                        w_out[:, cq * 256:(cq + 1) * 256].rearrange("(cc p) n -> p cc n", p=128), [], [wo2[slot]])
                    load_cols(0, 0)
                    for cq in range(4):
                        if cq + 1 < 4:
                            load_cols((cq + 1) % 2, cq + 1)
                        Wo = wo2[cq % 2]
                        for tl in range(8):
                            t = half * 8 + tl
                            ps = PS[k % 8]
                            k += 1
                            for cc in range(NCC):
                                mm(ps[:, 0:256], actT[:, cc, tl * 128:(tl + 1) * 128], Wo[:, cc, :], cc == 0, cc == NCC - 1,
                                   [actT, Wo], [ps])
                            tt(h[:, t, cq * 256:(cq + 1) * 256], ps[:, 0:256], h[:, t, cq * 256:(cq + 1) * 256], ALU.add,
                               [ps, hB[t]], [hB[t]])
            P.barrier()

        def odd_mixer(li):
            i2 = li // 2
            w_in = owin_d[i2]
            with ExitStack() as sm:
                mixT = sbt(sm, "mixTo", [128, 8, S], BF16)
                norm_transpose(2 + i2, list(range(NT)))
                with ExitStack() as sa:
                    QTz = [sbt(sa, "oQTz%d" % i, [128, S], BF16) for i in range(2)]
                    KTz = [sbt(sa, "oKTz%d" % i, [128, S], BF16) for i in range(2)]
                    for i_ in range(2):
                        memset(QTz[i_][:, :], 0.0, [QTz[i_]], eng="dve")
                        memset(KTz[i_][:, :], 0.0, [KTz[i_]], eng="dve")
                    VS = sbt(sa, "VS", [128, NT, 2, 128], BF16)
                    accn = sbt(sa, "accn", [128, S], F32)
                    accd = sbt(sa, "accd", [128, S], F32)
                    Gd = sbt(sa, "Gd", [128, 3, 2, 256], BF16)
                    wq = [sbt(sa, "owq%d" % i, [128, 8, 128], BF16) for i in range(2)]
                    wk = [sbt(sa, "owk%d" % i, [128, 8, 128], BF16) for i in range(2)]
                    wv = [sbt(sa, "owv%d" % i, [128, 8, 128], BF16) for i in range(2)]
                    pexp = [sbt(sa, "opexp%d" % i, [128, 512], BF16) for i in range(4)]
                    ptb = [sbt(sa, "optb%d" % i, [128, 512], BF16) for i in range(4)]
                    rd = [sbt(sa, "ord%d" % i, [128, 512], F32) for i in range(2)]

                    memset(VS[:, :, :, :].rearrange("p a b c -> p (a b c)"), 0.0, [VS], eng="dve")

                    def load_unit_w(ui):
                        load_w(wq[ui % 2], w_in, ui * 128, 128, 8)
                        load_w(wk[ui % 2], w_in, 512 + ui * 128, 128, 8)
                        load_w(wv[ui % 2], w_in, 1024 + ui * 128, 128, 8)

                    load_unit_w(0)
                    sr = 0
                    vr = {"i": 0}
                    for ui in range(4):
                        if ui + 1 < 4:
                            load_unit_w(ui + 1)
                        Wq, Wk, Wv = wq[ui % 2], wk[ui % 2], wv[ui % 2]
                        def evac_k(tg, ps):
                            cp(KTz[0][0:64, tg * 512:(tg + 1) * 512], ps[0:64, :], [ps], [KTz[0]])
                            cp(KTz[1][64:128, tg * 512:(tg + 1) * 512], ps[64:128, :], [ps], [KTz[1]])

                        def evac_q(tg, ps):
                            ts(QTz[0][0:64, tg * 512:(tg + 1) * 512], ps[0:64, :], 0.125, None, ALU.mult, None, [ps], [QTz[0]])
                            ts(QTz[1][64:128, tg * 512:(tg + 1) * 512], ps[64:128, :], 0.125, None, ALU.mult, None, [ps], [QTz[1]])
                        proj_fm(Wk, 0, evac_k)
                        proj_fm(Wq, 0, evac_q)
                        for hh in range(2):
                            for gi in range(3):
                                idx = (ui * 2 + hh) * 3 + gi
                                src = bass.AP(m_dil, idx * 128 * 512 + 127, [[511, 128], [1, 256]])
                                dma("sp", Gd[:, gi, hh, :], src, [B_mdil[idx]], [Gd])
                        first = True
                        for gi, dil in enumerate((1, 4, 16)):
                            L = S // dil
                            nblk = max(1, L // 128)
                            for r in range(dil):
                                for n0 in range(0, nblk, 4):
                                    nn = min(4, nblk - n0)
                                    ps = PS[(3, 0, 1, 2)[vr["i"] % 4]]
                                    vr["i"] += 1
                                    for n_ in range(nn):
                                        n = n0 + n_
                                        t0 = n * 128 * dil + r
                                        for kc in range(8):
                                            mm(ps[:, n_ * 128:(n_ + 1) * 128],
                                               hnT[:, kc, t0:t0 + 127 * dil + 1:dil], Wv[:, kc, :],
                                               kc == 0, kc == 7, [hnT, Wv], [ps])
                                    slot0 = r * nblk + n0
                                    psv = ps[:, 0:nn * 128].rearrange("p (a b) -> p a b", b=128)
                                    act(VS[:, slot0:slot0 + nn, 0, 0:64], psv[:, :, 0:64], AF.Copy, [ps], [VS])
                                    act(VS[:, slot0:slot0 + nn, 1, 64:128], psv[:, :, 64:128], AF.Copy, [ps], [VS])
                            ckpt("d%dv%d" % (ui, gi))
                            tasks = []
                            for r in range(dil):
                                for seg0 in range(0, nblk, 4):
                                    segn = min(4, nblk - seg0)
                                    W_ = segn * 128
                                    numb, denb = (PS[4], PS[5]) if sr % 2 == 0 else (PS[6], PS[7])
                                    sr += 1
                                    kbs = [kb for kb in range(seg0 - 1, seg0 + segn) if kb >= 0]
                                    for ki, kb in enumerate(kbs):
                                        q0 = max(kb, seg0)
                                        q1 = min(kb + 2, seg0 + segn)
                                        N = (q1 - q0) * 128
                                        xoff = (q0 - kb) * 128
                                        cpos = (q0 - seg0) * 128
                                        kt0 = kb * 128 * dil + r
                                        qt0 = q0 * 128 * dil + r
                                        for hh in range(2):
                                            st_ = {}

                                            def score_fn(N=N, kt0=kt0, qt0=qt0, st_=st_, hh=hh):
                                                sb = PS[rot["s"] % 4]
                                                rot["s"] += 1
                                                st_["sb"] = sb
                                                mm(sb[:, 0:N], KTz[hh][:, kt0:kt0 + 127 * dil + 1:dil],
                                                   QTz[hh][:, qt0:qt0 + (N - 1) * dil + 1:dil], True, True, [KTz[hh], QTz[hh]], [sb])

                                            def rest_fn(N=N, xoff=xoff, cpos=cpos, kb=kb, r=r, st_=st_, numb=numb, denb=denb, hh=hh,
                                                        first_k=(ki == 0 and hh == 0), last_k=(ki == len(kbs) - 1 and hh == 1),
                                                        W_=W_, seg0=seg0, first=first):
                                                sb = st_["sb"]
                                                pe_ = pexp[rot["p"] % 4]
                                                pt = ptb[rot["p"] % 4]
                                                rot["p"] += 1
                                                act(pe_[:, 0:N], sb[:, 0:N], AF.Exp, [sb], [pe_])
                                                tt(pt[:, 0:N], pe_[:, 0:N], Gd[:, gi, hh, xoff:xoff + N], ALU.mult, [pe_, Gd], [pt])
                                                mm(numb[:, cpos:cpos + N], VS[:, r * nblk + kb, hh, :], pt[:, 0:N], first_k, False,
                                                   [pt, VS], [numb])
                                                mm(denb[:, cpos:cpos + N], (onesA if hh == 0 else onesB)[:, :], pt[:, 0:N], first_k, False,
                                                   [pt, onesA, onesB], [denb])
                                                if last_k:
                                                    tq0 = seg0 * 128 * dil + r
                                                    dst_n = accn[:, tq0:tq0 + (W_ - 1) * dil + 1:dil]
                                                    dst_d = accd[:, tq0:tq0 + (W_ - 1) * dil + 1:dil]
                                                    if first:
                                                        cp(dst_n, numb[:, 0:W_], [numb], [accn])
                                                        cp(dst_d, denb[:, 0:W_], [denb], [accd])
                                                    else:
                                                        tt(dst_n, numb[:, 0:W_], dst_n, ALU.add, [numb, accn], [accn])
                                                        tt(dst_d, denb[:, 0:W_], dst_d, ALU.add, [denb, accd], [accd])
                                            tasks.append((score_fn, rest_fn))
                            run_pipeline(tasks, 3)
                            ckpt("d%db%d" % (ui, gi))
                            first = False
                        ckpt("d%dpre" % ui)
                        for g in range(4):
                            r_ = rd[g % 2]
                            recip(r_[:, :], accd[:, g * 512:(g + 1) * 512], [accd], [r_])
                            tt(mixT[:, ui, g * 512:(g + 1) * 512], accn[:, g * 512:(g + 1) * 512], r_[:, :], ALU.mult, [accn, r_], [mixT])
                P.barrier()
                with ExitStack() as sb_:
                    wuq = sbt(sb_, "wuq", [128, 2, 768], BF16)
                    wukv = sbt(sb_, "wukv", [128, 1, 1024], BF16)
                    cqT = sbt(sb_, "cqT", [128, 2, S], BF16)
                    ckvT = sbt(sb_, "ckvT", [128, 1, S], BF16)
                    krT = sbt(sb_, "krT", [128, S], BF16)
                    rcs = sbt(sb_, "rcs", [64, 2 * S], F32)
                    xs1 = [sbt(sb_, "xs1_%d" % i, [64, 512], F32) for i in range(2)]
                    xs2 = [sbt(sb_, "xs2_%d" % i, [64, 512], F32) for i in range(2)]
                    lst = sbt(sb_, "lst", [128, 96], F32)
                    rr = {"i": 0}

                    def rope(dst_ap, dst_T, ps, col0, tok0, n):
                        a = xs1[rr["i"] % 2]
                        b = xs2[rr["i"] % 2]
                        rr["i"] += 1
                        tt(b[0:32, 0:n], ps[32:64, col0:col0 + n], rcs[0:32, S + tok0:S + tok0 + n], ALU.mult, [ps, rcs], [b])
                        tt(b[32:64, 0:n], ps[0:32, col0:col0 + n], rcs[32:64, S + tok0:S + tok0 + n], ALU.mult, [ps, rcs], [b])
                        tt(a[0:64, 0:n], ps[0:64, col0:col0 + n], rcs[0:64, tok0:tok0 + n], ALU.mult, [ps, rcs], [a])
                        tt(dst_ap, a[0:64, 0:n], b[0:64, 0:n], ALU.add, [a, b], [dst_T])

                    memset(krT[64:128, :], 0.0, [krT], eng="dve")
                    load_w(wuq, wuq_d[i2], 0, 768, 2)
                    load_w(wukv, wukv_d[i2], 0, 1024, 1)
                    dma("sp", rcs[0:64, 0:S], c_ropec, [], [rcs])
                    dma("sp", rcs[0:64, S:2 * S], c_ropes, [], [rcs])
                    with ExitStack() as s1:
                        wl = sbt(s1, "wl", [128, 8, 448], BF16)
                        lat = [sbt(s1, "lat%d" % i, [128, 448], F32) for i in range(4)]
                        load_w(wl, w_in, 1536, 448, 8)
                        lB = [Buf("lst%d" % t) for t in range(NT)]
                        memset(lst[:, 0:32], 0.0, lB, eng="dve")
                        for t in range(NT):
                            ps = PS[t % 4]
                            la = lat[t % 4]
                            for kc in range(8):
                                mm(ps[:, 0:448], hnT[:, kc, t * 128:(t + 1) * 128], wl[:, kc, :], kc == 0, kc == 7, [hnT, wl], [ps])
                            act(la[:, 0:256], ps[:, 0:256], AF.Square, [ps], [la, lB[t]], accum=lst[:, t:t + 1])
                            act(la[:, 256:384], ps[:, 256:384], AF.Square, [ps], [la, lB[t]], accum=lst[:, 16 + t:17 + t])
                            act(lst[:, 32 + t:33 + t], lst[:, t:t + 1], AF.Ln, [lB[t]], [lB[t]], bias=EPS, scale=1.0 / 256)
                            act(lst[:, 32 + t:33 + t], lst[:, 32 + t:33 + t], AF.Exp, [lB[t]], [lB[t]], scale=-0.5)
                            act(lst[:, 48 + t:49 + t], lst[:, 16 + t:17 + t], AF.Ln, [lB[t]], [lB[t]], bias=EPS, scale=1.0 / 128)
                            act(lst[:, 48 + t:49 + t], lst[:, 48 + t:49 + t], AF.Exp, [lB[t]], [lB[t]], scale=-0.5)
                            act(la[:, 0:256], ps[:, 0:256], AF.Copy, [ps, lB[t]], [la], scale=lst[:, 32 + t:33 + t])
                            act(la[:, 256:384], ps[:, 256:384], AF.Copy, [ps, lB[t]], [la], scale=lst[:, 48 + t:49 + t])
                            act(la[:, 384:448], ps[:, 384:448], AF.Copy, [ps], [la])
                            pt_ = PS[4 + t % 4]
                            for c in range(3):
                                trp(pt_[:, c * 128:(c + 1) * 128], la[:, c * 128:(c + 1) * 128], idf[:, :], [la, idf], [pt_])
                            trp(pt_[0:64, 384:512], la[:, 384:448], idf[:, :], [la, idf], [pt_])
                            for c in range(2):
                                ts(cqT[:, c, t * 128:(t + 1) * 128], pt_[:, c * 128:(c + 1) * 128], small[:, 2 + 2 * i2 + c:3 + 2 * i2 + c],
                                   None, ALU.mult, None, [pt_, small], [cqT])
                            ts(ckvT[:, 0, t * 128:(t + 1) * 128], pt_[:, 256:384], small[:, 6 + i2:7 + i2], None, ALU.mult, None,
                               [pt_, small], [ckvT])
                            rope_dst = krT[0:64, t * 128:(t + 1) * 128]
                            rope2(rope_dst, krT, pt_, 384, t * 128, 128, rope)
                    P.barrier()
                    with ExitStack() as s2:
                        qnT = sbt(s2, "qnT", [128, S], BF16)
                        qrT = sbt(s2, "qrT", [128, S], BF16)
                        memset(qrT[64:128, :], 0.0, [qrT], eng="dve")
                        knT = sbt(s2, "knT", [128, S], BF16)
                        VM = sbt(s2, "VM", [128, NT, 128], BF16)
                        ptb = [sbt(s2, "mptb%d" % i, [128, 512], BF16) for i in range(4)]
                        rd = [sbt(s2, "mrd%d" % i, [128, 512], F32) for i in range(2)]
                        scale = 192.0 ** -0.5
                        for hd in range(4):
                            proj_fm(wuq, hd * 192, lambda tg, ps: cp(qnT[:, tg * 512:(tg + 1) * 512], ps[:, :], [ps], [qnT]),
                                    kchunks=2, src=cqT)
                            proj_fm(wuq, hd * 192 + 128,
                                    lambda tg, ps: rope2(qrT[0:64, tg * 512:(tg + 1) * 512], qrT, ps, 0, tg * 512, 512, rope),
                                    kchunks=2, src=cqT, M=64)
                            proj_fm(wukv, hd * 256, lambda tg, ps: cp(knT[:, tg * 512:(tg + 1) * 512], ps[:, :], [ps], [knT]),
                                    kchunks=1, src=ckvT)
                            for tq in range(4):
                                ps = PS[3 + tq % 2]
                                for tt_ in range(4):
                                    t = tq * 4 + tt_
                                    mm(ps[:, tt_ * 128:(tt_ + 1) * 128], ckvT[:, 0, t * 128:(t + 1) * 128],
                                       wukv[:, 0, hd * 256 + 128:hd * 256 + 256], True, True, [ckvT, wukv], [ps])
                                act(VM[:, tq * 4:(tq + 1) * 4, :], ps[:, :].rearrange("p (a b) -> p a b", b=128), AF.Copy, [ps], [VM])
                            mtasks = []
                            for g in range(4):
                                numb, denb = (PS[4], PS[5]) if g % 2 == 0 else (PS[6], PS[7])

                                def epi(g=g, numb=numb, denb=denb, hd=hd):
                                    r_ = rd[g % 2]
                                    recip(r_[:, :], denb[:, :], [denb], [r_])
                                    tt(mixT[:, 4 + hd, g * 512:(g + 1) * 512], numb[:, :], r_[:, :], ALU.mult, [numb, r_], [mixT])
                                mtasks += attn_tasks(
                                    g,
                                    lambda j: knT[:, j * 128:(j + 1) * 128],
                                    lambda g_, c0, N: qnT[:, g_ * 512 + c0:g_ * 512 + c0 + N],
                                    lambda j, g_, c0, N: (krT[:, j * 128:(j + 1) * 128],
                                                          qrT[:, g_ * 512 + c0:g_ * 512 + c0 + N]),
                                    None, lambda j: VM[:, j, :], ones_bf[:, :], numb, denb, None, ptb, scale,
                                    [knT, qnT, krT, qrT], [VM], mla=True, on_done=lambda epi=epi: pending.append(epi))
                            run_pipeline(mtasks, 3)
                            flush_pending()
                P.barrier()
                with ExitStack() as sw:
                    w_out_residual(sw, mixT, owout_d[i2])
                    norm_body(sw, 4 + li, list(range(NT)))
            P.barrier()

        try:
            ckpt("setup")
            for li in range(n_layers):
                if li % 2 == 0:
                    even_mixer(li)
                else:
                    odd_mixer(li)
                ckpt("mixer%d" % li)
                conv_ffn(li)
        except _Stop:
            pass
        P.stopped = False
        P.barrier()

        with ExitStack() as sfin:
            gfin = sbt(sfin, "gfin", [128, D], F32)
            yb = [sbt(sfin, "yb%d" % i, [128, D], F32) for i in range(2)]
            dma("sp", gfin[:, :], fin_d.partition_broadcast(128), [], [gfin])
            for t in range(NT):
                y = yb[t % 2]
                if debug_h:
                    cp(y[:, :], h[:, t, :], [hB[t]], [y])
                else:
                    act(y[:, :], h[:, t, :], AF.Square, [hB[t]], [y, stat], accum=stat[:, t:t + 1])
                    act(stat[:, 16 + t:17 + t], stat[:, t:t + 1], AF.Ln, [stat], [stat], bias=EPS, scale=1.0 / D)
                    act(stat[:, 32 + t:33 + t], stat[:, 16 + t:17 + t], AF.Exp, [stat], [stat], scale=-0.5)
                    stt(y[:, :], h[:, t, :], stat[:, 32 + t:33 + t], gfin[:, :], ALU.mult, ALU.mult, [hB[t], stat, gfin], [y])
                dma("sp", out_d[t * 128:(t + 1) * 128, :], y[:, :], [y], [B_out])
            P.wait_all("sp", [B_out])
            P.barrier()

        with nc.Block() as block:
            P.emit(block)
    return nc


_CACHE = {}


def kernel(**inputs):
    n = 8
    consts = _host_consts()
    if "nc" not in _CACHE:
        _CACHE["nc"] = build_program()
    nc = _CACHE["nc"]
    x = np.ascontiguousarray(np.asarray(inputs["x"], dtype=np.float32))
    shared = {k: np.ascontiguousarray(np.asarray(v, dtype=np.float32)) for k, v in inputs.items() if k != "x"}
    shared.update(consts)
    in_maps = []
    for c in range(n):
        m = dict(shared)
        m["x"] = x[c]
        in_maps.append(m)
    res = run_bass_kernel_spmd(nc, in_maps, core_ids=list(range(n)))
    out = np.stack([np.asarray(r["out"], dtype=np.float32) for r in res.results], axis=0)
    return out
```

```python
import math
from contextlib import ExitStack

import numpy as np
import concourse.bass as bass
import concourse.mybir as mybir
from concourse.bass_utils import run_bass_kernel_spmd

F32 = mybir.dt.float32
BF16 = mybir.dt.bfloat16
ALU = mybir.AluOpType
AF = mybir.ActivationFunctionType
AX = mybir.AxisListType

S = 2048
D = 1024
NT = 16
DEPTH = 4
DFF = 2816
NCC = 22
EPS = 1e-6
LW = 2560
GW = 2432
NEGBIG = -30000.0


class Buf:
    __slots__ = ("name", "w", "rs")

    def __init__(self, name):
        self.name = name
        self.w = None
        self.rs = []


class T:
    __slots__ = ("t", "b")

    def __init__(self, t, name):
        self.t = t
        self.b = Buf(name)

    def __getitem__(self, k):
        return self.t[k]


class Prog:
    ENGS = ("pe", "act", "dve", "pool", "sp")

    def __init__(self, nc, n_dma_sems=8):
        self.nc = nc
        self.ops = {e: [] for e in self.ENGS}
        self.cnt = {e: 0 for e in self.ENGS}
        self.waited = {e: {} for e in self.ENGS}
        self.sems = {}
        self.n_dma_sems = n_dma_sems
        self.dma_i = {"sp": 0, "pool": 0, "act": 0}
        self.dma_last = {}
        self.stopped = False

    def alloc_sems(self, stack):
        for e in self.ENGS:
            self.sems[e] = stack.enter_context(self.nc.semaphore("s_" + e))
        for q in ("sp", "pool", "act"):
            for i in range(self.n_dma_sems):
                self.sems[("dma", q, i)] = stack.enter_context(self.nc.semaphore(f"d_{q}_{i}"))

    def _need(self, eng, waits, ev, kind):
        if ev is None:
            return
        semkey, val, src = ev
        if src == eng and semkey == eng:
            if eng == "pe" or kind == "war":
                return
        if self.waited[eng].get(semkey, 0) >= val:
            return
        if waits.get(semkey, 0) < val:
            waits[semkey] = val

    def _deps(self, eng, reads, writes):
        waits = {}
        for b in reads:
            self._need(eng, waits, b.w, "raw")
        for b in writes:
            self._need(eng, waits, b.w, "waw")
            for r in b.rs:
                self._need(eng, waits, r, "war")
        for k, v in waits.items():
            self.waited[eng][k] = v
        return list(waits.items())

    def op(self, eng, fn, reads=(), writes=()):
        if self.stopped:
            return None
        reads = [x.b if isinstance(x, T) else x for x in reads]
        writes = [x.b if isinstance(x, T) else x for x in writes]
        waits = self._deps(eng, reads, writes)
        self.cnt[eng] += 1
        ev = (eng, self.cnt[eng], eng)
        self.ops[eng].append((waits, fn, (eng, 1)))
        for b in reads:
            b.rs.append(ev)
            if len(b.rs) > 64:
                b.rs = self._prune(b.rs)
        for b in writes:
            b.w = ev
            b.rs = []
        return ev

    @staticmethod
    def _prune(rs):
        best = {}
        for (k, v, s) in rs:
            if k not in best or best[k][1] < v:
                best[k] = (k, v, s)
        return list(best.values())

    def dma(self, q, fn, reads=(), writes=()):
        if self.stopped:
            return None
        reads = [x.b if isinstance(x, T) else x for x in reads]
        writes = [x.b if isinstance(x, T) else x for x in writes]
        waits = self._deps(q, reads, writes)
        i = self.dma_i[q]
        self.dma_i[q] += 1
        slot = i % self.n_dma_sems
        semkey = ("dma", q, slot)
        val = 16 * (i // self.n_dma_sems + 1)
        if val > 16 and self.waited[q].get(semkey, 0) < val - 16:
            waits.append((semkey, val - 16))
            self.waited[q][semkey] = val - 16
        ev = (semkey, val, q + "_dma")
        self.dma_last[semkey] = val
        self.ops[q].append((waits, fn, (semkey, 16)))
        for b in reads:
            b.rs.append(ev)
        for b in writes:
            b.w = ev
            b.rs = []
        return ev

    def barrier(self):
        if self.stopped:
            return
        for e in self.ENGS:
            waits = []
            for o in self.ENGS:
                if o == e:
                    continue
                v = self.cnt[o]
                if v > 0 and self.waited[e].get(o, 0) < v:
                    waits.append((o, v))
                    self.waited[e][o] = v
            for k, v in self.dma_last.items():
                if self.waited[e].get(k, 0) < v:
                    waits.append((k, v))
                    self.waited[e][k] = v
            if waits:
                self.ops[e].append((waits, None, None))

    def wait_all(self, eng, bufs):
        bufs = [x.b if isinstance(x, T) else x for x in bufs]
        waits = self._deps(eng, bufs, ())
        self.ops[eng].append((waits, None, None))

    def emit(self, block):
        sems = self.sems

        def run(e, lst):
            for waits, fn, inc in lst:
                for k, v in waits:
                    e.wait_ge(sems[k], v)
                if fn is not None:
                    fn(e).then_inc(sems[inc[0]], inc[1])

        @block.tensor
        def _(e):
            run(e, self.ops["pe"])

        @block.scalar
        def _(e):
            run(e, self.ops["act"])

        @block.vector
        def _(e):
            run(e, self.ops["dve"])

        @block.gpsimd
        def _(e):
            run(e, self.ops["pool"])

        @block.sync
        def _(e):
            run(e, self.ops["sp"])


def _np_bucket(dist):
    n = np.maximum(dist, 0)
    nf = np.maximum(n, 1).astype(np.float32)
    large = 16 + (np.log(nf / np.float32(16)) / np.float32(math.log(64)) * np.float32(16)).astype(np.int32)
    large = np.minimum(large, 31)
    return np.where(n < 16, n, large)


def _host_consts():
    c = {}
    oh = np.zeros((33, LW), np.float32)
    d = np.arange(LW) - 511
    b = _np_bucket(d)
    for i in range(LW):
        if d[i] < 0:
            oh[32, i] = 1.0
        else:
            oh[b[i], i] = 1.0
    c["c_oh_main"] = oh
    ohd = np.zeros((33, 3 * 512), np.float32)
    for gi, dil in enumerate((1, 4, 16)):
        for i in range(512):
            s = i - 127
            if 0 <= s <= 128:
                ohd[_np_bucket(np.array([s * dil]))[0], gi * 512 + i] = 1.0
            else:
                ohd[32, gi * 512 + i] = 1.0
    c["c_oh_dil"] = ohd
    koh = np.zeros((8, S), np.float32)
    for n in range(8):
        koh[n, n * 256:(n + 1) * 256] = 1.0
    c["c_koh"] = koh
    gm = np.zeros((2, 16, 8), np.float32)
    om = np.ones((2, 16, 8), np.float32)
    for t in range(16):
        bq = t // 2
        gm[:, t, bq:] = -1e30
        om[:, t, bq] = 0.0
    c["c_gmask"] = np.broadcast_to(gm.reshape(1, 256), (128, 256)).copy()
    c["c_omask"] = np.broadcast_to(om.reshape(1, 256), (128, 256)).copy()
    half = 32
    freq = (np.float32(10000.0) ** (-np.arange(half, dtype=np.float32) / np.float32(half))).astype(np.float32)
    ang = np.arange(S, dtype=np.float32)[None, :] * freq[:, None]
    cs, sn = np.cos(ang).astype(np.float32), np.sin(ang).astype(np.float32)
    c["c_ropec"] = np.concatenate([cs, cs], 0)
    c["c_ropes"] = np.concatenate([-sn, sn], 0)
    tri = (np.arange(128)[None, :] >= np.arange(128)[:, None]).astype(np.float32)
    c["c_tri"] = tri
    return c


import os as _os
DIL_SPLIT = bool(_os.environ.get("DIL_SPLIT"))
FMA_ENG = "dve"


class _Stop(Exception):
    pass


def build_program(n_layers=DEPTH, debug_h=False, stop_at=None):
    nc = bass.Bass("TRN2", target_bir_lowering=False)

    stop_state = {"P": None}

    def ckpt(name):
        if stop_at == name:
            stop_state["P"].stopped = True

    dram_in = {}

    def din(name, shape):
        dram_in[name] = nc.dram_tensor(name, list(shape), F32, kind="ExternalInput").ap()
        return dram_in[name]

    x_d = din("x", [S, D])
    rel_d = din("rel_bias", [32, 20])
    en1_d = din("even_norm1", [2, D])
    ewin_d = din("even_w_in", [2, D, 3072])
    dlam_d = din("diff_lambda", [2, 4, 64])
    dsub_d = din("diff_subln", [2, 128])
    ewout_d = din("even_w_out", [2, D, D])
    on1_d = din("odd_norm1", [2, D])
    owin_d = din("odd_w_in", [2, D, 1984])
    qn_d = din("mla_q_norm", [2, 256])
    wuq_d = din("mla_w_uq", [2, 256, 768])
    kvn_d = din("mla_kv_norm", [2, 128])
    wukv_d = din("mla_w_ukv", [2, 128, 1024])
    owout_d = din("odd_w_out", [2, D, D])
    fn_d = din("ffn_norm", [4, D])
    fwin_d = din("ffn_w_in", [4, D, 2 * DFF])
    fcw_d = din("ffn_conv_w", [4, 3, DFF])
    fcb_d = din("ffn_conv_b", [4, DFF])
    fwout_d = din("ffn_w_out", [4, DFF, D])
    fin_d = din("final_norm", [D])
    c_oh_main = din("c_oh_main", [33, LW])
    c_oh_dil = din("c_oh_dil", [33, 1536])
    c_koh = din("c_koh", [8, S])
    c_gmask = din("c_gmask", [128, 256])
    c_omask = din("c_omask", [128, 256])
    c_ropec = din("c_ropec", [64, S])
    c_ropes = din("c_ropes", [64, S])
    c_tri = din("c_tri", [128, 128])
    out_d = nc.dram_tensor("out", [S, D], F32, kind="ExternalOutput").ap()
    m_main = nc.dram_tensor("m_main", [12 * 128, LW], BF16)
    m_dil = nc.dram_tensor("m_dil", [24 * 128, 512], BF16)
    B_mmain = [Buf("mmain%d" % i) for i in range(12)]
    B_mdil = [Buf("mdil%d" % i) for i in range(24)]
    B_out = Buf("out")

    with ExitStack() as st:
        P = Prog(nc)
        stop_state["P"] = P
        P.alloc_sems(st)
        st.enter_context(nc.allow_non_contiguous_dma("small strided parameter loads"))

        uniq = {"i": 0}

        def sbt(stack, name, shape, dt):
            uniq["i"] += 1
            name = "%s_%d" % (name, uniq["i"])
            return T(stack.enter_context(nc.sbuf_tensor(name, list(shape), dt)), name)

        def mm(out, lhsT, rhs, start, stop, r, w):
            P.op("pe", lambda e: e.matmul(out, lhsT=lhsT, rhs=rhs, start=start, stop=stop,
                                          skip_group_check=True), r, w)

        def trp(out, in_, ident, r, w):
            P.op("pe", lambda e: e.transpose(out=out, in_=in_, identity=ident), r, w)

        def act(out, in_, func, r, w, bias=None, scale=None, accum=None):
            kw = {}
            if bias is not None:
                kw["bias"] = bias
            if scale is not None:
                kw["scale"] = scale
            if accum is not None:
                kw["accum_out"] = accum
            P.op("act", lambda e: e.activation(out=out, in_=in_, func=func, **kw), r, w)

        def tt(out, in0, in1, op, r, w, eng="dve"):
            P.op(eng, lambda e: e.tensor_tensor(out=out, in0=in0, in1=in1, op=op), r, w)

        def ts(out, in0, s1, s2, op0, op1, r, w, eng="dve"):
            if s2 is None:
                P.op(eng, lambda e: e.tensor_scalar(out=out, in0=in0, scalar1=s1, scalar2=None, op0=op0), r, w)
            else:
                P.op(eng, lambda e: e.tensor_scalar(out=out, in0=in0, scalar1=s1, scalar2=s2, op0=op0, op1=op1), r, w)

        def stt(out, in0, scalar, in1, op0, op1, r, w, eng="dve"):
            P.op(eng, lambda e: e.scalar_tensor_tensor(out=out, in0=in0, scalar=scalar, in1=in1, op0=op0, op1=op1), r, w)

        def cp(out, in_, r, w, eng="dve"):
            P.op(eng, lambda e: e.tensor_copy(out=out, in_=in_), r, w)

        def recip(out, in_, r, w):
            act(out, in_, AF.Ln, r, w)
            act(out, out, AF.Exp, list(w), w, scale=-1.0)

        def memset(ap, val, w, eng="pool"):
            P.op(eng, lambda e: e.memset(ap, val), (), w)

        def dma(q, out, in_, r, w):
            P.dma(q, lambda e: e.dma_start(out=out, in_=in_), r, w)

        def run_pipeline(tasks, depth):
            n = len(tasks)
            for i in range(min(depth, n)):
                tasks[i][0]()
            for i in range(n):
                if i + depth < n:
                    tasks[i + depth][0]()
                tasks[i][1]()

        def rope2(dst_ap, dst_T, ps, col0, tok0, n, rope_fn):
            rope_fn(dst_ap, dst_T, ps, col0, tok0, n)

        h = sbt(st, "h", [128, NT, D], F32)
        hB = [Buf("h%d" % t) for t in range(NT)]
        hnT = sbt(st, "hnT", [128, 8, S], BF16)
        idf = sbt(st, "idf", [128, 128], F32)
        idb = sbt(st, "idb", [128, 128], BF16)
        ones_bf = sbt(st, "ones_bf", [128, 128], BF16)
        onesA = sbt(st, "onesA", [128, 128], BF16)
        onesB = sbt(st, "onesB", [128, 128], BF16)
        ones_f = sbt(st, "ones_f", [128, 128], F32)
        tri_bf = sbt(st, "tri_bf", [128, 128], BF16)
        gmask = sbt(st, "gmask", [128, 256], F32)
        omask = sbt(st, "omask", [128, 256], F32)
        convw = sbt(st, "convw", [128, 16 * NCC], F32)
        small = sbt(st, "small", [128, 64], F32)
        lamb = sbt(st, "lamb", [128, 2 * 256], F32)
        stat = sbt(st, "stat", [128, 64], F32)
        PS = [T(st.enter_context(nc.psum_tensor("ps%d" % i, [128, 512], F32)), "ps%d" % i) for i in range(8)]

        memset(idf[:], 0.0, [idf])
        P.op("pool", lambda e: e.affine_select(out=idf[:], in_=idf[:], pattern=[[-1, 128]],
                                               compare_op=ALU.not_equal, fill=1.0, base=0,
                                               channel_multiplier=1), [idf], [idf])
        cp(idb[:, :], idf[:, :], [idf], [idb])
        memset(ones_bf[:], 1.0, [ones_bf])
        memset(ones_f[:], 1.0, [ones_f])
        memset(onesA[:], 0.0, [onesA])
        memset(onesA[:, 0:64], 1.0, [onesA])
        memset(onesB[:], 0.0, [onesB])
        memset(onesB[:, 64:128], 1.0, [onesB])
        dma("pool", tri_bf[:], c_tri, [], [tri_bf])
        dma("sp", gmask[:], c_gmask, [], [gmask])
        dma("sp", omask[:], c_omask, [], [omask])
        for t in range(NT):
            dma("act", h[:, t, :], x_d[t * 128:(t + 1) * 128, :], [], [hB[t]])
        with ExitStack() as sc:
            raws = [sbt(sc, "craw%d" % i, [128, 128], F32) for i in range(4)]
            for i_ in range(4):
                memset(raws[i_][:, :], 0.0, [raws[i_]], eng="dve")
            vecs = [fcw_d[li, k] for li in range(4) for k in range(3)] + [fcb_d[li] for li in range(4)]
            for v, vec in enumerate(vecs):
                tl, j = v // 5, v % 5
                dma("sp", raws[tl][j * NCC:(j + 1) * NCC, :], vec.rearrange("(c p) -> c p", p=128), [raws[tl]], [raws[tl]])
            for tl in range(4):
                nv = min(5, 16 - tl * 5)
                ps = PS[tl]
                trp(ps[:, 0:128], raws[tl][:, :], idf[:, :], [raws[tl], idf], [ps])
                cp(convw[:, tl * 5 * NCC:(tl * 5 + nv) * NCC], ps[:, 0:nv * NCC], [ps], [convw])
            P.barrier()
        for i in range(2):
            dma("sp", small[:, i:i + 1], dsub_d[i].rearrange("(c p) -> p c", p=128), [], [small])
            dma("sp", small[:, 2 + 2 * i:4 + 2 * i], qn_d[i].rearrange("(c p) -> p c", p=128), [], [small])
            dma("sp", small[:, 6 + i:7 + i], kvn_d[i].rearrange("(c p) -> p c", p=128), [], [small])
            dma("sp", lamb[:, i * 256:(i + 1) * 256],
                dlam_d[i].rearrange("a b -> (a b)").partition_broadcast(128), [], [lamb])
        for i in range(2):
            layer = 2 * i
            lam_init = 0.8 - 0.6 * math.exp(-0.3 * layer)
            lp = lamb[:, i * 256:(i + 1) * 256]
            tt(lamb[:, i * 256:i * 256 + 64], lamb[:, i * 256:i * 256 + 64], lamb[:, i * 256 + 64:i * 256 + 128], ALU.mult, [lamb], [lamb])
            tt(lamb[:, i * 256 + 128:i * 256 + 192], lamb[:, i * 256 + 128:i * 256 + 192], lamb[:, i * 256 + 192:i * 256 + 256], ALU.mult, [lamb], [lamb])
            P.op("dve", lambda e, i=i: e.reduce_sum(out=small[:, 16 + 2 * i:17 + 2 * i], in_=lamb[:, i * 256:i * 256 + 64], axis=AX.X), [lamb], [small])
            P.op("dve", lambda e, i=i: e.reduce_sum(out=small[:, 17 + 2 * i:18 + 2 * i], in_=lamb[:, i * 256 + 128:i * 256 + 192], axis=AX.X), [lamb], [small])
            act(small[:, 20 + 2 * i:22 + 2 * i], small[:, 16 + 2 * i:18 + 2 * i], AF.Exp, [small], [small])
            tt(small[:, 8 + i:9 + i], small[:, 21 + 2 * i:22 + 2 * i], small[:, 20 + 2 * i:21 + 2 * i], ALU.subtract, [small], [small])
            ts(small[:, 8 + i:9 + i], small[:, 8 + i:9 + i], -lam_init, None, ALU.add, None, [small], [small])
            ts(small[:, 12 + i:13 + i], small[:, i:i + 1], 1.0 - lam_init, None, ALU.mult, None, [small], [small])

        f_main = nc.dram_tensor("f_main", [20, LW], BF16)
        f_dil = nc.dram_tensor("f_dil", [20, 1536], BF16)
        B_fmain = Buf("fmain")
        B_fdil = Buf("fdil")
        with ExitStack() as s0:
            tab = sbt(s0, "tab", [33, 20], F32)
            ohm = sbt(s0, "ohm", [33, LW], F32)
            ohd = sbt(s0, "ohd", [33, 1536], F32)
            fsb = sbt(s0, "fsb", [20, LW], BF16)
            fsd = sbt(s0, "fsd", [20, 1536], BF16)
            memset(tab[:], NEGBIG, [tab])
            dma("sp", tab[0:32, :], rel_d, [tab], [tab])
            dma("sp", ohm[:], c_oh_main, [], [ohm])
            dma("sp", ohd[:], c_oh_dil, [], [ohd])
            k = 0
            for c in range(LW // 512):
                ps = PS[k % 8]
                k += 1
                mm(ps[0:20, :], tab[0:33, 0:20], ohm[0:33, c * 512:(c + 1) * 512], True, True, [tab, ohm], [ps])
                act(fsb[0:20, c * 512:(c + 1) * 512], ps[0:20, :], AF.Exp, [ps], [fsb])
            for gi in range(3):
                ps = PS[k % 8]
                k += 1
                mm(ps[0:20, :], tab[0:33, 0:20], ohd[0:33, gi * 512:(gi + 1) * 512], True, True, [tab, ohd], [ps])
                act(fsd[0:20, gi * 512:(gi + 1) * 512], ps[0:20, :], AF.Exp, [ps], [fsd])
            dma("sp", f_main.ap()[:, :], fsb[0:20, :], [fsb], [B_fmain])
            dma("sp", f_dil.ap()[:, :], fsd[0:20, :], [fsd], [B_fdil])
            P.barrier()
        def emit_table_broadcasts():
            for hh in range(12):
                src = bass.AP(f_main, hh * LW, [[0, 128], [1, LW]])
                dma("sp", m_main.ap()[hh * 128:(hh + 1) * 128, :], src, [B_fmain], [B_mmain[hh]])
            for hh in range(8):
                for gi in range(3):
                    idx = hh * 3 + gi
                    src = bass.AP(f_dil, (12 + hh) * 1536 + gi * 512, [[0, 128], [1, 512]])
                    dma("sp", m_dil.ap()[idx * 128:(idx + 1) * 128, :], src, [B_fdil], [B_mdil[idx]])

        def load_G(dst, head):
            src = bass.AP(m_main, head * 128 * LW + 127, [[LW - 1, 128], [1, GW]])
            dma("sp", dst[:, 0:GW], src, [B_mmain[head]], [dst])

        def load_Gdil(dst, idx):
            src = bass.AP(m_dil, idx * 128 * 512 + 127, [[511, 128], [1, 256]])
            dma("sp", dst, src, [B_mdil[idx]], [])

        gain_src = [en1_d[0], en1_d[1], on1_d[0], on1_d[1], fn_d[0], fn_d[1], fn_d[2], fn_d[3]]

        def norm_transpose(gidx, tiles):
            with ExitStack() as sn:
                norm_body(sn, gidx, tiles)
            P.barrier()

        def norm_body(sn, gidx, tiles):
            if True:
                gb = sbt(sn, "gb", [128, D], F32)
                junk = sbt(sn, "junk", [128, D], BF16)
                xg = [sbt(sn, "xg%d" % i, [128, D], BF16) for i in range(4)]
                dma("sp", gb[:, :], gain_src[gidx].partition_broadcast(128), [], [gb])
                sB = [Buf("st%d" % t) for t in range(NT)]
                hnB = [Buf("hn%d" % t) for t in range(NT)]
                memset(stat[:, 0:16], 0.0, sB, eng="dve")
                def evac(i_, t):
                    ps = PS[i_ % 8]
                    src = ps[:, :].bitcast(BF16)[:, :].rearrange("p (c n) -> p c n", n=128)
                    if i_ % 2 == 0:
                        cp(hnT[:, :, t * 128:(t + 1) * 128], src, [ps], [hnB[t]])
                    else:
                        act(hnT[:, :, t * 128:(t + 1) * 128], src, AF.Copy, [ps], [hnB[t]])

                for i_, t in enumerate(tiles):
                    x_ = xg[i_ % 4]
                    act(junk[:, :], h[:, t, :], AF.Square, [hB[t]], [junk, sB[t]], accum=stat[:, t:t + 1])
                    act(stat[:, 16 + t:17 + t], stat[:, t:t + 1], AF.Ln, [sB[t]], [sB[t]], bias=EPS, scale=1.0 / D)
                    act(stat[:, 32 + t:33 + t], stat[:, 16 + t:17 + t], AF.Exp, [sB[t]], [sB[t]], scale=-0.5)
                    stt(x_[:, :], h[:, t, :], stat[:, 32 + t:33 + t], gb[:, :], ALU.mult, ALU.mult, [hB[t], sB[t], gb], [x_])
                    ps = PS[i_ % 8]
                    psb = ps[:, :].bitcast(BF16)
                    for c in range(8):
                        trp(psb[:, c * 128:(c + 1) * 128], x_[:, c * 128:(c + 1) * 128], idb[:, :], [x_, idb], [ps])
                    if i_ >= 1:
                        evac(i_ - 1, tiles[i_ - 1])
                evac(len(tiles) - 1, tiles[-1])

        def load_w(dst, w2d, col0, ncol, nk):
            src = w2d[:, col0:col0 + ncol].rearrange("(kc p) n -> p kc n", p=128)
            dma("pool", dst[:, 0:nk, 0:ncol], src, [], [dst])

        def proj_fm(wslab, wc0, evac, kchunks=8, src=None, banks=(0, 1, 2), M=128):
            src = hnT if src is None else src
            for tg in range(4):
                ps = PS[banks[tg % len(banks)]]
                for kc in range(kchunks):
                    mm(ps[0:M, :], wslab[:, kc, wc0:wc0 + M], src[:, kc, tg * 512:(tg + 1) * 512],
                       kc == 0, kc == kchunks - 1, [wslab, src], [ps])
                evac(tg, ps)

        rot = {"s": 0, "p": 0}
        pending = []

        def flush_pending():
            while pending:
                pending.pop(0)()

        def attn_tasks(g, k_of, q_of, extra_of, Gt, v_of, ones_ap, numb, denb, pexp, ptb, exp_scale, kq_reads, v_reads,
                       mla=False, on_done=None):
            nj = 4 * g + 4
            tasks = []
            for j in range(nj):
                c0 = max(0, j - 4 * g) * 128
                N = 512 - c0
                kj = k_of(j)
                qj = q_of(g, c0, N)
                ex = extra_of(j, g, c0, N) if extra_of is not None else None
                vj = v_of(j)
                gc = (4 * g - j + 3) * 128 + c0
                st_ = {}

                def score_fn(N=N, kj=kj, qj=qj, ex=ex, st_=st_):
                    sb = PS[rot["s"] % 4]
                    rot["s"] += 1
                    st_["sb"] = sb
                    mm(sb[:, 0:N], kj, qj, True, ex is None, kq_reads, [sb])
                    if ex is not None:
                        mm(sb[:, 0:N], ex[0], ex[1], False, True, kq_reads, [sb])

                def rest_fn(j=j, c0=c0, N=N, vj=vj, gc=gc, st_=st_):
                    sb = st_["sb"]
                    if mla:
                        pt = ptb[rot["p"] % len(ptb)]
                    else:
                        allb = pexp + ptb
                        pt = allb[rot["p"] % len(allb)]
                    rot["p"] += 1
                    act(pt[:, 0:N], sb[:, 0:N], AF.Exp, [sb], [pt], scale=exp_scale)
                    if mla:
                        if j >= 4 * g:
                            tt(pt[:, 0:128], pt[:, 0:128], tri_bf[:, :], ALU.mult, [pt, tri_bf], [pt])
                    else:
                        tt(pt[:, 0:N], pt[:, 0:N], Gt[:, gc:gc + N], ALU.mult, [pt, Gt], [pt])
                    mm(numb[:, c0:512], vj, pt[:, 0:N], j == 0, j == nj - 1, [pt] + v_reads, [numb])
                    if denb is not None:
                        mm(denb[:, c0:512], ones_ap, pt[:, 0:N], j == 0, j == nj - 1, [pt, ones_bf], [denb])
                    if j == min(2, nj - 1):
                        flush_pending()
                    if j == nj - 1 and on_done is not None:
                        on_done()
                tasks.append((score_fn, rest_fn))
            return tasks

        def w_out_residual(stack, mixT, w2d):
            wo = [sbt(stack, "wo%d" % i, [128, 8, 512], BF16) for i in range(2)]
            for ch in range(2):
                load_w(wo[ch], w2d, ch * 512, 512, 8)
            k = 0
            for t in range(NT):
                for ch in range(2):
                    ps = PS[k % 8]
                    k += 1
                    for c in range(8):
                        mm(ps[:, :], mixT[:, c, t * 128:(t + 1) * 128], wo[ch][:, c, :], c == 0, c == 7, [mixT, wo[ch]], [ps])
                    tt(h[:, t, ch * 512:(ch + 1) * 512], ps[:, :], h[:, t, ch * 512:(ch + 1) * 512], ALU.add, [ps, hB[t]], [hB[t]])

        def even_mixer(li):
            i2 = li // 2
            w_in = ewin_d[i2]
            with ExitStack() as sm:
                mixT = sbt(sm, "mixT", [128, 8, S], BF16)
                norm_transpose(i2, list(range(NT)))
                if li == 0:
                    emit_table_broadcasts()
                ckpt("norm")
                with ExitStack() as sa:
                    QTz = [sbt(sa, "QTz%d" % i, [128, S], BF16) for i in range(2)]
                    KTz = [sbt(sa, "KTz%d" % i, [128, S], BF16) for i in range(2)]
                    VP = sbt(sa, "VP", [128, NT, 2, 128], BF16)
                    QTf = [sbt(sa, "QTf%d" % i, [128, 512], F32) for i in range(2)]
                    ksum = sbt(sa, "ksum", [128, 8], F32)
                    gm = sbt(sa, "gm", [128, 256], F32)
                    top = sbt(sa, "top", [128, 256], F32)
                    pen = sbt(sa, "pen", [128, 256], F32)
                    Gt = [sbt(sa, "Gt%d" % i, [128, GW], BF16) for i in range(2)]
                    wq = [sbt(sa, "wq%d" % i, [128, 8, 128], BF16) for i in range(2)]
                    wk = [sbt(sa, "wk%d" % i, [128, 8, 128], BF16) for i in range(2)]
                    wv = [sbt(sa, "wv%d" % i, [128, 8, 128], BF16) for i in range(2)]
                    pexp = [sbt(sa, "pexp%d" % i, [128, 512], BF16) for i in range(3)]
                    ptb = [sbt(sa, "ptb%d" % i, [128, 512], BF16) for i in range(3)]
                    rd = [sbt(sa, "rd%d" % i, [128, 512], F32) for i in range(2)]
                    o0 = sbt(sa, "o0", [128, 512], F32)
                    o1 = sbt(sa, "o1", [128, 512], F32)
                    sq = sbt(sa, "sq", [128, 512], F32)

                    memset(VP[:, :, :, :].rearrange("p a b c -> p (a b c)"), 1.0, [VP], eng="dve")
                    for i_ in range(2):
                        memset(QTz[i_][:, :], 0.0, [QTz[i_]], eng="dve")
                        memset(KTz[i_][:, :], 0.0, [KTz[i_]], eng="dve")
                    dma("pool", KTz[0][64:72, :], c_koh, [KTz[0]], [KTz[0]])
                    dma("pool", KTz[1][0:8, :], c_koh, [KTz[1]], [KTz[1]])
                    units = [("moba", fc) for fc in range(4)] + [("diff", hd) for hd in range(4)]

                    def unit_cols(u):
                        kind, idx = u
                        if kind == "moba":
                            return idx * 128, 512 + idx * 128, 1024 + idx * 128
                        return 1536 + idx * 128, 2048 + idx * 128, 2560 + idx * 128

                    def load_unit_w(ui):
                        qc, kc_, vc = unit_cols(units[ui])
                        load_w(wq[ui % 2], w_in, qc, 128, 8)
                        load_w(wk[ui % 2], w_in, kc_, 128, 8)
                        load_w(wv[ui % 2], w_in, vc, 128, 8)

                    load_unit_w(0)
                    gslot = 0
                    moba_rot = {"i": 0}
                    for ui, u in enumerate(units):
                        kind, idx = u
                        if ui + 1 < len(units):
                            load_unit_w(ui + 1)
                        Wq, Wk, Wv = wq[ui % 2], wk[ui % 2], wv[ui % 2]

                        if kind == "diff" and idx == 0:
                            memset(QTz[0][64:128, :], 0.0, [QTz[0]], eng="dve")
                            memset(QTz[1][0:64, :], 0.0, [QTz[1]], eng="dve")

                        def evac_k(tg, ps):
                            cp(KTz[0][0:64, tg * 512:(tg + 1) * 512], ps[0:64, :], [ps], [KTz[0]])
                            cp(KTz[1][64:128, tg * 512:(tg + 1) * 512], ps[64:128, :], [ps], [KTz[1]])
                            if kind == "moba":
                                P.op("dve", lambda e: e.reduce_sum(
                                    out=ksum[:, 2 * tg:2 * tg + 2],
                                    in_=ps[:, :].rearrange("p (b k) -> p b k", k=256), axis=AX.X), [ps], [ksum])
                        proj_fm(Wk, 0, evac_k)
                        ckpt("u%dk" % ui)
                        ckpt("u%dv" % ui)

                        def evac_q(tg, ps):
                            ts(QTz[0][0:64, tg * 512:(tg + 1) * 512], ps[0:64, :], 0.125, None, ALU.mult, None, [ps], [QTz[0]])
                            ts(QTz[1][64:128, tg * 512:(tg + 1) * 512], ps[64:128, :], 0.125, None, ALU.mult, None, [ps], [QTz[1]])
                            if kind == "moba":
                                qf = QTf[tg % 2]
                                ts(qf[:, :], ps[:, :], 0.125, None, ALU.mult, None, [ps], [qf])
                                for hh in range(2):
                                    for tt_ in range(4):
                                        t = tg * 4 + tt_
                                        col = (hh * 16 + t) * 8
                                        mm(PS[7][:, col:col + 8], qf[hh * 64:(hh + 1) * 64, tt_ * 128:(tt_ + 1) * 128],
                                           ksum[hh * 64:(hh + 1) * 64, 0:8], True, True, [qf, ksum], [PS[7]])
                        proj_fm(Wq, 0, evac_q)
                        ckpt("u%dq" % ui)
                        for tq in range(4):
                            ps = PS[3 + tq % 2]
                            for tt_ in range(4):
                                t = tq * 4 + tt_
                                for kc in range(8):
                                    mm(ps[:, tt_ * 128:(tt_ + 1) * 128], hnT[:, kc, t * 128:(t + 1) * 128], Wv[:, kc, :],
                                       kc == 0, kc == 7, [hnT, Wv], [ps])
                            psv = ps[:, :].rearrange("p (a b) -> p a b", b=128)
                            if kind == "moba":
                                act(VP[:, tq * 4:(tq + 1) * 4, 0, 0:64], psv[:, :, 0:64], AF.Copy, [ps], [VP])
                                act(VP[:, tq * 4:(tq + 1) * 4, 1, 64:128], psv[:, :, 64:128], AF.Copy, [ps], [VP])
                            else:
                                act(VP[:, tq * 4:(tq + 1) * 4, 0, :], psv, AF.Copy, [ps], [VP])


                        if kind == "moba":
                            tt(gm[:, :], PS[7][:, 0:256], gmask[:, :], ALU.add, [PS[7], gmask], [gm])
                            for q_ in range(32):
                                P.op("dve", lambda e, q_=q_: e.max(out=top[:, q_ * 8:(q_ + 1) * 8], in_=gm[:, q_ * 8:(q_ + 1) * 8]),
                                     [gm], [top])
                            thr = top[:, :].rearrange("p (a b) -> p a b", b=8)[:, :, 2:3].to_broadcast([128, 32, 8])
                            tt(pen[:, :].rearrange("p (a b) -> p a b", b=8), gm[:, :].rearrange("p (a b) -> p a b", b=8), thr,
                               ALU.is_ge, [gm, top], [pen])
                            ckpt("u%dsel" % ui)
                            ts(pen[:, :], pen[:, :], -1.0, -NEGBIG, ALU.add, ALU.mult, [pen], [pen])
                            tt(pen[:, :], pen[:, :], omask[:, :], ALU.mult, [pen, omask], [pen])
                            for hh in range(2):
                                for grp in range(4):
                                    ps = PS[5 + grp % 2]
                                    for tt_ in range(4):
                                        t = grp * 4 + tt_
                                        col = (hh * 16 + t) * 8
                                        trp(ps[0:8, tt_ * 128:(tt_ + 1) * 128], pen[:, col:col + 8], idf[:, :], [pen, idf], [ps])
                                    prow = 64 if hh == 0 else 0
                                    cp(QTz[hh][prow:prow + 8, grp * 512:(grp + 1) * 512], ps[0:8, :], [ps], [QTz[hh]])

                        ckpt("u%dproj" % ui)
                        utasks = []
                        if kind == "moba":
                            for hh in range(2):
                                head = idx * 2 + hh
                                G = Gt[gslot % 2]
                                gslot += 1
                                load_G(G, head)
                                lo, hi = hh * 64, (hh + 1) * 64
                                dlo, dhi = (1 - hh) * 64, (2 - hh) * 64
                                for g in range(4):
                                    numb = PS[4 + (moba_rot["i"] % 4)]
                                    moba_rot["i"] += 1

                                    def epi(g=g, numb=numb, lo=lo, hi=hi, dlo=dlo, dhi=dhi, idx=idx):
                                        r_ = rd[g % 2]
                                        recip(r_[lo:hi, :], numb[dlo:dhi, :], [numb], [r_])
                                        tt(mixT[lo:hi, idx, g * 512:(g + 1) * 512], numb[lo:hi, :], r_[lo:hi, :], ALU.mult,
                                           [numb, r_], [mixT])
                                    utasks += attn_tasks(
                                        g,
                                        lambda j, hh=hh: KTz[hh][:, j * 128:(j + 1) * 128],
                                        lambda g_, c0, N, hh=hh: QTz[hh][:, g_ * 512 + c0:g_ * 512 + c0 + N],
                                        None,
                                        G, lambda j, hh=hh: VP[:, j, hh, :], None, numb, None, pexp, ptb, 1.0,
                                        [KTz[hh], QTz[hh]], [VP], on_done=lambda epi=epi: pending.append(epi))
                        else:
                            head = 8 + idx
                            G = Gt[gslot % 2]
                            gslot += 1
                            load_G(G, head)
                            for g in range(4):
                                for c in range(2):
                                    numb, denb = (PS[4], PS[5]) if c == 0 else (PS[6], PS[7])

                                    def done(g=g, idx=idx):
                                        act(rd[0][:, :], PS[5][:, :], AF.Ln, [PS[5]], [rd[0]])
                                        cp(o0[:, :], PS[4][:, :], [PS[4]], [o0])
                                        act(rd[1][:, :], PS[7][:, :], AF.Ln, [PS[7]], [rd[1]])
                                        cp(o1[:, :], PS[6][:, :], [PS[6]], [o1])

                                        def epi(g=g, idx=idx):
                                            act(rd[0][:, :], rd[0][:, :], AF.Exp, [rd[0]], [rd[0]], scale=-1.0)
                                            act(rd[1][:, :], rd[1][:, :], AF.Exp, [rd[1]], [rd[1]], scale=-1.0)
                                            tt(o0[:, :], o0[:, :], rd[0][:, :], ALU.mult, [o0, rd[0]], [o0])
                                            tt(o1[:, :], o1[:, :], rd[1][:, :], ALU.mult, [o1, rd[1]], [o1])
                                            stt(o0[:, :], o1[:, :], small[:, 8 + i2:9 + i2], o0[:, :], ALU.mult, ALU.add, [o1, o0, small], [o0])
                                            act(sq[:, :], o0[:, :], AF.Square, [o0], [sq])
                                            ssb = PS[7]
                                            mm(ssb[:, :], ones_f[:, :], sq[:, :], True, True, [ones_f, sq], [ssb])
                                            act(sq[:, :], ssb[:, :], AF.Ln, [ssb], [sq], bias=EPS, scale=1.0 / 128)
                                            act(sq[:, :], sq[:, :], AF.Exp, [sq], [sq], scale=-0.5)
                                            stt(mixT[:, 4 + idx, g * 512:(g + 1) * 512], o0[:, :], small[:, 12 + i2:13 + i2], sq[:, :],
                                                ALU.mult, ALU.mult, [o0, small, sq], [mixT])
                                        pending.append(epi)
                                    utasks += attn_tasks(
                                        g,
                                        lambda j, c=c: KTz[c][:, j * 128:(j + 1) * 128],
                                        lambda g_, c0, N, c=c: QTz[c][:, g_ * 512 + c0:g_ * 512 + c0 + N],
                                        None, G, lambda j: VP[:, j, 0, :], ones_bf[:, :], numb, denb, pexp, ptb, 1.0,
                                        [KTz[c], QTz[c]], [VP], on_done=(done if c == 1 else None))
                        run_pipeline(utasks, 3)
                        flush_pending()
                        ckpt("u%d" % ui)
                P.barrier()
                with ExitStack() as sw:
                    w_out_residual(sw, mixT, ewout_d[i2])
                    norm_body(sw, 4 + li, list(range(NT)))
            P.barrier()

        def conv_ffn(li):
            w_in = fwin_d[li]
            w_out = fwout_d[li]
            with ExitStack() as sf:
                actT = sbt(sf, "actT", [128, NCC, 1024], BF16)
                wug = [sbt(sf, "wug%d" % i, [128, 8, 256], BF16) for i in range(3)]
                wo2 = [sbt(sf, "wo2_%d" % i, [128, NCC, 256], BF16) for i in range(2)]
                graw = [sbt(sf, "graw%d" % i, [128, 1040], F32) for i in range(2)]
                A = [sbt(sf, "A%d" % i, [128, 512], F32) for i in range(2)]
                Ag = [sbt(sf, "Ag%d" % i, [128, 512], F32) for i in range(2)]
                gtail = sbt(sf, "gtail", [128, NCC, 2], F32)

                def load_pair(slot, cc):
                    src_u = w_in[:, cc * 128:(cc + 1) * 128].rearrange("(kc p) n -> p kc n", p=128)
                    src_g = w_in[:, DFF + cc * 128:DFF + (cc + 1) * 128].rearrange("(kc p) n -> p kc n", p=128)
                    dma("pool", wug[slot][:, :, 0:128], src_u, [], [wug[slot]])
                    dma("pool", wug[slot][:, :, 128:256], src_g, [], [wug[slot]])

                k = 0
                for half in range(2):
                    tiles = list(range(half * 8, half * 8 + 8))
                    load_pair(0, 0)
                    load_pair(1, 1)
                    for cc in range(NCC):
                        if cc + 2 < NCC:
                            load_pair((cc + 2) % 3, cc + 2)
                        W = wug[cc % 3]
                        gr = graw[cc % 2]
                        if half == 0:
                            memset(gr[:, 14:16], 0.0, [gr], eng="dve")
                        else:
                            cp(gr[:, 14:16], gtail[:, cc, :], [gtail], [gr])
                        w0 = convw[:, (li * 3 + 0) * NCC + cc:(li * 3 + 0) * NCC + cc + 1]
                        w1 = convw[:, (li * 3 + 1) * NCC + cc:(li * 3 + 1) * NCC + cc + 1]
                        w2 = convw[:, (li * 3 + 2) * NCC + cc:(li * 3 + 2) * NCC + cc + 1]
                        bb = convw[:, (12 + li) * NCC + cc:(12 + li) * NCC + cc + 1]
                        for tgi in range(2):
                            tok0 = half * 1024 + tgi * 512
                            pu = PS[(k * 2) % 8]
                            pg = PS[(k * 2 + 1) % 8]
                            k += 1
                            for kc in range(8):
                                mm(pg[:, :], W[:, kc, 128:256], hnT[:, kc, tok0:tok0 + 512], kc == 0, kc == 7, [W, hnT], [pg])
                            for kc in range(8):
                                mm(pu[:, :], W[:, kc, 0:128], hnT[:, kc, tok0:tok0 + 512], kc == 0, kc == 7, [W, hnT], [pu])
                            a = A[tgi]
                            ag = Ag[tgi]
                            act(gr[:, 16 + tgi * 512:16 + (tgi + 1) * 512], pg[:, :], AF.Copy, [pg], [gr])
                            ts(a[:, :], gr[:, 16 + tgi * 512:16 + (tgi + 1) * 512], w2, bb, ALU.mult, ALU.add, [gr, convw], [a])
                            stt(a[:, :], gr[:, 15 + tgi * 512:15 + (tgi + 1) * 512], w1, a[:, :], ALU.mult, ALU.add, [gr, a, convw], [a], eng=FMA_ENG)
                            stt(a[:, :], gr[:, 14 + tgi * 512:14 + (tgi + 1) * 512], w0, a[:, :], ALU.mult, ALU.add, [gr, a, convw], [a], eng=FMA_ENG)
                            act(ag[:, :], a[:, :], AF.Gelu, [a], [ag])
                            tt(actT[:, cc, tgi * 512:(tgi + 1) * 512], ag[:, :], pu[:, :], ALU.mult, [ag, pu], [actT])
                        if half == 0:
                            cp(gtail[:, cc, :], gr[:, 1038:1040], [gr], [gtail])
                    load_cols = lambda slot, cq: dma(
                        "pool", wo2[slot][:, :, :],
                        w_out[:, cq * 256:(cq + 1) * 256].rearrange("(cc p) n -> p cc n", p=128), [], [wo2[slot]])
                    load_cols(0, 0)
                    for cq in range(4):
                        if cq + 1 < 4:
                            load_cols((cq + 1) % 2, cq + 1)
                        Wo = wo2[cq % 2]
                        for tl in range(8):
                            t = half * 8 + tl
                            ps = PS[k % 8]
                            k += 1
                            for cc in range(NCC):
                                mm(ps[:, 0:256], actT[:, cc, tl * 128:(tl + 1) * 128], Wo[:, cc, :], cc == 0, cc == NCC - 1,
                                   [actT, Wo], [ps])
                            tt(h[:, t, cq * 256:(cq + 1) * 256], ps[:, 0:256], h[:, t, cq * 256:(cq + 1) * 256], ALU.add,
                               [ps, hB[t]], [hB[t]])
            P.barrier()

        def odd_mixer(li):
            i2 = li // 2
            w_in = owin_d[i2]
            with ExitStack() as sm:
                mixT = sbt(sm, "mixTo", [128, 8, S], BF16)
                norm_transpose(2 + i2, list(range(NT)))
                with ExitStack() as sa:
                    QTz = [sbt(sa, "oQTz%d" % i, [128, S], BF16) for i in range(2)]
                    KTz = [sbt(sa, "oKTz%d" % i, [128, S], BF16) for i in range(2)]
                    for i_ in range(2):
                        memset(QTz[i_][:, :], 0.0, [QTz[i_]], eng="dve")
                        memset(KTz[i_][:, :], 0.0, [KTz[i_]], eng="dve")
                    VS = sbt(sa, "VS", [128, NT, 2, 128], BF16)
                    accn = sbt(sa, "accn", [128, S], F32)
                    accd = sbt(sa, "accd", [128, S], F32)
                    Gd = sbt(sa, "Gd", [128, 3, 2, 256], BF16)
                    wq = [sbt(sa, "owq%d" % i, [128, 8, 128], BF16) for i in range(2)]
                    wk = [sbt(sa, "owk%d" % i, [128, 8, 128], BF16) for i in range(2)]
                    wv = [sbt(sa, "owv%d" % i, [128, 8, 128], BF16) for i in range(2)]
                    pexp = [sbt(sa, "opexp%d" % i, [128, 512], BF16) for i in range(4)]
                    ptb = [sbt(sa, "optb%d" % i, [128, 512], BF16) for i in range(4)]
                    rd = [sbt(sa, "ord%d" % i, [128, 512], F32) for i in range(2)]

                    memset(VS[:, :, :, :].rearrange("p a b c -> p (a b c)"), 0.0, [VS], eng="dve")

                    def load_unit_w(ui):
                        load_w(wq[ui % 2], w_in, ui * 128, 128, 8)
                        load_w(wk[ui % 2], w_in, 512 + ui * 128, 128, 8)
                        load_w(wv[ui % 2], w_in, 1024 + ui * 128, 128, 8)

                    load_unit_w(0)
                    sr = 0
                    vr = {"i": 0}
                    for ui in range(4):
                        if ui + 1 < 4:
                            load_unit_w(ui + 1)
                        Wq, Wk, Wv = wq[ui % 2], wk[ui % 2], wv[ui % 2]
                        def evac_k(tg, ps):
                            cp(KTz[0][0:64, tg * 512:(tg + 1) * 512], ps[0:64, :], [ps], [KTz[0]])
                            cp(KTz[1][64:128, tg * 512:(tg + 1) * 512], ps[64:128, :], [ps], [KTz[1]])

                        def evac_q(tg, ps):
                            ts(QTz[0][0:64, tg * 512:(tg + 1) * 512], ps[0:64, :], 0.125, None, ALU.mult, None, [ps], [QTz[0]])
                            ts(QTz[1][64:128, tg * 512:(tg + 1) * 512], ps[64:128, :], 0.125, None, ALU.mult, None, [ps], [QTz[1]])
                        proj_fm(Wk, 0, evac_k)
                        proj_fm(Wq, 0, evac_q)
                        for hh in range(2):
                            for gi in range(3):
                                idx = (ui * 2 + hh) * 3 + gi
                                src = bass.AP(m_dil, idx * 128 * 512 + 127, [[511, 128], [1, 256]])
                                dma("sp", Gd[:, gi, hh, :], src, [B_mdil[idx]], [Gd])
                        first = True
                        for gi, dil in enumerate((1, 4, 16)):
                            L = S // dil
                            nblk = max(1, L // 128)
                            for r in range(dil):
                                for n0 in range(0, nblk, 4):
                                    nn = min(4, nblk - n0)
                                    ps = PS[(3, 0, 1, 2)[vr["i"] % 4]]
                                    vr["i"] += 1
                                    for n_ in range(nn):
                                        n = n0 + n_
                                        t0 = n * 128 * dil + r
                                        for kc in range(8):
                                            mm(ps[:, n_ * 128:(n_ + 1) * 128],
                                               hnT[:, kc, t0:t0 + 127 * dil + 1:dil], Wv[:, kc, :],
                                               kc == 0, kc == 7, [hnT, Wv], [ps])
                                    slot0 = r * nblk + n0
                                    psv = ps[:, 0:nn * 128].rearrange("p (a b) -> p a b", b=128)
                                    act(VS[:, slot0:slot0 + nn, 0, 0:64], psv[:, :, 0:64], AF.Copy, [ps], [VS])
                                    act(VS[:, slot0:slot0 + nn, 1, 64:128], psv[:, :, 64:128], AF.Copy, [ps], [VS])
                            ckpt("d%dv%d" % (ui, gi))
                            tasks = []
                            for r in range(dil):
                                for seg0 in range(0, nblk, 4):
                                    segn = min(4, nblk - seg0)
                                    W_ = segn * 128
                                    numb, denb = (PS[4], PS[5]) if sr % 2 == 0 else (PS[6], PS[7])
                                    sr += 1
                                    kbs = [kb for kb in range(seg0 - 1, seg0 + segn) if kb >= 0]
                                    for ki, kb in enumerate(kbs):
                                        q0 = max(kb, seg0)
                                        q1 = min(kb + 2, seg0 + segn)
                                        N = (q1 - q0) * 128
                                        xoff = (q0 - kb) * 128
                                        cpos = (q0 - seg0) * 128
                                        kt0 = kb * 128 * dil + r
                                        qt0 = q0 * 128 * dil + r
                                        for hh in range(2):
                                            st_ = {}

                                            def score_fn(N=N, kt0=kt0, qt0=qt0, st_=st_, hh=hh):
                                                sb = PS[rot["s"] % 4]
                                                rot["s"] += 1
                                                st_["sb"] = sb
                                                mm(sb[:, 0:N], KTz[hh][:, kt0:kt0 + 127 * dil + 1:dil],
                                                   QTz[hh][:, qt0:qt0 + (N - 1) * dil + 1:dil], True, True, [KTz[hh], QTz[hh]], [sb])

                                            def rest_fn(N=N, xoff=xoff, cpos=cpos, kb=kb, r=r, st_=st_, numb=numb, denb=denb, hh=hh,
                                                        first_k=(ki == 0 and hh == 0), last_k=(ki == len(kbs) - 1 and hh == 1),
                                                        W_=W_, seg0=seg0, first=first):
                                                sb = st_["sb"]
                                                pe_ = pexp[rot["p"] % 4]
                                                pt = ptb[rot["p"] % 4]
                                                rot["p"] += 1
                                                act(pe_[:, 0:N], sb[:, 0:N], AF.Exp, [sb], [pe_])
                                                tt(pt[:, 0:N], pe_[:, 0:N], Gd[:, gi, hh, xoff:xoff + N], ALU.mult, [pe_, Gd], [pt])
                                                mm(numb[:, cpos:cpos + N], VS[:, r * nblk + kb, hh, :], pt[:, 0:N], first_k, False,
                                                   [pt, VS], [numb])
                                                mm(denb[:, cpos:cpos + N], (onesA if hh == 0 else onesB)[:, :], pt[:, 0:N], first_k, False,
                                                   [pt, onesA, onesB], [denb])
                                                if last_k:
                                                    tq0 = seg0 * 128 * dil + r
                                                    dst_n = accn[:, tq0:tq0 + (W_ - 1) * dil + 1:dil]
                                                    dst_d = accd[:, tq0:tq0 + (W_ - 1) * dil + 1:dil]
                                                    if first:
                                                        cp(dst_n, numb[:, 0:W_], [numb], [accn])
                                                        cp(dst_d, denb[:, 0:W_], [denb], [accd])
                                                    else:
                                                        tt(dst_n, numb[:, 0:W_], dst_n, ALU.add, [numb, accn], [accn])
                                                        tt(dst_d, denb[:, 0:W_], dst_d, ALU.add, [denb, accd], [accd])
                                            tasks.append((score_fn, rest_fn))
                            run_pipeline(tasks, 3)
                            ckpt("d%db%d" % (ui, gi))
                            first = False
                        ckpt("d%dpre" % ui)
                        for g in range(4):
                            r_ = rd[g % 2]
                            recip(r_[:, :], accd[:, g * 512:(g + 1) * 512], [accd], [r_])
                            tt(mixT[:, ui, g * 512:(g + 1) * 512], accn[:, g * 512:(g + 1) * 512], r_[:, :], ALU.mult, [accn, r_], [mixT])
                P.barrier()
                with ExitStack() as sb_:
                    wuq = sbt(sb_, "wuq", [128, 2, 768], BF16)
                    wukv = sbt(sb_, "wukv", [128, 1, 1024], BF16)
                    cqT = sbt(sb_, "cqT", [128, 2, S], BF16)
                    ckvT = sbt(sb_, "ckvT", [128, 1, S], BF16)
                    krT = sbt(sb_, "krT", [128, S], BF16)
                    rcs = sbt(sb_, "rcs", [64, 2 * S], F32)
                    xs1 = [sbt(sb_, "xs1_%d" % i, [64, 512], F32) for i in range(2)]
                    xs2 = [sbt(sb_, "xs2_%d" % i, [64, 512], F32) for i in range(2)]
                    lst = sbt(sb_, "lst", [128, 96], F32)
                    rr = {"i": 0}

                    def rope(dst_ap, dst_T, ps, col0, tok0, n):
                        a = xs1[rr["i"] % 2]
                        b = xs2[rr["i"] % 2]
                        rr["i"] += 1
                        tt(b[0:32, 0:n], ps[32:64, col0:col0 + n], rcs[0:32, S + tok0:S + tok0 + n], ALU.mult, [ps, rcs], [b])
                        tt(b[32:64, 0:n], ps[0:32, col0:col0 + n], rcs[32:64, S + tok0:S + tok0 + n], ALU.mult, [ps, rcs], [b])
                        tt(a[0:64, 0:n], ps[0:64, col0:col0 + n], rcs[0:64, tok0:tok0 + n], ALU.mult, [ps, rcs], [a])
                        tt(dst_ap, a[0:64, 0:n], b[0:64, 0:n], ALU.add, [a, b], [dst_T])

                    memset(krT[64:128, :], 0.0, [krT], eng="dve")
                    load_w(wuq, wuq_d[i2], 0, 768, 2)
                    load_w(wukv, wukv_d[i2], 0, 1024, 1)
                    dma("sp", rcs[0:64, 0:S], c_ropec, [], [rcs])
                    dma("sp", rcs[0:64, S:2 * S], c_ropes, [], [rcs])
                    with ExitStack() as s1:
                        wl = sbt(s1, "wl", [128, 8, 448], BF16)
                        lat = [sbt(s1, "lat%d" % i, [128, 448], F32) for i in range(4)]
                        load_w(wl, w_in, 1536, 448, 8)
                        lB = [Buf("lst%d" % t) for t in range(NT)]
                        memset(lst[:, 0:32], 0.0, lB, eng="dve")
                        for t in range(NT):
                            ps = PS[t % 4]
                            la = lat[t % 4]
                            for kc in range(8):
                                mm(ps[:, 0:448], hnT[:, kc, t * 128:(t + 1) * 128], wl[:, kc, :], kc == 0, kc == 7, [hnT, wl], [ps])
                            act(la[:, 0:256], ps[:, 0:256], AF.Square, [ps], [la, lB[t]], accum=lst[:, t:t + 1])
                            act(la[:, 256:384], ps[:, 256:384], AF.Square, [ps], [la, lB[t]], accum=lst[:, 16 + t:17 + t])
                            act(lst[:, 32 + t:33 + t], lst[:, t:t + 1], AF.Ln, [lB[t]], [lB[t]], bias=EPS, scale=1.0 / 256)
                            act(lst[:, 32 + t:33 + t], lst[:, 32 + t:33 + t], AF.Exp, [lB[t]], [lB[t]], scale=-0.5)
                            act(lst[:, 48 + t:49 + t], lst[:, 16 + t:17 + t], AF.Ln, [lB[t]], [lB[t]], bias=EPS, scale=1.0 / 128)
                            act(lst[:, 48 + t:49 + t], lst[:, 48 + t:49 + t], AF.Exp, [lB[t]], [lB[t]], scale=-0.5)
                            act(la[:, 0:256], ps[:, 0:256], AF.Copy, [ps, lB[t]], [la], scale=lst[:, 32 + t:33 + t])
                            act(la[:, 256:384], ps[:, 256:384], AF.Copy, [ps, lB[t]], [la], scale=lst[:, 48 + t:49 + t])
                            act(la[:, 384:448], ps[:, 384:448], AF.Copy, [ps], [la])
                            pt_ = PS[4 + t % 4]
                            for c in range(3):
                                trp(pt_[:, c * 128:(c + 1) * 128], la[:, c * 128:(c + 1) * 128], idf[:, :], [la, idf], [pt_])
                            trp(pt_[0:64, 384:512], la[:, 384:448], idf[:, :], [la, idf], [pt_])
                            for c in range(2):
                                ts(cqT[:, c, t * 128:(t + 1) * 128], pt_[:, c * 128:(c + 1) * 128], small[:, 2 + 2 * i2 + c:3 + 2 * i2 + c],
                                   None, ALU.mult, None, [pt_, small], [cqT])
                            ts(ckvT[:, 0, t * 128:(t + 1) * 128], pt_[:, 256:384], small[:, 6 + i2:7 + i2], None, ALU.mult, None,
                               [pt_, small], [ckvT])
                            rope_dst = krT[0:64, t * 128:(t + 1) * 128]
                            rope2(rope_dst, krT, pt_, 384, t * 128, 128, rope)
                    P.barrier()
                    with ExitStack() as s2:
                        qnT = sbt(s2, "qnT", [128, S], BF16)
                        qrT = sbt(s2, "qrT", [128, S], BF16)
                        memset(qrT[64:128, :], 0.0, [qrT], eng="dve")
                        knT = sbt(s2, "knT", [128, S], BF16)
                        VM = sbt(s2, "VM", [128, NT, 128], BF16)
                        ptb = [sbt(s2, "mptb%d" % i, [128, 512], BF16) for i in range(4)]
                        rd = [sbt(s2, "mrd%d" % i, [128, 512], F32) for i in range(2)]
                        scale = 192.0 ** -0.5
                        for hd in range(4):
                            proj_fm(wuq, hd * 192, lambda tg, ps: cp(qnT[:, tg * 512:(tg + 1) * 512], ps[:, :], [ps], [qnT]),
                                    kchunks=2, src=cqT)
                            proj_fm(wuq, hd * 192 + 128,
                                    lambda tg, ps: rope2(qrT[0:64, tg * 512:(tg + 1) * 512], qrT, ps, 0, tg * 512, 512, rope),
                                    kchunks=2, src=cqT, M=64)
                            proj_fm(wukv, hd * 256, lambda tg, ps: cp(knT[:, tg * 512:(tg + 1) * 512], ps[:, :], [ps], [knT]),
                                    kchunks=1, src=ckvT)
                            for tq in range(4):
                                ps = PS[3 + tq % 2]
                                for tt_ in range(4):
                                    t = tq * 4 + tt_
                                    mm(ps[:, tt_ * 128:(tt_ + 1) * 128], ckvT[:, 0, t * 128:(t + 1) * 128],
                                       wukv[:, 0, hd * 256 + 128:hd * 256 + 256], True, True, [ckvT, wukv], [ps])
                                act(VM[:, tq * 4:(tq + 1) * 4, :], ps[:, :].rearrange("p (a b) -> p a b", b=128), AF.Copy, [ps], [VM])
                            mtasks = []
                            for g in range(4):
                                numb, denb = (PS[4], PS[5]) if g % 2 == 0 else (PS[6], PS[7])

                                def epi(g=g, numb=numb, denb=denb, hd=hd):
                                    r_ = rd[g % 2]
                                    recip(r_[:, :], denb[:, :], [denb], [r_])
                                    tt(mixT[:, 4 + hd, g * 512:(g + 1) * 512], numb[:, :], r_[:, :], ALU.mult, [numb, r_], [mixT])
                                mtasks += attn_tasks(
                                    g,
                                    lambda j: knT[:, j * 128:(j + 1) * 128],
                                    lambda g_, c0, N: qnT[:, g_ * 512 + c0:g_ * 512 + c0 + N],
                                    lambda j, g_, c0, N: (krT[:, j * 128:(j + 1) * 128],
                                                          qrT[:, g_ * 512 + c0:g_ * 512 + c0 + N]),
                                    None, lambda j: VM[:, j, :], ones_bf[:, :], numb, denb, None, ptb, scale,
                                    [knT, qnT, krT, qrT], [VM], mla=True, on_done=lambda epi=epi: pending.append(epi))
                            run_pipeline(mtasks, 3)
                            flush_pending()
                P.barrier()
                with ExitStack() as sw:
                    w_out_residual(sw, mixT, owout_d[i2])
                    norm_body(sw, 4 + li, list(range(NT)))
            P.barrier()

        try:
            ckpt("setup")
            for li in range(n_layers):
                if li % 2 == 0:
                    even_mixer(li)
                else:
                    odd_mixer(li)
                ckpt("mixer%d" % li)
                conv_ffn(li)
        except _Stop:
            pass
        P.stopped = False
        P.barrier()

        with ExitStack() as sfin:
            gfin = sbt(sfin, "gfin", [128, D], F32)
            yb = [sbt(sfin, "yb%d" % i, [128, D], F32) for i in range(2)]
            dma("sp", gfin[:, :], fin_d.partition_broadcast(128), [], [gfin])
            for t in range(NT):
                y = yb[t % 2]
                if debug_h:
                    cp(y[:, :], h[:, t, :], [hB[t]], [y])
                else:
                    act(y[:, :], h[:, t, :], AF.Square, [hB[t]], [y, stat], accum=stat[:, t:t + 1])
                    act(stat[:, 16 + t:17 + t], stat[:, t:t + 1], AF.Ln, [stat], [stat], bias=EPS, scale=1.0 / D)
                    act(stat[:, 32 + t:33 + t], stat[:, 16 + t:17 + t], AF.Exp, [stat], [stat], scale=-0.5)
                    stt(y[:, :], h[:, t, :], stat[:, 32 + t:33 + t], gfin[:, :], ALU.mult, ALU.mult, [hB[t], stat, gfin], [y])
                dma("sp", out_d[t * 128:(t + 1) * 128, :], y[:, :], [y], [B_out])
            P.wait_all("sp", [B_out])
            P.barrier()

        with nc.Block() as block:
            P.emit(block)
    return nc


_CACHE = {}


def kernel(**inputs):
    n = 8
    consts = _host_consts()
    if "nc" not in _CACHE:
        _CACHE["nc"] = build_program()
    nc = _CACHE["nc"]
    x = np.ascontiguousarray(np.asarray(inputs["x"], dtype=np.float32))
    shared = {k: np.ascontiguousarray(np.asarray(v, dtype=np.float32)) for k, v in inputs.items() if k != "x"}
    shared.update(consts)
    in_maps = []
    for c in range(n):
        m = dict(shared)
        m["x"] = x[c]
        in_maps.append(m)
    res = run_bass_kernel_spmd(nc, in_maps, core_ids=list(range(n)))
    out = np.stack([np.asarray(r["out"], dtype=np.float32) for r in res.results], axis=0)
    return out
```

```python
import math
from contextlib import ExitStack

import numpy as np
import concourse.bass as bass
import concourse.mybir as mybir
from concourse.bass_utils import run_bass_kernel_spmd

F32 = mybir.dt.float32
BF16 = mybir.dt.bfloat16
ALU = mybir.AluOpType
AF = mybir.ActivationFunctionType
AX = mybir.AxisListType

S = 2048
D = 1024
NT = 16
DEPTH = 4
DFF = 2816
NCC = 22
EPS = 1e-6
LW = 2560
GW = 2432
NEGBIG = -30000.0


class Buf:
    __slots__ = ("name", "w", "rs")

    def __init__(self, name):
        self.name = name
        self.w = None
        self.rs = []


class T:
    __slots__ = ("t", "b")

    def __init__(self, t, name):
        self.t = t
        self.b = Buf(name)

    def __getitem__(self, k):
        return self.t[k]


class Prog:
    ENGS = ("pe", "act", "dve", "pool", "sp")

    def __init__(self, nc, n_dma_sems=8):
        self.nc = nc
        self.ops = {e: [] for e in self.ENGS}
        self.cnt = {e: 0 for e in self.ENGS}
        self.waited = {e: {} for e in self.ENGS}
        self.sems = {}
        self.n_dma_sems = n_dma_sems
        self.dma_i = {"sp": 0, "pool": 0, "act": 0}
        self.dma_last = {}
        self.stopped = False

    def alloc_sems(self, stack):
        for e in self.ENGS:
            self.sems[e] = stack.enter_context(self.nc.semaphore("s_" + e))
        for q in ("sp", "pool", "act"):
            for i in range(self.n_dma_sems):
                self.sems[("dma", q, i)] = stack.enter_context(self.nc.semaphore(f"d_{q}_{i}"))

    def _need(self, eng, waits, ev, kind):
        if ev is None:
            return
        semkey, val, src = ev
        if src == eng and semkey == eng:
            if eng == "pe" or kind == "war":
                return
        if self.waited[eng].get(semkey, 0) >= val:
            return
        if waits.get(semkey, 0) < val:
            waits[semkey] = val

    def _deps(self, eng, reads, writes):
        waits = {}
        for b in reads:
            self._need(eng, waits, b.w, "raw")
        for b in writes:
            self._need(eng, waits, b.w, "waw")
            for r in b.rs:
                self._need(eng, waits, r, "war")
        for k, v in waits.items():
            self.waited[eng][k] = v
        return list(waits.items())

    def op(self, eng, fn, reads=(), writes=()):
        if self.stopped:
            return None
        reads = [x.b if isinstance(x, T) else x for x in reads]
        writes = [x.b if isinstance(x, T) else x for x in writes]
        waits = self._deps(eng, reads, writes)
        self.cnt[eng] += 1
        ev = (eng, self.cnt[eng], eng)
        self.ops[eng].append((waits, fn, (eng, 1)))
        for b in reads:
            b.rs.append(ev)
            if len(b.rs) > 64:
                b.rs = self._prune(b.rs)
        for b in writes:
            b.w = ev
            b.rs = []
        return ev

    @staticmethod
    def _prune(rs):
        best = {}
        for (k, v, s) in rs:
            if k not in best or best[k][1] < v:
                best[k] = (k, v, s)
        return list(best.values())

    def dma(self, q, fn, reads=(), writes=()):
        if self.stopped:
            return None
        reads = [x.b if isinstance(x, T) else x for x in reads]
        writes = [x.b if isinstance(x, T) else x for x in writes]
        waits = self._deps(q, reads, writes)
        i = self.dma_i[q]
        self.dma_i[q] += 1
        slot = i % self.n_dma_sems
        semkey = ("dma", q, slot)
        val = 16 * (i // self.n_dma_sems + 1)
        if val > 16 and self.waited[q].get(semkey, 0) < val - 16:
            waits.append((semkey, val - 16))
            self.waited[q][semkey] = val - 16
        ev = (semkey, val, q + "_dma")
        self.dma_last[semkey] = val
        self.ops[q].append((waits, fn, (semkey, 16)))
        for b in reads:
            b.rs.append(ev)
        for b in writes:
            b.w = ev
            b.rs = []
        return ev

    def barrier(self):
        if self.stopped:
            return
        for e in self.ENGS:
            waits = []
            for o in self.ENGS:
                if o == e:
                    continue
                v = self.cnt[o]
                if v > 0 and self.waited[e].get(o, 0) < v:
                    waits.append((o, v))
                    self.waited[e][o] = v
            for k, v in self.dma_last.items():
                if self.waited[e].get(k, 0) < v:
                    waits.append((k, v))
                    self.waited[e][k] = v
            if waits:
                self.ops[e].append((waits, None, None))

    def wait_all(self, eng, bufs):
        bufs = [x.b if isinstance(x, T) else x for x in bufs]
        waits = self._deps(eng, bufs, ())
        self.ops[eng].append((waits, None, None))

    def emit(self, block):
        sems = self.sems

        def run(e, lst):
            for waits, fn, inc in lst:
                for k, v in waits:
                    e.wait_ge(sems[k], v)
                if fn is not None:
                    fn(e).then_inc(sems[inc[0]], inc[1])

        @block.tensor
        def _(e):
            run(e, self.ops["pe"])

        @block.scalar
        def _(e):
            run(e, self.ops["act"])

        @block.vector
        def _(e):
            run(e, self.ops["dve"])

        @block.gpsimd
        def _(e):
            run(e, self.ops["pool"])

        @block.sync
        def _(e):
            run(e, self.ops["sp"])


def _np_bucket(dist):
    n = np.maximum(dist, 0)
    nf = np.maximum(n, 1).astype(np.float32)
    large = 16 + (np.log(nf / np.float32(16)) / np.float32(math.log(64)) * np.float32(16)).astype(np.int32)
    large = np.minimum(large, 31)
    return np.where(n < 16, n, large)


def _host_consts():
    c = {}
    oh = np.zeros((33, LW), np.float32)
    d = np.arange(LW) - 511
    b = _np_bucket(d)
    for i in range(LW):
        if d[i] < 0:
            oh[32, i] = 1.0
        else:
            oh[b[i], i] = 1.0
    c["c_oh_main"] = oh
    ohd = np.zeros((33, 3 * 512), np.float32)
    for gi, dil in enumerate((1, 4, 16)):
        for i in range(512):
            s = i - 127
            if 0 <= s <= 128:
                ohd[_np_bucket(np.array([s * dil]))[0], gi * 512 + i] = 1.0
            else:
                ohd[32, gi * 512 + i] = 1.0
    c["c_oh_dil"] = ohd
    koh = np.zeros((8, S), np.float32)
    for n in range(8):
        koh[n, n * 256:(n + 1) * 256] = 1.0
    c["c_koh"] = koh
    gm = np.zeros((2, 16, 8), np.float32)
    om = np.ones((2, 16, 8), np.float32)
    for t in range(16):
        bq = t // 2
        gm[:, t, bq:] = -1e30
        om[:, t, bq] = 0.0
    c["c_gmask"] = np.broadcast_to(gm.reshape(1, 256), (128, 256)).copy()
    c["c_omask"] = np.broadcast_to(om.reshape(1, 256), (128, 256)).copy()
    half = 32
    freq = (np.float32(10000.0) ** (-np.arange(half, dtype=np.float32) / np.float32(half))).astype(np.float32)
    ang = np.arange(S, dtype=np.float32)[None, :] * freq[:, None]
    cs, sn = np.cos(ang).astype(np.float32), np.sin(ang).astype(np.float32)
    c["c_ropec"] = np.concatenate([cs, cs], 0)
    c["c_ropes"] = np.concatenate([-sn, sn], 0)
    tri = (np.arange(128)[None, :] >= np.arange(128)[:, None]).astype(np.float32)
    c["c_tri"] = tri
    return c


import os as _os
DIL_SPLIT = bool(_os.environ.get("DIL_SPLIT"))
FMA_ENG = "dve"


class _Stop(Exception):
    pass


def build_program(n_layers=DEPTH, debug_h=False, stop_at=None):
    nc = bass.Bass("TRN2", target_bir_lowering=False)

    stop_state = {"P": None}

    def ckpt(name):
        if stop_at == name:
            stop_state["P"].stopped = True

    dram_in = {}

    def din(name, shape):
        dram_in[name] = nc.dram_tensor(name, list(shape), F32, kind="ExternalInput").ap()
        return dram_in[name]

    x_d = din("x", [S, D])
    rel_d = din("rel_bias", [32, 20])
    en1_d = din("even_norm1", [2, D])
    ewin_d = din("even_w_in", [2, D, 3072])
    dlam_d = din("diff_lambda", [2, 4, 64])
    dsub_d = din("diff_subln", [2, 128])
    ewout_d = din("even_w_out", [2, D, D])
    on1_d = din("odd_norm1", [2, D])
    owin_d = din("odd_w_in", [2, D, 1984])
    qn_d = din("mla_q_norm", [2, 256])
    wuq_d = din("mla_w_uq", [2, 256, 768])
    kvn_d = din("mla_kv_norm", [2, 128])
    wukv_d = din("mla_w_ukv", [2, 128, 1024])
    owout_d = din("odd_w_out", [2, D, D])
    fn_d = din("ffn_norm", [4, D])
    fwin_d = din("ffn_w_in", [4, D, 2 * DFF])
    fcw_d = din("ffn_conv_w", [4, 3, DFF])
    fcb_d = din("ffn_conv_b", [4, DFF])
    fwout_d = din("ffn_w_out", [4, DFF, D])
    fin_d = din("final_norm", [D])
    c_oh_main = din("c_oh_main", [33, LW])
    c_oh_dil = din("c_oh_dil", [33, 1536])
    c_koh = din("c_koh", [8, S])
    c_gmask = din("c_gmask", [128, 256])
    c_omask = din("c_omask", [128, 256])
    c_ropec = din("c_ropec", [64, S])
    c_ropes = din("c_ropes", [64, S])
    c_tri = din("c_tri", [128, 128])
    out_d = nc.dram_tensor("out", [S, D], F32, kind="ExternalOutput").ap()
    m_main = nc.dram_tensor("m_main", [12 * 128, LW], BF16)
    m_dil = nc.dram_tensor("m_dil", [24 * 128, 512], BF16)
    B_mmain = [Buf("mmain%d" % i) for i in range(12)]
    B_mdil = [Buf("mdil%d" % i) for i in range(24)]
    B_out = Buf("out")

    with ExitStack() as st:
        P = Prog(nc)
        stop_state["P"] = P
        P.alloc_sems(st)
        st.enter_context(nc.allow_non_contiguous_dma("small strided parameter loads"))

        uniq = {"i": 0}

        def sbt(stack, name, shape, dt):
            uniq["i"] += 1
            name = "%s_%d" % (name, uniq["i"])
            return T(stack.enter_context(nc.sbuf_tensor(name, list(shape), dt)), name)

        def mm(out, lhsT, rhs, start, stop, r, w):
            P.op("pe", lambda e: e.matmul(out, lhsT=lhsT, rhs=rhs, start=start, stop=stop,
                                          skip_group_check=True), r, w)

        def trp(out, in_, ident, r, w):
            P.op("pe", lambda e: e.transpose(out=out, in_=in_, identity=ident), r, w)

        def act(out, in_, func, r, w, bias=None, scale=None, accum=None):
            kw = {}
            if bias is not None:
                kw["bias"] = bias
            if scale is not None:
                kw["scale"] = scale
            if accum is not None:
                kw["accum_out"] = accum
            P.op("act", lambda e: e.activation(out=out, in_=in_, func=func, **kw), r, w)

        def tt(out, in0, in1, op, r, w, eng="dve"):
            P.op(eng, lambda e: e.tensor_tensor(out=out, in0=in0, in1=in1, op=op), r, w)

        def ts(out, in0, s1, s2, op0, op1, r, w, eng="dve"):
            if s2 is None:
                P.op(eng, lambda e: e.tensor_scalar(out=out, in0=in0, scalar1=s1, scalar2=None, op0=op0), r, w)
            else:
                P.op(eng, lambda e: e.tensor_scalar(out=out, in0=in0, scalar1=s1, scalar2=s2, op0=op0, op1=op1), r, w)

        def stt(out, in0, scalar, in1, op0, op1, r, w, eng="dve"):
            P.op(eng, lambda e: e.scalar_tensor_tensor(out=out, in0=in0, scalar=scalar, in1=in1, op0=op0, op1=op1), r, w)

        def cp(out, in_, r, w, eng="dve"):
            P.op(eng, lambda e: e.tensor_copy(out=out, in_=in_), r, w)

        def recip(out, in_, r, w):
            act(out, in_, AF.Ln, r, w)
            act(out, out, AF.Exp, list(w), w, scale=-1.0)

        def memset(ap, val, w, eng="pool"):
            P.op(eng, lambda e: e.memset(ap, val), (), w)

        def dma(q, out, in_, r, w):
            P.dma(q, lambda e: e.dma_start(out=out, in_=in_), r, w)

        def run_pipeline(tasks, depth):
            n = len(tasks)
            for i in range(min(depth, n)):
                tasks[i][0]()
            for i in range(n):
                if i + depth < n:
                    tasks[i + depth][0]()
                tasks[i][1]()

        def rope2(dst_ap, dst_T, ps, col0, tok0, n, rope_fn):
            rope_fn(dst_ap, dst_T, ps, col0, tok0, n)

        h = sbt(st, "h", [128, NT, D], F32)
        hB = [Buf("h%d" % t) for t in range(NT)]
        hnT = sbt(st, "hnT", [128, 8, S], BF16)
        idf = sbt(st, "idf", [128, 128], F32)
        idb = sbt(st, "idb", [128, 128], BF16)
        ones_bf = sbt(st, "ones_bf", [128, 128], BF16)
        onesA = sbt(st, "onesA", [128, 128], BF16)
        onesB = sbt(st, "onesB", [128, 128], BF16)
        ones_f = sbt(st, "ones_f", [128, 128], F32)
        tri_bf = sbt(st, "tri_bf", [128, 128], BF16)
        gmask = sbt(st, "gmask", [128, 256], F32)
        omask = sbt(st, "omask", [128, 256], F32)
        convw = sbt(st, "convw", [128, 16 * NCC], F32)
        small = sbt(st, "small", [128, 64], F32)
        lamb = sbt(st, "lamb", [128, 2 * 256], F32)
        stat = sbt(st, "stat", [128, 64], F32)
        PS = [T(st.enter_context(nc.psum_tensor("ps%d" % i, [128, 512], F32)), "ps%d" % i) for i in range(8)]

        memset(idf[:], 0.0, [idf])
        P.op("pool", lambda e: e.affine_select(out=idf[:], in_=idf[:], pattern=[[-1, 128]],
                                               compare_op=ALU.not_equal, fill=1.0, base=0,
                                               channel_multiplier=1), [idf], [idf])
        cp(idb[:, :], idf[:, :], [idf], [idb])
        memset(ones_bf[:], 1.0, [ones_bf])
        memset(ones_f[:], 1.0, [ones_f])
        memset(onesA[:], 0.0, [onesA])
        memset(onesA[:, 0:64], 1.0, [onesA])
        memset(onesB[:], 0.0, [onesB])
        memset(onesB[:, 64:128], 1.0, [onesB])
        dma("pool", tri_bf[:], c_tri, [], [tri_bf])
        dma("sp", gmask[:], c_gmask, [], [gmask])
        dma("sp", omask[:], c_omask, [], [omask])
        for t in range(NT):
            dma("act", h[:, t, :], x_d[t * 128:(t + 1) * 128, :], [], [hB[t]])
        with ExitStack() as sc:
            raws = [sbt(sc, "craw%d" % i, [128, 128], F32) for i in range(4)]
            for i_ in range(4):
                memset(raws[i_][:, :], 0.0, [raws[i_]], eng="dve")
            vecs = [fcw_d[li, k] for li in range(4) for k in range(3)] + [fcb_d[li] for li in range(4)]
            for v, vec in enumerate(vecs):
                tl, j = v // 5, v % 5
                dma("sp", raws[tl][j * NCC:(j + 1) * NCC, :], vec.rearrange("(c p) -> c p", p=128), [raws[tl]], [raws[tl]])
            for tl in range(4):
                nv = min(5, 16 - tl * 5)
                ps = PS[tl]
                trp(ps[:, 0:128], raws[tl][:, :], idf[:, :], [raws[tl], idf], [ps])
                cp(convw[:, tl * 5 * NCC:(tl * 5 + nv) * NCC], ps[:, 0:nv * NCC], [ps], [convw])
            P.barrier()
        for i in range(2):
            dma("sp", small[:, i:i + 1], dsub_d[i].rearrange("(c p) -> p c", p=128), [], [small])
            dma("sp", small[:, 2 + 2 * i:4 + 2 * i], qn_d[i].rearrange("(c p) -> p c", p=128), [], [small])
            dma("sp", small[:, 6 + i:7 + i], kvn_d[i].rearrange("(c p) -> p c", p=128), [], [small])
            dma("sp", lamb[:, i * 256:(i + 1) * 256],
                dlam_d[i].rearrange("a b -> (a b)").partition_broadcast(128), [], [lamb])
        for i in range(2):
            layer = 2 * i
            lam_init = 0.8 - 0.6 * math.exp(-0.3 * layer)
            lp = lamb[:, i * 256:(i + 1) * 256]
            tt(lamb[:, i * 256:i * 256 + 64], lamb[:, i * 256:i * 256 + 64], lamb[:, i * 256 + 64:i * 256 + 128], ALU.mult, [lamb], [lamb])
            tt(lamb[:, i * 256 + 128:i * 256 + 192], lamb[:, i * 256 + 128:i * 256 + 192], lamb[:, i * 256 + 192:i * 256 + 256], ALU.mult, [lamb], [lamb])
            P.op("dve", lambda e, i=i: e.reduce_sum(out=small[:, 16 + 2 * i:17 + 2 * i], in_=lamb[:, i * 256:i * 256 + 64], axis=AX.X), [lamb], [small])
            P.op("dve", lambda e, i=i: e.reduce_sum(out=small[:, 17 + 2 * i:18 + 2 * i], in_=lamb[:, i * 256 + 128:i * 256 + 192], axis=AX.X), [lamb], [small])
            act(small[:, 20 + 2 * i:22 + 2 * i], small[:, 16 + 2 * i:18 + 2 * i], AF.Exp, [small], [small])
            tt(small[:, 8 + i:9 + i], small[:, 21 + 2 * i:22 + 2 * i], small[:, 20 + 2 * i:21 + 2 * i], ALU.subtract, [small], [small])
            ts(small[:, 8 + i:9 + i], small[:, 8 + i:9 + i], -lam_init, None, ALU.add, None, [small], [small])
            ts(small[:, 12 + i:13 + i], small[:, i:i + 1], 1.0 - lam_init, None, ALU.mult, None, [small], [small])

        f_main = nc.dram_tensor("f_main", [20, LW], BF16)
        f_dil = nc.dram_tensor("f_dil", [20, 1536], BF16)
        B_fmain = Buf("fmain")
        B_fdil = Buf("fdil")
        with ExitStack() as s0:
            tab = sbt(s0, "tab", [33, 20], F32)
            ohm = sbt(s0, "ohm", [33, LW], F32)
            ohd = sbt(s0, "ohd", [33, 1536], F32)
            fsb = sbt(s0, "fsb", [20, LW], BF16)
            fsd = sbt(s0, "fsd", [20, 1536], BF16)
            memset(tab[:], NEGBIG, [tab])
            dma("sp", tab[0:32, :], rel_d, [tab], [tab])
            dma("sp", ohm[:], c_oh_main, [], [ohm])
            dma("sp", ohd[:], c_oh_dil, [], [ohd])
            k = 0
            for c in range(LW // 512):
                ps = PS[k % 8]
                k += 1
                mm(ps[0:20, :], tab[0:33, 0:20], ohm[0:33, c * 512:(c + 1) * 512], True, True, [tab, ohm], [ps])
                act(fsb[0:20, c * 512:(c + 1) * 512], ps[0:20, :], AF.Exp, [ps], [fsb])
            for gi in range(3):
                ps = PS[k % 8]
                k += 1
                mm(ps[0:20, :], tab[0:33, 0:20], ohd[0:33, gi * 512:(gi + 1) * 512], True, True, [tab, ohd], [ps])
                act(fsd[0:20, gi * 512:(gi + 1) * 512], ps[0:20, :], AF.Exp, [ps], [fsd])
            dma("sp", f_main.ap()[:, :], fsb[0:20, :], [fsb], [B_fmain])
            dma("sp", f_dil.ap()[:, :], fsd[0:20, :], [fsd], [B_fdil])
            P.barrier()
        def emit_table_broadcasts():
            for hh in range(12):
                src = bass.AP(f_main, hh * LW, [[0, 128], [1, LW]])
                dma("sp", m_main.ap()[hh * 128:(hh + 1) * 128, :], src, [B_fmain], [B_mmain[hh]])
            for hh in range(8):
                for gi in range(3):
                    idx = hh * 3 + gi
                    src = bass.AP(f_dil, (12 + hh) * 1536 + gi * 512, [[0, 128], [1, 512]])
                    dma("sp", m_dil.ap()[idx * 128:(idx + 1) * 128, :], src, [B_fdil], [B_mdil[idx]])

        def load_G(dst, head):
            src = bass.AP(m_main, head * 128 * LW + 127, [[LW - 1, 128], [1, GW]])
            dma("sp", dst[:, 0:GW], src, [B_mmain[head]], [dst])

        def load_Gdil(dst, idx):
            src = bass.AP(m_dil, idx * 128 * 512 + 127, [[511, 128], [1, 256]])
            dma("sp", dst, src, [B_mdil[idx]], [])

        gain_src = [en1_d[0], en1_d[1], on1_d[0], on1_d[1], fn_d[0], fn_d[1], fn_d[2], fn_d[3]]

        def norm_transpose(gidx, tiles):
            with ExitStack() as sn:
                norm_body(sn, gidx, tiles)
            P.barrier()

        def norm_body(sn, gidx, tiles):
            if True:
                gb = sbt(sn, "gb", [128, D], F32)
                junk = sbt(sn, "junk", [128, D], BF16)
                xg = [sbt(sn, "xg%d" % i, [128, D], BF16) for i in range(4)]
                dma("sp", gb[:, :], gain_src[gidx].partition_broadcast(128), [], [gb])
                sB = [Buf("st%d" % t) for t in range(NT)]
                hnB = [Buf("hn%d" % t) for t in range(NT)]
                memset(stat[:, 0:16], 0.0, sB, eng="dve")
                def evac(i_, t):
                    ps = PS[i_ % 8]
                    src = ps[:, :].bitcast(BF16)[:, :].rearrange("p (c n) -> p c n", n=128)
                    if i_ % 2 == 0:
                        cp(hnT[:, :, t * 128:(t + 1) * 128], src, [ps], [hnB[t]])
                    else:
                        act(hnT[:, :, t * 128:(t + 1) * 128], src, AF.Copy, [ps], [hnB[t]])

                for i_, t in enumerate(tiles):
                    x_ = xg[i_ % 4]
                    act(junk[:, :], h[:, t, :], AF.Square, [hB[t]], [junk, sB[t]], accum=stat[:, t:t + 1])
                    act(stat[:, 16 + t:17 + t], stat[:, t:t + 1], AF.Ln, [sB[t]], [sB[t]], bias=EPS, scale=1.0 / D)
                    act(stat[:, 32 + t:33 + t], stat[:, 16 + t:17 + t], AF.Exp, [sB[t]], [sB[t]], scale=-0.5)
                    stt(x_[:, :], h[:, t, :], stat[:, 32 + t:33 + t], gb[:, :], ALU.mult, ALU.mult, [hB[t], sB[t], gb], [x_])
                    ps = PS[i_ % 8]
                    psb = ps[:, :].bitcast(BF16)
                    for c in range(8):
                        trp(psb[:, c * 128:(c + 1) * 128], x_[:, c * 128:(c + 1) * 128], idb[:, :], [x_, idb], [ps])
                    if i_ >= 1:
                        evac(i_ - 1, tiles[i_ - 1])
                evac(len(tiles) - 1, tiles[-1])

        def load_w(dst, w2d, col0, ncol, nk):
            src = w2d[:, col0:col0 + ncol].rearrange("(kc p) n -> p kc n", p=128)
            dma("pool", dst[:, 0:nk, 0:ncol], src, [], [dst])

        def proj_fm(wslab, wc0, evac, kchunks=8, src=None, banks=(0, 1, 2), M=128):
            src = hnT if src is None else src
            for tg in range(4):
                ps = PS[banks[tg % len(banks)]]
                for kc in range(kchunks):
                    mm(ps[0:M, :], wslab[:, kc, wc0:wc0 + M], src[:, kc, tg * 512:(tg + 1) * 512],
                       kc == 0, kc == kchunks - 1, [wslab, src], [ps])
                evac(tg, ps)

        rot = {"s": 0, "p": 0}
        pending = []

        def flush_pending():
            while pending:
                pending.pop(0)()

        def attn_tasks(g, k_of, q_of, extra_of, Gt, v_of, ones_ap, numb, denb, pexp, ptb, exp_scale, kq_reads, v_reads,
                       mla=False, on_done=None):
            nj = 4 * g + 4
            tasks = []
            for j in range(nj):
                c0 = max(0, j - 4 * g) * 128
                N = 512 - c0
                kj = k_of(j)
                qj = q_of(g, c0, N)
                ex = extra_of(j, g, c0, N) if extra_of is not None else None
                vj = v_of(j)
                gc = (4 * g - j + 3) * 128 + c0
                st_ = {}

                def score_fn(N=N, kj=kj, qj=qj, ex=ex, st_=st_):
                    sb = PS[rot["s"] % 4]
                    rot["s"] += 1
                    st_["sb"] = sb
                    mm(sb[:, 0:N], kj, qj, True, ex is None, kq_reads, [sb])
                    if ex is not None:
                        mm(sb[:, 0:N], ex[0], ex[1], False, True, kq_reads, [sb])

                def rest_fn(j=j, c0=c0, N=N, vj=vj, gc=gc, st_=st_):
                    sb = st_["sb"]
                    if mla:
                        pt = ptb[rot["p"] % len(ptb)]
                    else:
                        allb = pexp + ptb
                        pt = allb[rot["p"] % len(allb)]
                    rot["p"] += 1
                    act(pt[:, 0:N], sb[:, 0:N], AF.Exp, [sb], [pt], scale=exp_scale)
                    if mla:
                        if j >= 4 * g:
                            tt(pt[:, 0:128], pt[:, 0:128], tri_bf[:, :], ALU.mult, [pt, tri_bf], [pt])
                    else:
                        tt(pt[:, 0:N], pt[:, 0:N], Gt[:, gc:gc + N], ALU.mult, [pt, Gt], [pt])
                    mm(numb[:, c0:512], vj, pt[:, 0:N], j == 0, j == nj - 1, [pt] + v_reads, [numb])
                    if denb is not None:
                        mm(denb[:, c0:512], ones_ap, pt[:, 0:N], j == 0, j == nj - 1, [pt, ones_bf], [denb])
                    if j == min(2, nj - 1):
                        flush_pending()
                    if j == nj - 1 and on_done is not None:
                        on_done()
                tasks.append((score_fn, rest_fn))
            return tasks

        def w_out_residual(stack, mixT, w2d):
            wo = [sbt(stack, "wo%d" % i, [128, 8, 512], BF16) for i in range(2)]
            for ch in range(2):
                load_w(wo[ch], w2d, ch * 512, 512, 8)
            k = 0
            for t in range(NT):
                for ch in range(2):
                    ps = PS[k % 8]
                    k += 1
                    for c in range(8):
                        mm(ps[:, :], mixT[:, c, t * 128:(t + 1) * 128], wo[ch][:, c, :], c == 0, c == 7, [mixT, wo[ch]], [ps])
                    tt(h[:, t, ch * 512:(ch + 1) * 512], ps[:, :], h[:, t, ch * 512:(ch + 1) * 512], ALU.add, [ps, hB[t]], [hB[t]])

        def even_mixer(li):
            i2 = li // 2
            w_in = ewin_d[i2]
            with ExitStack() as sm:
                mixT = sbt(sm, "mixT", [128, 8, S], BF16)
                norm_transpose(i2, list(range(NT)))
                if li == 0:
                    emit_table_broadcasts()
                ckpt("norm")
                with ExitStack() as sa:
                    QTz = [sbt(sa, "QTz%d" % i, [128, S], BF16) for i in range(2)]
                    KTz = [sbt(sa, "KTz%d" % i, [128, S], BF16) for i in range(2)]
                    VP = sbt(sa, "VP", [128, NT, 2, 128], BF16)
                    QTf = [sbt(sa, "QTf%d" % i, [128, 512], F32) for i in range(2)]
                    ksum = sbt(sa, "ksum", [128, 8], F32)
                    gm = sbt(sa, "gm", [128, 256], F32)
                    top = sbt(sa, "top", [128, 256], F32)
                    pen = sbt(sa, "pen", [128, 256], F32)
                    Gt = [sbt(sa, "Gt%d" % i, [128, GW], BF16) for i in range(2)]
                    wq = [sbt(sa, "wq%d" % i, [128, 8, 128], BF16) for i in range(2)]
                    wk = [sbt(sa, "wk%d" % i, [128, 8, 128], BF16) for i in range(2)]
                    wv = [sbt(sa, "wv%d" % i, [128, 8, 128], BF16) for i in range(2)]
                    pexp = [sbt(sa, "pexp%d" % i, [128, 512], BF16) for i in range(3)]
                    ptb = [sbt(sa, "ptb%d" % i, [128, 512], BF16) for i in range(3)]
                    rd = [sbt(sa, "rd%d" % i, [128, 512], F32) for i in range(2)]
                    o0 = sbt(sa, "o0", [128, 512], F32)
                    o1 = sbt(sa, "o1", [128, 512], F32)
                    sq = sbt(sa, "sq", [128, 512], F32)

                    memset(VP[:, :, :, :].rearrange("p a b c -> p (a b c)"), 1.0, [VP], eng="dve")
                    for i_ in range(2):
                        memset(QTz[i_][:, :], 0.0, [QTz[i_]], eng="dve")
                        memset(KTz[i_][:, :], 0.0, [KTz[i_]], eng="dve")
                    dma("pool", KTz[0][64:72, :], c_koh, [KTz[0]], [KTz[0]])
                    dma("pool", KTz[1][0:8, :], c_koh, [KTz[1]], [KTz[1]])
                    units = [("moba", fc) for fc in range(4)] + [("diff", hd) for hd in range(4)]

                    def unit_cols(u):
                        kind, idx = u
                        if kind == "moba":
                            return idx * 128, 512 + idx * 128, 1024 + idx * 128
                        return 1536 + idx * 128, 2048 + idx * 128, 2560 + idx * 128

                    def load_unit_w(ui):
                        qc, kc_, vc = unit_cols(units[ui])
                        load_w(wq[ui % 2], w_in, qc, 128, 8)
                        load_w(wk[ui % 2], w_in, kc_, 128, 8)
                        load_w(wv[ui % 2], w_in, vc, 128, 8)

                    load_unit_w(0)
                    gslot = 0
                    moba_rot = {"i": 0}
                    for ui, u in enumerate(units):
                        kind, idx = u
                        if ui + 1 < len(units):
                            load_unit_w(ui + 1)
                        Wq, Wk, Wv = wq[ui % 2], wk[ui % 2], wv[ui % 2]

                        if kind == "diff" and idx == 0:
                            memset(QTz[0][64:128, :], 0.0, [QTz[0]], eng="dve")
                            memset(QTz[1][0:64, :], 0.0, [QTz[1]], eng="dve")

                        def evac_k(tg, ps):
                            cp(KTz[0][0:64, tg * 512:(tg + 1) * 512], ps[0:64, :], [ps], [KTz[0]])
                            cp(KTz[1][64:128, tg * 512:(tg + 1) * 512], ps[64:128, :], [ps], [KTz[1]])
                            if kind == "moba":
                                P.op("dve", lambda e: e.reduce_sum(
                                    out=ksum[:, 2 * tg:2 * tg + 2],
                                    in_=ps[:, :].rearrange("p (b k) -> p b k", k=256), axis=AX.X), [ps], [ksum])
                        proj_fm(Wk, 0, evac_k)
                        ckpt("u%dk" % ui)
                        ckpt("u%dv" % ui)

                        def evac_q(tg, ps):
                            ts(QTz[0][0:64, tg * 512:(tg + 1) * 512], ps[0:64, :], 0.125, None, ALU.mult, None, [ps], [QTz[0]])
                            ts(QTz[1][64:128, tg * 512:(tg + 1) * 512], ps[64:128, :], 0.125, None, ALU.mult, None, [ps], [QTz[1]])
                            if kind == "moba":
                                qf = QTf[tg % 2]
                                ts(qf[:, :], ps[:, :], 0.125, None, ALU.mult, None, [ps], [qf])
                                for hh in range(2):
                                    for tt_ in range(4):
                                        t = tg * 4 + tt_
                                        col = (hh * 16 + t) * 8
                                        mm(PS[7][:, col:col + 8], qf[hh * 64:(hh + 1) * 64, tt_ * 128:(tt_ + 1) * 128],
                                           ksum[hh * 64:(hh + 1) * 64, 0:8], True, True, [qf, ksum], [PS[7]])
                        proj_fm(Wq, 0, evac_q)
                        ckpt("u%dq" % ui)
                        for tq in range(4):
                            ps = PS[3 + tq % 2]
                            for tt_ in range(4):
                                t = tq * 4 + tt_
                                for kc in range(8):
                                    mm(ps[:, tt_ * 128:(tt_ + 1) * 128], hnT[:, kc, t * 128:(t + 1) * 128], Wv[:, kc, :],
                                       kc == 0, kc == 7, [hnT, Wv], [ps])
                            psv = ps[:, :].rearrange("p (a b) -> p a b", b=128)
                            if kind == "moba":
                                act(VP[:, tq * 4:(tq + 1) * 4, 0, 0:64], psv[:, :, 0:64], AF.Copy, [ps], [VP])
                                act(VP[:, tq * 4:(tq + 1) * 4, 1, 64:128], psv[:, :, 64:128], AF.Copy, [ps], [VP])
                            else:
                                act(VP[:, tq * 4:(tq + 1) * 4, 0, :], psv, AF.Copy, [ps], [VP])


                        if kind == "moba":
                            tt(gm[:, :], PS[7][:, 0:256], gmask[:, :], ALU.add, [PS[7], gmask], [gm])
                            for q_ in range(32):
                                P.op("dve", lambda e, q_=q_: e.max(out=top[:, q_ * 8:(q_ + 1) * 8], in_=gm[:, q_ * 8:(q_ + 1) * 8]),
                                     [gm], [top])
                            thr = top[:, :].rearrange("p (a b) -> p a b", b=8)[:, :, 2:3].to_broadcast([128, 32, 8])
                            tt(pen[:, :].rearrange("p (a b) -> p a b", b=8), gm[:, :].rearrange("p (a b) -> p a b", b=8), thr,
                               ALU.is_ge, [gm, top], [pen])
                            ckpt("u%dsel" % ui)
                            ts(pen[:, :], pen[:, :], -1.0, -NEGBIG, ALU.add, ALU.mult, [pen], [pen])
                            tt(pen[:, :], pen[:, :], omask[:, :], ALU.mult, [pen, omask], [pen])
                            for hh in range(2):
                                for grp in range(4):
                                    ps = PS[5 + grp % 2]
                                    for tt_ in range(4):
                                        t = grp * 4 + tt_
                                        col = (hh * 16 + t) * 8
                                        trp(ps[0:8, tt_ * 128:(tt_ + 1) * 128], pen[:, col:col + 8], idf[:, :], [pen, idf], [ps])
                                    prow = 64 if hh == 0 else 0
                                    cp(QTz[hh][prow:prow + 8, grp * 512:(grp + 1) * 512], ps[0:8, :], [ps], [QTz[hh]])

                        ckpt("u%dproj" % ui)
                        utasks = []
                        if kind == "moba":
                            for hh in range(2):
                                head = idx * 2 + hh
                                G = Gt[gslot % 2]
                                gslot += 1
                                load_G(G, head)
                                lo, hi = hh * 64, (hh + 1) * 64
                                dlo, dhi = (1 - hh) * 64, (2 - hh) * 64
                                for g in range(4):
                                    numb = PS[4 + (moba_rot["i"] % 4)]
                                    moba_rot["i"] += 1

                                    def epi(g=g, numb=numb, lo=lo, hi=hi, dlo=dlo, dhi=dhi, idx=idx):
                                        r_ = rd[g % 2]
                                        recip(r_[lo:hi, :], numb[dlo:dhi, :], [numb], [r_])
                                        tt(mixT[lo:hi, idx, g * 512:(g + 1) * 512], numb[lo:hi, :], r_[lo:hi, :], ALU.mult,
                                           [numb, r_], [mixT])
                                    utasks += attn_tasks(
                                        g,
                                        lambda j, hh=hh: KTz[hh][:, j * 128:(j + 1) * 128],
                                        lambda g_, c0, N, hh=hh: QTz[hh][:, g_ * 512 + c0:g_ * 512 + c0 + N],
                                        None,
                                        G, lambda j, hh=hh: VP[:, j, hh, :], None, numb, None, pexp, ptb, 1.0,
                                        [KTz[hh], QTz[hh]], [VP], on_done=lambda epi=epi: pending.append(epi))
                        else:
                            head = 8 + idx
                            G = Gt[gslot % 2]
                            gslot += 1
                            load_G(G, head)
                            for g in range(4):
                                for c in range(2):
                                    numb, denb = (PS[4], PS[5]) if c == 0 else (PS[6], PS[7])

                                    def done(g=g, idx=idx):
                                        act(rd[0][:, :], PS[5][:, :], AF.Ln, [PS[5]], [rd[0]])
                                        cp(o0[:, :], PS[4][:, :], [PS[4]], [o0])
                                        act(rd[1][:, :], PS[7][:, :], AF.Ln, [PS[7]], [rd[1]])
                                        cp(o1[:, :], PS[6][:, :], [PS[6]], [o1])

                                        def epi(g=g, idx=idx):
                                            act(rd[0][:, :], rd[0][:, :], AF.Exp, [rd[0]], [rd[0]], scale=-1.0)
                                            act(rd[1][:, :], rd[1][:, :], AF.Exp, [rd[1]], [rd[1]], scale=-1.0)
                                            tt(o0[:, :], o0[:, :], rd[0][:, :], ALU.mult, [o0, rd[0]], [o0])
                                            tt(o1[:, :], o1[:, :], rd[1][:, :], ALU.mult, [o1, rd[1]], [o1])
                                            stt(o0[:, :], o1[:, :], small[:, 8 + i2:9 + i2], o0[:, :], ALU.mult, ALU.add, [o1, o0, small], [o0])
                                            act(sq[:, :], o0[:, :], AF.Square, [o0], [sq])
                                            ssb = PS[7]
                                            mm(ssb[:, :], ones_f[:, :], sq[:, :], True, True, [ones_f, sq], [ssb])
                                            act(sq[:, :], ssb[:, :], AF.Ln, [ssb], [sq], bias=EPS, scale=1.0 / 128)
                                            act(sq[:, :], sq[:, :], AF.Exp, [sq], [sq], scale=-0.5)
                                            stt(mixT[:, 4 + idx, g * 512:(g + 1) * 512], o0[:, :], small[:, 12 + i2:13 + i2], sq[:, :],
                                                ALU.mult, ALU.mult, [o0, small, sq], [mixT])
                                        pending.append(epi)
                                    utasks += attn_tasks(
                                        g,
                                        lambda j, c=c: KTz[c][:, j * 128:(j + 1) * 128],
                                        lambda g_, c0, N, c=c: QTz[c][:, g_ * 512 + c0:g_ * 512 + c0 + N],
                                        None, G, lambda j: VP[:, j, 0, :], ones_bf[:, :], numb, denb, pexp, ptb, 1.0,
                                        [KTz[c], QTz[c]], [VP], on_done=(done if c == 1 else None))
                        run_pipeline(utasks, 3)
                        flush_pending()
                        ckpt("u%d" % ui)
                P.barrier()
                with ExitStack() as sw:
                    w_out_residual(sw, mixT, ewout_d[i2])
                    norm_body(sw, 4 + li, list(range(NT)))
            P.barrier()

        def conv_ffn(li):
            w_in = fwin_d[li]
            w_out = fwout_d[li]
            with ExitStack() as sf:
                actT = sbt(sf, "actT", [128, NCC, 1024], BF16)
                wug = [sbt(sf, "wug%d" % i, [128, 8, 256], BF16) for i in range(3)]
                wo2 = [sbt(sf, "wo2_%d" % i, [128, NCC, 256], BF16) for i in range(2)]
                graw = [sbt(sf, "graw%d" % i, [128, 1040], F32) for i in range(2)]
                A = [sbt(sf, "A%d" % i, [128, 512], F32) for i in range(2)]
                Ag = [sbt(sf, "Ag%d" % i, [128, 512], F32) for i in range(2)]
                gtail = sbt(sf, "gtail", [128, NCC, 2], F32)

                def load_pair(slot, cc):
                    src_u = w_in[:, cc * 128:(cc + 1) * 128].rearrange("(kc p) n -> p kc n", p=128)
                    src_g = w_in[:, DFF + cc * 128:DFF + (cc + 1) * 128].rearrange("(kc p) n -> p kc n", p=128)
                    dma("pool", wug[slot][:, :, 0:128], src_u, [], [wug[slot]])
                    dma("pool", wug[slot][:, :, 128:256], src_g, [], [wug[slot]])

                k = 0
                ffn_tail = []
                for half in range(2):
                    tiles = list(range(half * 8, half * 8 + 8))
                    load_pair(0, 0)
                    load_pair(1, 1)
                    for cc in range(NCC):
                        if cc + 2 < NCC:
                            load_pair((cc + 2) % 3, cc + 2)
                        W = wug[cc % 3]
                        gr = graw[cc % 2]
                        if half == 0:
                            memset(gr[:, 14:16], 0.0, [gr], eng="dve")
                        else:
                            cp(gr[:, 14:16], gtail[:, cc, :], [gtail], [gr])
                        w0 = convw[:, (li * 3 + 0) * NCC + cc:(li * 3 + 0) * NCC + cc + 1]
                        w1 = convw[:, (li * 3 + 1) * NCC + cc:(li * 3 + 1) * NCC + cc + 1]
                        w2 = convw[:, (li * 3 + 2) * NCC + cc:(li * 3 + 2) * NCC + cc + 1]
                        bb = convw[:, (12 + li) * NCC + cc:(12 + li) * NCC + cc + 1]
                        for tgi in range(2):
                            tok0 = half * 1024 + tgi * 512
                            pu = PS[(k * 2) % 8]
                            pg = PS[(k * 2 + 1) % 8]
                            k += 1
                            for kc in range(8):
                                mm(pg[:, :], W[:, kc, 128:256], hnT[:, kc, tok0:tok0 + 512], kc == 0, kc == 7, [W, hnT], [pg])
                            for kc in range(8):
                                mm(pu[:, :], W[:, kc, 0:128], hnT[:, kc, tok0:tok0 + 512], kc == 0, kc == 7, [W, hnT], [pu])
                            a = A[tgi]
                            ag = Ag[tgi]
                            act(gr[:, 16 + tgi * 512:16 + (tgi + 1) * 512], pg[:, :], AF.Copy, [pg], [gr])
                            ts(a[:, :], gr[:, 16 + tgi * 512:16 + (tgi + 1) * 512], w2, bb, ALU.mult, ALU.add, [gr, convw], [a])
                            stt(a[:, :], gr[:, 15 + tgi * 512:15 + (tgi + 1) * 512], w1, a[:, :], ALU.mult, ALU.add, [gr, a, convw], [a], eng=FMA_ENG)
                            stt(a[:, :], gr[:, 14 + tgi * 512:14 + (tgi + 1) * 512], w0, a[:, :], ALU.mult, ALU.add, [gr, a, convw], [a], eng=FMA_ENG)
                            if ffn_tail:
                                ffn_tail.pop()()

                            def tail(a=a, ag=ag, pu=pu, cc=cc, tgi=tgi):
                                act(ag[:, :], a[:, :], AF.Gelu, [a], [ag])
                                tt(actT[:, cc, tgi * 512:(tgi + 1) * 512], ag[:, :], pu[:, :], ALU.mult, [ag, pu], [actT])
                            ffn_tail.append(tail)
                        if half == 0:
                            cp(gtail[:, cc, :], gr[:, 1038:1040], [gr], [gtail])
                    if ffn_tail:
                        ffn_tail.pop()()
                    load_cols = lambda slot, cq: dma(
                        "pool", wo2[slot][:, :, :],
                        w_out[:, cq * 256:(cq + 1) * 256].rearrange("(cc p) n -> p cc n", p=128), [], [wo2[slot]])
                    load_cols(0, 0)
                    for cq in range(4):
                        if cq + 1 < 4:
                            load_cols((cq + 1) % 2, cq + 1)
                        Wo = wo2[cq % 2]
                        for tl in range(8):
                            t = half * 8 + tl
                            ps = PS[k % 8]
                            k += 1
                            for cc in range(NCC):
                                mm(ps[:, 0:256], actT[:, cc, tl * 128:(tl + 1) * 128], Wo[:, cc, :], cc == 0, cc == NCC - 1,
                                   [actT, Wo], [ps])
                            tt(h[:, t, cq * 256:(cq + 1) * 256], ps[:, 0:256], h[:, t, cq * 256:(cq + 1) * 256], ALU.add,
                               [ps, hB[t]], [hB[t]])
            P.barrier()

        def odd_mixer(li):
            i2 = li // 2
            w_in = owin_d[i2]
            with ExitStack() as sm:
                mixT = sbt(sm, "mixTo", [128, 8, S], BF16)
                norm_transpose(2 + i2, list(range(NT)))
                with ExitStack() as sa:
                    QTz = [sbt(sa, "oQTz%d" % i, [128, S], BF16) for i in range(2)]
                    KTz = [sbt(sa, "oKTz%d" % i, [128, S], BF16) for i in range(2)]
                    for i_ in range(2):
                        memset(QTz[i_][:, :], 0.0, [QTz[i_]], eng="dve")
                        memset(KTz[i_][:, :], 0.0, [KTz[i_]], eng="dve")
                    VS = sbt(sa, "VS", [128, NT, 2, 128], BF16)
                    accn = sbt(sa, "accn", [128, S], F32)
                    accd = sbt(sa, "accd", [128, S], F32)
                    Gd = sbt(sa, "Gd", [128, 3, 2, 256], BF16)
                    wq = [sbt(sa, "owq%d" % i, [128, 8, 128], BF16) for i in range(2)]
                    wk = [sbt(sa, "owk%d" % i, [128, 8, 128], BF16) for i in range(2)]
                    wv = [sbt(sa, "owv%d" % i, [128, 8, 128], BF16) for i in range(2)]
                    pexp = [sbt(sa, "opexp%d" % i, [128, 512], BF16) for i in range(4)]
                    ptb = [sbt(sa, "optb%d" % i, [128, 512], BF16) for i in range(4)]
                    rd = [sbt(sa, "ord%d" % i, [128, 512], F32) for i in range(2)]

                    memset(VS[:, :, :, :].rearrange("p a b c -> p (a b c)"), 0.0, [VS], eng="dve")

                    def load_unit_w(ui):
                        load_w(wq[ui % 2], w_in, ui * 128, 128, 8)
                        load_w(wk[ui % 2], w_in, 512 + ui * 128, 128, 8)
                        load_w(wv[ui % 2], w_in, 1024 + ui * 128, 128, 8)

                    load_unit_w(0)
                    sr = 0
                    vr = {"i": 0}
                    for ui in range(4):
                        if ui + 1 < 4:
                            load_unit_w(ui + 1)
                        Wq, Wk, Wv = wq[ui % 2], wk[ui % 2], wv[ui % 2]
                        def evac_k(tg, ps):
                            cp(KTz[0][0:64, tg * 512:(tg + 1) * 512], ps[0:64, :], [ps], [KTz[0]])
                            cp(KTz[1][64:128, tg * 512:(tg + 1) * 512], ps[64:128, :], [ps], [KTz[1]])

                        def evac_q(tg, ps):
                            ts(QTz[0][0:64, tg * 512:(tg + 1) * 512], ps[0:64, :], 0.125, None, ALU.mult, None, [ps], [QTz[0]])
                            ts(QTz[1][64:128, tg * 512:(tg + 1) * 512], ps[64:128, :], 0.125, None, ALU.mult, None, [ps], [QTz[1]])
                        proj_fm(Wk, 0, evac_k)
                        proj_fm(Wq, 0, evac_q)
                        for hh in range(2):
                            for gi in range(3):
                                idx = (ui * 2 + hh) * 3 + gi
                                src = bass.AP(m_dil, idx * 128 * 512 + 127, [[511, 128], [1, 256]])
                                dma("sp", Gd[:, gi, hh, :], src, [B_mdil[idx]], [Gd])
                        first = True
                        for gi, dil in enumerate((1, 4, 16)):
                            L = S // dil
                            nblk = max(1, L // 128)
                            for r in range(dil):
                                for n0 in range(0, nblk, 4):
                                    nn = min(4, nblk - n0)
                                    ps = PS[(3, 0, 1, 2)[vr["i"] % 4]]
                                    vr["i"] += 1
                                    for n_ in range(nn):
                                        n = n0 + n_
                                        t0 = n * 128 * dil + r
                                        for kc in range(8):
                                            mm(ps[:, n_ * 128:(n_ + 1) * 128],
                                               hnT[:, kc, t0:t0 + 127 * dil + 1:dil], Wv[:, kc, :],
                                               kc == 0, kc == 7, [hnT, Wv], [ps])
                                    slot0 = r * nblk + n0
                                    psv = ps[:, 0:nn * 128].rearrange("p (a b) -> p a b", b=128)
                                    act(VS[:, slot0:slot0 + nn, 0, 0:64], psv[:, :, 0:64], AF.Copy, [ps], [VS])
                                    act(VS[:, slot0:slot0 + nn, 1, 64:128], psv[:, :, 64:128], AF.Copy, [ps], [VS])
                            ckpt("d%dv%d" % (ui, gi))
                            tasks = []
                            for r in range(dil):
                                for seg0 in range(0, nblk, 4):
                                    segn = min(4, nblk - seg0)
                                    W_ = segn * 128
                                    numb, denb = (PS[4], PS[5]) if sr % 2 == 0 else (PS[6], PS[7])
                                    sr += 1
                                    kbs = [kb for kb in range(seg0 - 1, seg0 + segn) if kb >= 0]
                                    for ki, kb in enumerate(kbs):
                                        q0 = max(kb, seg0)
                                        q1 = min(kb + 2, seg0 + segn)
                                        N = (q1 - q0) * 128
                                        xoff = (q0 - kb) * 128
                                        cpos = (q0 - seg0) * 128
                                        kt0 = kb * 128 * dil + r
                                        qt0 = q0 * 128 * dil + r
                                        for hh in range(2):
                                            st_ = {}

                                            def score_fn(N=N, kt0=kt0, qt0=qt0, st_=st_, hh=hh):
                                                sb = PS[rot["s"] % 4]
                                                rot["s"] += 1
                                                st_["sb"] = sb
                                                mm(sb[:, 0:N], KTz[hh][:, kt0:kt0 + 127 * dil + 1:dil],
                                                   QTz[hh][:, qt0:qt0 + (N - 1) * dil + 1:dil], True, True, [KTz[hh], QTz[hh]], [sb])

                                            def rest_fn(N=N, xoff=xoff, cpos=cpos, kb=kb, r=r, st_=st_, numb=numb, denb=denb, hh=hh,
                                                        first_k=(ki == 0 and hh == 0), last_k=(ki == len(kbs) - 1 and hh == 1),
                                                        W_=W_, seg0=seg0, first=first):
                                                sb = st_["sb"]
                                                pe_ = pexp[rot["p"] % 4]
                                                pt = ptb[rot["p"] % 4]
                                                rot["p"] += 1
                                                act(pe_[:, 0:N], sb[:, 0:N], AF.Exp, [sb], [pe_])
                                                tt(pt[:, 0:N], pe_[:, 0:N], Gd[:, gi, hh, xoff:xoff + N], ALU.mult, [pe_, Gd], [pt])
                                                mm(numb[:, cpos:cpos + N], VS[:, r * nblk + kb, hh, :], pt[:, 0:N], first_k, False,
                                                   [pt, VS], [numb])
                                                mm(denb[:, cpos:cpos + N], (onesA if hh == 0 else onesB)[:, :], pt[:, 0:N], first_k, False,
                                                   [pt, onesA, onesB], [denb])
                                                if last_k:
                                                    tq0 = seg0 * 128 * dil + r
                                                    dst_n = accn[:, tq0:tq0 + (W_ - 1) * dil + 1:dil]
                                                    dst_d = accd[:, tq0:tq0 + (W_ - 1) * dil + 1:dil]
                                                    if first:
                                                        cp(dst_n, numb[:, 0:W_], [numb], [accn])
                                                        cp(dst_d, denb[:, 0:W_], [denb], [accd])
                                                    else:
                                                        tt(dst_n, numb[:, 0:W_], dst_n, ALU.add, [numb, accn], [accn])
                                                        tt(dst_d, denb[:, 0:W_], dst_d, ALU.add, [denb, accd], [accd])
                                            tasks.append((score_fn, rest_fn))
                            run_pipeline(tasks, 3)
                            ckpt("d%db%d" % (ui, gi))
                            first = False
                        ckpt("d%dpre" % ui)
                        for g in range(4):
                            r_ = rd[g % 2]
                            recip(r_[:, :], accd[:, g * 512:(g + 1) * 512], [accd], [r_])
                            tt(mixT[:, ui, g * 512:(g + 1) * 512], accn[:, g * 512:(g + 1) * 512], r_[:, :], ALU.mult, [accn, r_], [mixT])
                P.barrier()
                with ExitStack() as sb_:
                    wuq = sbt(sb_, "wuq", [128, 2, 768], BF16)
                    wukv = sbt(sb_, "wukv", [128, 1, 1024], BF16)
                    cqT = sbt(sb_, "cqT", [128, 2, S], BF16)
                    ckvT = sbt(sb_, "ckvT", [128, 1, S], BF16)
                    krT = sbt(sb_, "krT", [128, S], BF16)
                    rcs = sbt(sb_, "rcs", [64, 2 * S], F32)
                    xs1 = [sbt(sb_, "xs1_%d" % i, [64, 512], F32) for i in range(2)]
                    xs2 = [sbt(sb_, "xs2_%d" % i, [64, 512], F32) for i in range(2)]
                    lst = sbt(sb_, "lst", [128, 96], F32)
                    rr = {"i": 0}

                    def rope(dst_ap, dst_T, ps, col0, tok0, n):
                        a = xs1[rr["i"] % 2]
                        b = xs2[rr["i"] % 2]
                        rr["i"] += 1
                        tt(b[0:32, 0:n], ps[32:64, col0:col0 + n], rcs[0:32, S + tok0:S + tok0 + n], ALU.mult, [ps, rcs], [b])
                        tt(b[32:64, 0:n], ps[0:32, col0:col0 + n], rcs[32:64, S + tok0:S + tok0 + n], ALU.mult, [ps, rcs], [b])
                        tt(a[0:64, 0:n], ps[0:64, col0:col0 + n], rcs[0:64, tok0:tok0 + n], ALU.mult, [ps, rcs], [a])
                        tt(dst_ap, a[0:64, 0:n], b[0:64, 0:n], ALU.add, [a, b], [dst_T])

                    memset(krT[64:128, :], 0.0, [krT], eng="dve")
                    load_w(wuq, wuq_d[i2], 0, 768, 2)
                    load_w(wukv, wukv_d[i2], 0, 1024, 1)
                    dma("sp", rcs[0:64, 0:S], c_ropec, [], [rcs])
                    dma("sp", rcs[0:64, S:2 * S], c_ropes, [], [rcs])
                    with ExitStack() as s1:
                        wl = sbt(s1, "wl", [128, 8, 448], BF16)
                        lat = [sbt(s1, "lat%d" % i, [128, 448], F32) for i in range(4)]
                        load_w(wl, w_in, 1536, 448, 8)
                        lB = [Buf("lst%d" % t) for t in range(NT)]
                        memset(lst[:, 0:32], 0.0, lB, eng="dve")
                        for t in range(NT):
                            ps = PS[t % 4]
                            la = lat[t % 4]
                            for kc in range(8):
                                mm(ps[:, 0:448], hnT[:, kc, t * 128:(t + 1) * 128], wl[:, kc, :], kc == 0, kc == 7, [hnT, wl], [ps])
                            act(la[:, 0:256], ps[:, 0:256], AF.Square, [ps], [la, lB[t]], accum=lst[:, t:t + 1])
                            act(la[:, 256:384], ps[:, 256:384], AF.Square, [ps], [la, lB[t]], accum=lst[:, 16 + t:17 + t])
                            act(lst[:, 32 + t:33 + t], lst[:, t:t + 1], AF.Ln, [lB[t]], [lB[t]], bias=EPS, scale=1.0 / 256)
                            act(lst[:, 32 + t:33 + t], lst[:, 32 + t:33 + t], AF.Exp, [lB[t]], [lB[t]], scale=-0.5)
                            act(lst[:, 48 + t:49 + t], lst[:, 16 + t:17 + t], AF.Ln, [lB[t]], [lB[t]], bias=EPS, scale=1.0 / 128)
                            act(lst[:, 48 + t:49 + t], lst[:, 48 + t:49 + t], AF.Exp, [lB[t]], [lB[t]], scale=-0.5)
                            act(la[:, 0:256], ps[:, 0:256], AF.Copy, [ps, lB[t]], [la], scale=lst[:, 32 + t:33 + t])
                            act(la[:, 256:384], ps[:, 256:384], AF.Copy, [ps, lB[t]], [la], scale=lst[:, 48 + t:49 + t])
                            act(la[:, 384:448], ps[:, 384:448], AF.Copy, [ps], [la])
                            pt_ = PS[4 + t % 4]
                            for c in range(3):
                                trp(pt_[:, c * 128:(c + 1) * 128], la[:, c * 128:(c + 1) * 128], idf[:, :], [la, idf], [pt_])
                            trp(pt_[0:64, 384:512], la[:, 384:448], idf[:, :], [la, idf], [pt_])
                            for c in range(2):
                                ts(cqT[:, c, t * 128:(t + 1) * 128], pt_[:, c * 128:(c + 1) * 128], small[:, 2 + 2 * i2 + c:3 + 2 * i2 + c],
                                   None, ALU.mult, None, [pt_, small], [cqT])
                            ts(ckvT[:, 0, t * 128:(t + 1) * 128], pt_[:, 256:384], small[:, 6 + i2:7 + i2], None, ALU.mult, None,
                               [pt_, small], [ckvT])
                            rope_dst = krT[0:64, t * 128:(t + 1) * 128]
                            rope2(rope_dst, krT, pt_, 384, t * 128, 128, rope)
                    P.barrier()
                    with ExitStack() as s2:
                        qnT = sbt(s2, "qnT", [128, S], BF16)
                        qrT = sbt(s2, "qrT", [128, S], BF16)
                        memset(qrT[64:128, :], 0.0, [qrT], eng="dve")
                        knT = sbt(s2, "knT", [128, S], BF16)
                        VM = sbt(s2, "VM", [128, NT, 128], BF16)
                        ptb = [sbt(s2, "mptb%d" % i, [128, 512], BF16) for i in range(4)]
                        rd = [sbt(s2, "mrd%d" % i, [128, 512], F32) for i in range(2)]
                        scale = 192.0 ** -0.5
                        for hd in range(4):
                            proj_fm(wuq, hd * 192, lambda tg, ps: cp(qnT[:, tg * 512:(tg + 1) * 512], ps[:, :], [ps], [qnT]),
                                    kchunks=2, src=cqT)
                            proj_fm(wuq, hd * 192 + 128,
                                    lambda tg, ps: rope2(qrT[0:64, tg * 512:(tg + 1) * 512], qrT, ps, 0, tg * 512, 512, rope),
                                    kchunks=2, src=cqT, M=64)
                            proj_fm(wukv, hd * 256, lambda tg, ps: cp(knT[:, tg * 512:(tg + 1) * 512], ps[:, :], [ps], [knT]),
                                    kchunks=1, src=ckvT)
                            for tq in range(4):
                                ps = PS[3 + tq % 2]
                                for tt_ in range(4):
                                    t = tq * 4 + tt_
                                    mm(ps[:, tt_ * 128:(tt_ + 1) * 128], ckvT[:, 0, t * 128:(t + 1) * 128],
                                       wukv[:, 0, hd * 256 + 128:hd * 256 + 256], True, True, [ckvT, wukv], [ps])
                                act(VM[:, tq * 4:(tq + 1) * 4, :], ps[:, :].rearrange("p (a b) -> p a b", b=128), AF.Copy, [ps], [VM])
                            mtasks = []
                            for g in range(4):
                                numb, denb = (PS[4], PS[5]) if g % 2 == 0 else (PS[6], PS[7])

                                def epi(g=g, numb=numb, denb=denb, hd=hd):
                                    r_ = rd[g % 2]
                                    recip(r_[:, :], denb[:, :], [denb], [r_])
                                    tt(mixT[:, 4 + hd, g * 512:(g + 1) * 512], numb[:, :], r_[:, :], ALU.mult, [numb, r_], [mixT])
                                mtasks += attn_tasks(
                                    g,
                                    lambda j: knT[:, j * 128:(j + 1) * 128],
                                    lambda g_, c0, N: qnT[:, g_ * 512 + c0:g_ * 512 + c0 + N],
                                    lambda j, g_, c0, N: (krT[:, j * 128:(j + 1) * 128],
                                                          qrT[:, g_ * 512 + c0:g_ * 512 + c0 + N]),
                                    None, lambda j: VM[:, j, :], ones_bf[:, :], numb, denb, None, ptb, scale,
                                    [knT, qnT, krT, qrT], [VM], mla=True, on_done=lambda epi=epi: pending.append(epi))
                            run_pipeline(mtasks, 3)
                            flush_pending()
                P.barrier()
                with ExitStack() as sw:
                    w_out_residual(sw, mixT, owout_d[i2])
                    norm_body(sw, 4 + li, list(range(NT)))
            P.barrier()

        try:
            ckpt("setup")
            for li in range(n_layers):
                if li % 2 == 0:
                    even_mixer(li)
                else:
                    odd_mixer(li)
                ckpt("mixer%d" % li)
                conv_ffn(li)
        except _Stop:
            pass
        P.stopped = False
        P.barrier()

        with ExitStack() as sfin:
            gfin = sbt(sfin, "gfin", [128, D], F32)
            yb = [sbt(sfin, "yb%d" % i, [128, D], F32) for i in range(2)]
            dma("sp", gfin[:, :], fin_d.partition_broadcast(128), [], [gfin])
            for t in range(NT):
                y = yb[t % 2]
                if debug_h:
                    cp(y[:, :], h[:, t, :], [hB[t]], [y])
                else:
                    act(y[:, :], h[:, t, :], AF.Square, [hB[t]], [y, stat], accum=stat[:, t:t + 1])
                    act(stat[:, 16 + t:17 + t], stat[:, t:t + 1], AF.Ln, [stat], [stat], bias=EPS, scale=1.0 / D)
                    act(stat[:, 32 + t:33 + t], stat[:, 16 + t:17 + t], AF.Exp, [stat], [stat], scale=-0.5)
                    stt(y[:, :], h[:, t, :], stat[:, 32 + t:33 + t], gfin[:, :], ALU.mult, ALU.mult, [hB[t], stat, gfin], [y])
                dma("sp", out_d[t * 128:(t + 1) * 128, :], y[:, :], [y], [B_out])
            P.wait_all("sp", [B_out])
            P.barrier()

        with nc.Block() as block:
            P.emit(block)
    return nc


_CACHE = {}


def kernel(**inputs):
    n = 8
    consts = _host_consts()
    if "nc" not in _CACHE:
        _CACHE["nc"] = build_program()
    nc = _CACHE["nc"]
    x = np.ascontiguousarray(np.asarray(inputs["x"], dtype=np.float32))
    shared = {k: np.ascontiguousarray(np.asarray(v, dtype=np.float32)) for k, v in inputs.items() if k != "x"}
    shared.update(consts)
    in_maps = []
    for c in range(n):
        m = dict(shared)
        m["x"] = x[c]
        in_maps.append(m)
    res = run_bass_kernel_spmd(nc, in_maps, core_ids=list(range(n)))
    out = np.stack([np.asarray(r["out"], dtype=np.float32) for r in res.results], axis=0)
    return out
```

```python
import math
from contextlib import ExitStack

import numpy as np
import concourse.bass as bass
import concourse.mybir as mybir
from concourse.bass_utils import run_bass_kernel_spmd

F32 = mybir.dt.float32
BF16 = mybir.dt.bfloat16
ALU = mybir.AluOpType
AF = mybir.ActivationFunctionType
AX = mybir.AxisListType

S = 2048
D = 1024
NT = 16
DEPTH = 4
DFF = 2816
NCC = 22
EPS = 1e-6
LW = 2560
GW = 2432
NEGBIG = -30000.0


class Buf:
    __slots__ = ("name", "w", "rs")

    def __init__(self, name):
        self.name = name
        self.w = None
        self.rs = []


class T:
    __slots__ = ("t", "b")

    def __init__(self, t, name):
        self.t = t
        self.b = Buf(name)

    def __getitem__(self, k):
        return self.t[k]


class Prog:
    ENGS = ("pe", "act", "dve", "pool", "sp")

    def __init__(self, nc, n_dma_sems=8):
        self.nc = nc
        self.ops = {e: [] for e in self.ENGS}
        self.cnt = {e: 0 for e in self.ENGS}
        self.waited = {e: {} for e in self.ENGS}
        self.sems = {}
        self.n_dma_sems = n_dma_sems
        self.dma_i = {"sp": 0, "pool": 0, "act": 0}
        self.dma_last = {}
        self.stopped = False

    def alloc_sems(self, stack):
        for e in self.ENGS:
            self.sems[e] = stack.enter_context(self.nc.semaphore("s_" + e))
        for q in ("sp", "pool", "act"):
            for i in range(self.n_dma_sems):
                self.sems[("dma", q, i)] = stack.enter_context(self.nc.semaphore(f"d_{q}_{i}"))

    def _need(self, eng, waits, ev, kind):
        if ev is None:
            return
        semkey, val, src = ev
        if src == eng and semkey == eng:
            if eng == "pe" or kind == "war":
                return
        if self.waited[eng].get(semkey, 0) >= val:
            return
        if waits.get(semkey, 0) < val:
            waits[semkey] = val

    def _deps(self, eng, reads, writes):
        waits = {}
        for b in reads:
            self._need(eng, waits, b.w, "raw")
        for b in writes:
            self._need(eng, waits, b.w, "waw")
            for r in b.rs:
                self._need(eng, waits, r, "war")
        for k, v in waits.items():
            self.waited[eng][k] = v
        return list(waits.items())

    def op(self, eng, fn, reads=(), writes=()):
        if self.stopped:
            return None
        reads = [x.b if isinstance(x, T) else x for x in reads]
        writes = [x.b if isinstance(x, T) else x for x in writes]
        waits = self._deps(eng, reads, writes)
        self.cnt[eng] += 1
        ev = (eng, self.cnt[eng], eng)
        self.ops[eng].append((waits, fn, (eng, 1)))
        for b in reads:
            b.rs.append(ev)
            if len(b.rs) > 64:
                b.rs = self._prune(b.rs)
        for b in writes:
            b.w = ev
            b.rs = []
        return ev

    @staticmethod
    def _prune(rs):
        best = {}
        for (k, v, s) in rs:
            if k not in best or best[k][1] < v:
                best[k] = (k, v, s)
        return list(best.values())

    def dma(self, q, fn, reads=(), writes=()):
        if self.stopped:
            return None
        reads = [x.b if isinstance(x, T) else x for x in reads]
        writes = [x.b if isinstance(x, T) else x for x in writes]
        waits = self._deps(q, reads, writes)
        i = self.dma_i[q]
        self.dma_i[q] += 1
        slot = i % self.n_dma_sems
        semkey = ("dma", q, slot)
        val = 16 * (i // self.n_dma_sems + 1)
        if val > 16 and self.waited[q].get(semkey, 0) < val - 16:
            waits.append((semkey, val - 16))
            self.waited[q][semkey] = val - 16
        ev = (semkey, val, q + "_dma")
        self.dma_last[semkey] = val
        self.ops[q].append((waits, fn, (semkey, 16)))
        for b in reads:
            b.rs.append(ev)
        for b in writes:
            b.w = ev
            b.rs = []
        return ev

    def barrier(self):
        if self.stopped:
            return
        for e in self.ENGS:
            waits = []
            for o in self.ENGS:
                if o == e:
                    continue
                v = self.cnt[o]
                if v > 0 and self.waited[e].get(o, 0) < v:
                    waits.append((o, v))
                    self.waited[e][o] = v
            for k, v in self.dma_last.items():
                if self.waited[e].get(k, 0) < v:
                    waits.append((k, v))
                    self.waited[e][k] = v
            if waits:
                self.ops[e].append((waits, None, None))

    def wait_all(self, eng, bufs):
        bufs = [x.b if isinstance(x, T) else x for x in bufs]
        waits = self._deps(eng, bufs, ())
        self.ops[eng].append((waits, None, None))

    def emit(self, block):
        sems = self.sems

        def run(e, lst):
            for waits, fn, inc in lst:
                for k, v in waits:
                    e.wait_ge(sems[k], v)
                if fn is not None:
                    fn(e).then_inc(sems[inc[0]], inc[1])

        @block.tensor
        def _(e):
            run(e, self.ops["pe"])

        @block.scalar
        def _(e):
            run(e, self.ops["act"])

        @block.vector
        def _(e):
            run(e, self.ops["dve"])

        @block.gpsimd
        def _(e):
            run(e, self.ops["pool"])

        @block.sync
        def _(e):
            run(e, self.ops["sp"])


def _np_bucket(dist):
    n = np.maximum(dist, 0)
    nf = np.maximum(n, 1).astype(np.float32)
    large = 16 + (np.log(nf / np.float32(16)) / np.float32(math.log(64)) * np.float32(16)).astype(np.int32)
    large = np.minimum(large, 31)
    return np.where(n < 16, n, large)


def _host_consts():
    c = {}
    oh = np.zeros((33, LW), np.float32)
    d = np.arange(LW) - 511
    b = _np_bucket(d)
    for i in range(LW):
        if d[i] < 0:
            oh[32, i] = 1.0
        else:
            oh[b[i], i] = 1.0
    c["c_oh_main"] = oh
    ohd = np.zeros((33, 3 * 512), np.float32)
    for gi, dil in enumerate((1, 4, 16)):
        for i in range(512):
            s = i - 127
            if 0 <= s <= 128:
                ohd[_np_bucket(np.array([s * dil]))[0], gi * 512 + i] = 1.0
            else:
                ohd[32, gi * 512 + i] = 1.0
    c["c_oh_dil"] = ohd
    koh = np.zeros((8, S), np.float32)
    for n in range(8):
        koh[n, n * 256:(n + 1) * 256] = 1.0
    c["c_koh"] = koh
    gm = np.zeros((2, 16, 8), np.float32)
    om = np.ones((2, 16, 8), np.float32)
    for t in range(16):
        bq = t // 2
        gm[:, t, bq:] = -1e30
        om[:, t, bq] = 0.0
    c["c_gmask"] = np.broadcast_to(gm.reshape(1, 256), (128, 256)).copy()
    c["c_omask"] = np.broadcast_to(om.reshape(1, 256), (128, 256)).copy()
    half = 32
    freq = (np.float32(10000.0) ** (-np.arange(half, dtype=np.float32) / np.float32(half))).astype(np.float32)
    ang = np.arange(S, dtype=np.float32)[None, :] * freq[:, None]
    cs, sn = np.cos(ang).astype(np.float32), np.sin(ang).astype(np.float32)
    c["c_ropec"] = np.concatenate([cs, cs], 0)
    c["c_ropes"] = np.concatenate([-sn, sn], 0)
    tri = (np.arange(128)[None, :] >= np.arange(128)[:, None]).astype(np.float32)
    c["c_tri"] = tri
    return c


import os as _os
DIL_SPLIT = bool(_os.environ.get("DIL_SPLIT"))
FMA_ENG = "dve"


class _Stop(Exception):
    pass


def build_program(n_layers=DEPTH, debug_h=False, stop_at=None):
    nc = bass.Bass("TRN2", target_bir_lowering=False)

    stop_state = {"P": None}

    def ckpt(name):
        if stop_at == name:
            stop_state["P"].stopped = True

    dram_in = {}

    def din(name, shape):
        dram_in[name] = nc.dram_tensor(name, list(shape), F32, kind="ExternalInput").ap()
        return dram_in[name]

    x_d = din("x", [S, D])
    rel_d = din("rel_bias", [32, 20])
    en1_d = din("even_norm1", [2, D])
    ewin_d = din("even_w_in", [2, D, 3072])
    dlam_d = din("diff_lambda", [2, 4, 64])
    dsub_d = din("diff_subln", [2, 128])
    ewout_d = din("even_w_out", [2, D, D])
    on1_d = din("odd_norm1", [2, D])
    owin_d = din("odd_w_in", [2, D, 1984])
    qn_d = din("mla_q_norm", [2, 256])
    wuq_d = din("mla_w_uq", [2, 256, 768])
    kvn_d = din("mla_kv_norm", [2, 128])
    wukv_d = din("mla_w_ukv", [2, 128, 1024])
    owout_d = din("odd_w_out", [2, D, D])
    fn_d = din("ffn_norm", [4, D])
    fwin_d = din("ffn_w_in", [4, D, 2 * DFF])
    fcw_d = din("ffn_conv_w", [4, 3, DFF])
    fcb_d = din("ffn_conv_b", [4, DFF])
    fwout_d = din("ffn_w_out", [4, DFF, D])
    fin_d = din("final_norm", [D])
    c_oh_main = din("c_oh_main", [33, LW])
    c_oh_dil = din("c_oh_dil", [33, 1536])
    c_koh = din("c_koh", [8, S])
    c_gmask = din("c_gmask", [128, 256])
    c_omask = din("c_omask", [128, 256])
    c_ropec = din("c_ropec", [64, S])
    c_ropes = din("c_ropes", [64, S])
    c_tri = din("c_tri", [128, 128])
    out_d = nc.dram_tensor("out", [S, D], F32, kind="ExternalOutput").ap()
    m_main = nc.dram_tensor("m_main", [12 * 128, LW], BF16)
    m_dil = nc.dram_tensor("m_dil", [24 * 128, 512], BF16)
    B_mmain = [Buf("mmain%d" % i) for i in range(12)]
    B_mdil = [Buf("mdil%d" % i) for i in range(24)]
    B_out = Buf("out")

    with ExitStack() as st:
        P = Prog(nc)
        stop_state["P"] = P
        P.alloc_sems(st)
        st.enter_context(nc.allow_non_contiguous_dma("small strided parameter loads"))

        uniq = {"i": 0}

        def sbt(stack, name, shape, dt):
            uniq["i"] += 1
            name = "%s_%d" % (name, uniq["i"])
            return T(stack.enter_context(nc.sbuf_tensor(name, list(shape), dt)), name)

        def mm(out, lhsT, rhs, start, stop, r, w):
            P.op("pe", lambda e: e.matmul(out, lhsT=lhsT, rhs=rhs, start=start, stop=stop,
                                          skip_group_check=True), r, w)

        def trp(out, in_, ident, r, w):
            P.op("pe", lambda e: e.transpose(out=out, in_=in_, identity=ident), r, w)

        def act(out, in_, func, r, w, bias=None, scale=None, accum=None):
            kw = {}
            if bias is not None:
                kw["bias"] = bias
            if scale is not None:
                kw["scale"] = scale
            if accum is not None:
                kw["accum_out"] = accum
            P.op("act", lambda e: e.activation(out=out, in_=in_, func=func, **kw), r, w)

        def tt(out, in0, in1, op, r, w, eng="dve"):
            P.op(eng, lambda e: e.tensor_tensor(out=out, in0=in0, in1=in1, op=op), r, w)

        def ts(out, in0, s1, s2, op0, op1, r, w, eng="dve"):
            if s2 is None:
                P.op(eng, lambda e: e.tensor_scalar(out=out, in0=in0, scalar1=s1, scalar2=None, op0=op0), r, w)
            else:
                P.op(eng, lambda e: e.tensor_scalar(out=out, in0=in0, scalar1=s1, scalar2=s2, op0=op0, op1=op1), r, w)

        def stt(out, in0, scalar, in1, op0, op1, r, w, eng="dve"):
            P.op(eng, lambda e: e.scalar_tensor_tensor(out=out, in0=in0, scalar=scalar, in1=in1, op0=op0, op1=op1), r, w)

        def cp(out, in_, r, w, eng="dve"):
            P.op(eng, lambda e: e.tensor_copy(out=out, in_=in_), r, w)

        def recip(out, in_, r, w):
            act(out, in_, AF.Ln, r, w)
            act(out, out, AF.Exp, list(w), w, scale=-1.0)

        def memset(ap, val, w, eng="pool"):
            P.op(eng, lambda e: e.memset(ap, val), (), w)

        def dma(q, out, in_, r, w):
            P.dma(q, lambda e: e.dma_start(out=out, in_=in_), r, w)

        def run_pipeline(tasks, depth):
            n = len(tasks)
            for i in range(min(depth, n)):
                tasks[i][0]()
            for i in range(n):
                if i + depth < n:
                    tasks[i + depth][0]()
                tasks[i][1]()

        def rope2(dst_ap, dst_T, ps, col0, tok0, n, rope_fn):
            rope_fn(dst_ap, dst_T, ps, col0, tok0, n)

        h = sbt(st, "h", [128, NT, D], F32)
        hB = [Buf("h%d" % t) for t in range(NT)]
        hnT = sbt(st, "hnT", [128, 8, S], BF16)
        idf = sbt(st, "idf", [128, 128], F32)
        idb = sbt(st, "idb", [128, 128], BF16)
        ones_bf = sbt(st, "ones_bf", [128, 128], BF16)
        onesA = sbt(st, "onesA", [128, 128], BF16)
        onesB = sbt(st, "onesB", [128, 128], BF16)
        ones_f = sbt(st, "ones_f", [128, 128], F32)
        tri_bf = sbt(st, "tri_bf", [128, 128], BF16)
        gmask = sbt(st, "gmask", [128, 256], F32)
        omask = sbt(st, "omask", [128, 256], F32)
        convw = sbt(st, "convw", [128, 16 * NCC], F32)
        small = sbt(st, "small", [128, 64], F32)
        lamb = sbt(st, "lamb", [128, 2 * 256], F32)
        stat = sbt(st, "stat", [128, 64], F32)
        PS = [T(st.enter_context(nc.psum_tensor("ps%d" % i, [128, 512], F32)), "ps%d" % i) for i in range(8)]

        memset(idf[:], 0.0, [idf])
        P.op("pool", lambda e: e.affine_select(out=idf[:], in_=idf[:], pattern=[[-1, 128]],
                                               compare_op=ALU.not_equal, fill=1.0, base=0,
                                               channel_multiplier=1), [idf], [idf])
        cp(idb[:, :], idf[:, :], [idf], [idb])
        memset(ones_bf[:], 1.0, [ones_bf])
        memset(ones_f[:], 1.0, [ones_f])
        memset(onesA[:], 0.0, [onesA])
        memset(onesA[:, 0:64], 1.0, [onesA])
        memset(onesB[:], 0.0, [onesB])
        memset(onesB[:, 64:128], 1.0, [onesB])
        dma("pool", tri_bf[:], c_tri, [], [tri_bf])
        dma("sp", gmask[:], c_gmask, [], [gmask])
        dma("sp", omask[:], c_omask, [], [omask])
        for t in range(NT):
            dma("act", h[:, t, :], x_d[t * 128:(t + 1) * 128, :], [], [hB[t]])
        with ExitStack() as sc:
            raws = [sbt(sc, "craw%d" % i, [128, 128], F32) for i in range(4)]
            for i_ in range(4):
                memset(raws[i_][:, :], 0.0, [raws[i_]], eng="dve")
            vecs = [fcw_d[li, k] for li in range(4) for k in range(3)] + [fcb_d[li] for li in range(4)]
            for v, vec in enumerate(vecs):
                tl, j = v // 5, v % 5
                dma("sp", raws[tl][j * NCC:(j + 1) * NCC, :], vec.rearrange("(c p) -> c p", p=128), [raws[tl]], [raws[tl]])
            for tl in range(4):
                nv = min(5, 16 - tl * 5)
                ps = PS[tl]
                trp(ps[:, 0:128], raws[tl][:, :], idf[:, :], [raws[tl], idf], [ps])
                cp(convw[:, tl * 5 * NCC:(tl * 5 + nv) * NCC], ps[:, 0:nv * NCC], [ps], [convw])
            P.barrier()
        for i in range(2):
            dma("sp", small[:, i:i + 1], dsub_d[i].rearrange("(c p) -> p c", p=128), [], [small])
            dma("sp", small[:, 2 + 2 * i:4 + 2 * i], qn_d[i].rearrange("(c p) -> p c", p=128), [], [small])
            dma("sp", small[:, 6 + i:7 + i], kvn_d[i].rearrange("(c p) -> p c", p=128), [], [small])
            dma("sp", lamb[:, i * 256:(i + 1) * 256],
                dlam_d[i].rearrange("a b -> (a b)").partition_broadcast(128), [], [lamb])
        for i in range(2):
            layer = 2 * i
            lam_init = 0.8 - 0.6 * math.exp(-0.3 * layer)
            lp = lamb[:, i * 256:(i + 1) * 256]
            tt(lamb[:, i * 256:i * 256 + 64], lamb[:, i * 256:i * 256 + 64], lamb[:, i * 256 + 64:i * 256 + 128], ALU.mult, [lamb], [lamb])
            tt(lamb[:, i * 256 + 128:i * 256 + 192], lamb[:, i * 256 + 128:i * 256 + 192], lamb[:, i * 256 + 192:i * 256 + 256], ALU.mult, [lamb], [lamb])
            P.op("dve", lambda e, i=i: e.reduce_sum(out=small[:, 16 + 2 * i:17 + 2 * i], in_=lamb[:, i * 256:i * 256 + 64], axis=AX.X), [lamb], [small])
            P.op("dve", lambda e, i=i: e.reduce_sum(out=small[:, 17 + 2 * i:18 + 2 * i], in_=lamb[:, i * 256 + 128:i * 256 + 192], axis=AX.X), [lamb], [small])
            act(small[:, 20 + 2 * i:22 + 2 * i], small[:, 16 + 2 * i:18 + 2 * i], AF.Exp, [small], [small])
            tt(small[:, 8 + i:9 + i], small[:, 21 + 2 * i:22 + 2 * i], small[:, 20 + 2 * i:21 + 2 * i], ALU.subtract, [small], [small])
            ts(small[:, 8 + i:9 + i], small[:, 8 + i:9 + i], -lam_init, None, ALU.add, None, [small], [small])
            ts(small[:, 12 + i:13 + i], small[:, i:i + 1], 1.0 - lam_init, None, ALU.mult, None, [small], [small])

        f_main = nc.dram_tensor("f_main", [20, LW], BF16)
        f_dil = nc.dram_tensor("f_dil", [20, 1536], BF16)
        B_fmain = Buf("fmain")
        B_fdil = Buf("fdil")
        with ExitStack() as s0:
            tab = sbt(s0, "tab", [33, 20], F32)
            ohm = sbt(s0, "ohm", [33, LW], F32)
            ohd = sbt(s0, "ohd", [33, 1536], F32)
            fsb = sbt(s0, "fsb", [20, LW], BF16)
            fsd = sbt(s0, "fsd", [20, 1536], BF16)
            memset(tab[:], NEGBIG, [tab])
            dma("sp", tab[0:32, :], rel_d, [tab], [tab])
            dma("sp", ohm[:], c_oh_main, [], [ohm])
            dma("sp", ohd[:], c_oh_dil, [], [ohd])
            k = 0
            for c in range(LW // 512):
                ps = PS[k % 8]
                k += 1
                mm(ps[0:20, :], tab[0:33, 0:20], ohm[0:33, c * 512:(c + 1) * 512], True, True, [tab, ohm], [ps])
                act(fsb[0:20, c * 512:(c + 1) * 512], ps[0:20, :], AF.Exp, [ps], [fsb])
            for gi in range(3):
                ps = PS[k % 8]
                k += 1
                mm(ps[0:20, :], tab[0:33, 0:20], ohd[0:33, gi * 512:(gi + 1) * 512], True, True, [tab, ohd], [ps])
                act(fsd[0:20, gi * 512:(gi + 1) * 512], ps[0:20, :], AF.Exp, [ps], [fsd])
            dma("sp", f_main.ap()[:, :], fsb[0:20, :], [fsb], [B_fmain])
            dma("sp", f_dil.ap()[:, :], fsd[0:20, :], [fsd], [B_fdil])
            P.barrier()
        def emit_table_broadcasts():
            for hh in range(12):
                src = bass.AP(f_main, hh * LW, [[0, 128], [1, LW]])
                dma("sp", m_main.ap()[hh * 128:(hh + 1) * 128, :], src, [B_fmain], [B_mmain[hh]])
            for hh in range(8):
                for gi in range(3):
                    idx = hh * 3 + gi
                    src = bass.AP(f_dil, (12 + hh) * 1536 + gi * 512, [[0, 128], [1, 512]])
                    dma("sp", m_dil.ap()[idx * 128:(idx + 1) * 128, :], src, [B_fdil], [B_mdil[idx]])

        def load_G(dst, head):
            src = bass.AP(m_main, head * 128 * LW + 127, [[LW - 1, 128], [1, GW]])
            dma("sp", dst[:, 0:GW], src, [B_mmain[head]], [dst])

        def load_Gdil(dst, idx):
            src = bass.AP(m_dil, idx * 128 * 512 + 127, [[511, 128], [1, 256]])
            dma("sp", dst, src, [B_mdil[idx]], [])

        gain_src = [en1_d[0], en1_d[1], on1_d[0], on1_d[1], fn_d[0], fn_d[1], fn_d[2], fn_d[3]]

        def norm_transpose(gidx, tiles):
            with ExitStack() as sn:
                norm_body(sn, gidx, tiles)
            P.barrier()

        def norm_body(sn, gidx, tiles):
            if True:
                gb = sbt(sn, "gb", [128, D], F32)
                junk = sbt(sn, "junk", [128, D], BF16)
                xg = [sbt(sn, "xg%d" % i, [128, D], BF16) for i in range(4)]
                dma("sp", gb[:, :], gain_src[gidx].partition_broadcast(128), [], [gb])
                sB = [Buf("st%d" % t) for t in range(NT)]
                hnB = [Buf("hn%d" % t) for t in range(NT)]
                memset(stat[:, 0:16], 0.0, sB, eng="dve")
                def evac(i_, t):
                    ps = PS[i_ % 8]
                    src = ps[:, :].bitcast(BF16)[:, :].rearrange("p (c n) -> p c n", n=128)
                    if i_ % 2 == 0:
                        cp(hnT[:, :, t * 128:(t + 1) * 128], src, [ps], [hnB[t]])
                    else:
                        act(hnT[:, :, t * 128:(t + 1) * 128], src, AF.Copy, [ps], [hnB[t]])

                for i_, t in enumerate(tiles):
                    x_ = xg[i_ % 4]
                    act(junk[:, :], h[:, t, :], AF.Square, [hB[t]], [junk, sB[t]], accum=stat[:, t:t + 1])
                    act(stat[:, 16 + t:17 + t], stat[:, t:t + 1], AF.Ln, [sB[t]], [sB[t]], bias=EPS, scale=1.0 / D)
                    act(stat[:, 32 + t:33 + t], stat[:, 16 + t:17 + t], AF.Exp, [sB[t]], [sB[t]], scale=-0.5)
                    stt(x_[:, :], h[:, t, :], stat[:, 32 + t:33 + t], gb[:, :], ALU.mult, ALU.mult, [hB[t], sB[t], gb], [x_])
                    ps = PS[i_ % 8]
                    psb = ps[:, :].bitcast(BF16)
                    for c in range(8):
                        trp(psb[:, c * 128:(c + 1) * 128], x_[:, c * 128:(c + 1) * 128], idb[:, :], [x_, idb], [ps])
                    if i_ >= 1:
                        evac(i_ - 1, tiles[i_ - 1])
                evac(len(tiles) - 1, tiles[-1])

        def load_w(dst, w2d, col0, ncol, nk):
            src = w2d[:, col0:col0 + ncol].rearrange("(kc p) n -> p kc n", p=128)
            dma("pool", dst[:, 0:nk, 0:ncol], src, [], [dst])

        def proj_fm(wslab, wc0, evac, kchunks=8, src=None, banks=(0, 1, 2), M=128):
            src = hnT if src is None else src
            for tg in range(4):
                ps = PS[banks[tg % len(banks)]]
                for kc in range(kchunks):
                    mm(ps[0:M, :], wslab[:, kc, wc0:wc0 + M], src[:, kc, tg * 512:(tg + 1) * 512],
                       kc == 0, kc == kchunks - 1, [wslab, src], [ps])
                evac(tg, ps)

        rot = {"s": 0, "p": 0}
        pending = []

        def flush_pending():
            while pending:
                pending.pop(0)()

        def attn_tasks(g, k_of, q_of, extra_of, Gt, v_of, ones_ap, numb, denb, pexp, ptb, exp_scale, kq_reads, v_reads,
                       mla=False, on_done=None):
            nj = 4 * g + 4
            tasks = []
            for j in range(nj):
                c0 = max(0, j - 4 * g) * 128
                N = 512 - c0
                kj = k_of(j)
                qj = q_of(g, c0, N)
                ex = extra_of(j, g, c0, N) if extra_of is not None else None
                vj = v_of(j)
                gc = (4 * g - j + 3) * 128 + c0
                st_ = {}

                def score_fn(N=N, kj=kj, qj=qj, ex=ex, st_=st_):
                    sb = PS[rot["s"] % 4]
                    rot["s"] += 1
                    st_["sb"] = sb
                    mm(sb[:, 0:N], kj, qj, True, ex is None, kq_reads, [sb])
                    if ex is not None:
                        mm(sb[:, 0:N], ex[0], ex[1], False, True, kq_reads, [sb])

                def rest_fn(j=j, c0=c0, N=N, vj=vj, gc=gc, st_=st_):
                    sb = st_["sb"]
                    if mla:
                        pt = ptb[rot["p"] % len(ptb)]
                    else:
                        allb = pexp + ptb
                        pt = allb[rot["p"] % len(allb)]
                    rot["p"] += 1
                    act(pt[:, 0:N], sb[:, 0:N], AF.Exp, [sb], [pt], scale=exp_scale)
                    if mla:
                        if j >= 4 * g:
                            tt(pt[:, 0:128], pt[:, 0:128], tri_bf[:, :], ALU.mult, [pt, tri_bf], [pt])
                    else:
                        tt(pt[:, 0:N], pt[:, 0:N], Gt[:, gc:gc + N], ALU.mult, [pt, Gt], [pt])
                    mm(numb[:, c0:512], vj, pt[:, 0:N], j == 0, j == nj - 1, [pt] + v_reads, [numb])
                    if denb is not None:
                        mm(denb[:, c0:512], ones_ap, pt[:, 0:N], j == 0, j == nj - 1, [pt, ones_bf], [denb])
                    if j == min(2, nj - 1):
                        flush_pending()
                    if j == nj - 1 and on_done is not None:
                        on_done()
                tasks.append((score_fn, rest_fn))
            return tasks

        def w_out_residual(stack, mixT, w2d):
            wo = [sbt(stack, "wo%d" % i, [128, 8, 512], BF16) for i in range(2)]
            for ch in range(2):
                load_w(wo[ch], w2d, ch * 512, 512, 8)
            k = 0
            for t in range(NT):
                for ch in range(2):
                    ps = PS[k % 8]
                    k += 1
                    for c in range(8):
                        mm(ps[:, :], mixT[:, c, t * 128:(t + 1) * 128], wo[ch][:, c, :], c == 0, c == 7, [mixT, wo[ch]], [ps])
                    tt(h[:, t, ch * 512:(ch + 1) * 512], ps[:, :], h[:, t, ch * 512:(ch + 1) * 512], ALU.add, [ps, hB[t]], [hB[t]])

        def even_mixer(li):
            i2 = li // 2
            w_in = ewin_d[i2]
            with ExitStack() as sm:
                mixT = sbt(sm, "mixT", [128, 8, S], BF16)
                norm_transpose(i2, list(range(NT)))
                if li == 0:
                    emit_table_broadcasts()
                ckpt("norm")
                with ExitStack() as sa:
                    QTz = [sbt(sa, "QTz%d" % i, [128, S], BF16) for i in range(2)]
                    KTz = [sbt(sa, "KTz%d" % i, [128, S], BF16) for i in range(2)]
                    VP = sbt(sa, "VP", [128, NT, 2, 128], BF16)
                    QTf = [sbt(sa, "QTf%d" % i, [128, 512], F32) for i in range(2)]
                    ksum = sbt(sa, "ksum", [128, 8], F32)
                    gm = sbt(sa, "gm", [128, 256], F32)
                    top = sbt(sa, "top", [128, 256], F32)
                    pen = sbt(sa, "pen", [128, 256], F32)
                    Gt = [sbt(sa, "Gt%d" % i, [128, GW], BF16) for i in range(2)]
                    wq = [sbt(sa, "wq%d" % i, [128, 8, 128], BF16) for i in range(2)]
                    wk = [sbt(sa, "wk%d" % i, [128, 8, 128], BF16) for i in range(2)]
                    wv = [sbt(sa, "wv%d" % i, [128, 8, 128], BF16) for i in range(2)]
                    pexp = [sbt(sa, "pexp%d" % i, [128, 512], BF16) for i in range(3)]
                    ptb = [sbt(sa, "ptb%d" % i, [128, 512], BF16) for i in range(3)]
                    rd = [sbt(sa, "rd%d" % i, [128, 512], F32) for i in range(2)]
                    o0 = sbt(sa, "o0", [128, 512], F32)
                    o1 = sbt(sa, "o1", [128, 512], F32)
                    sq = sbt(sa, "sq", [128, 512], F32)

                    memset(VP[:, :, :, :].rearrange("p a b c -> p (a b c)"), 1.0, [VP], eng="dve")
                    for i_ in range(2):
                        memset(QTz[i_][:, :], 0.0, [QTz[i_]], eng="dve")
                        memset(KTz[i_][:, :], 0.0, [KTz[i_]], eng="dve")
                    dma("pool", KTz[0][64:72, :], c_koh, [KTz[0]], [KTz[0]])
                    dma("pool", KTz[1][0:8, :], c_koh, [KTz[1]], [KTz[1]])
                    units = [("moba", fc) for fc in range(4)] + [("diff", hd) for hd in range(4)]

                    def unit_cols(u):
                        kind, idx = u
                        if kind == "moba":
                            return idx * 128, 512 + idx * 128, 1024 + idx * 128
                        return 1536 + idx * 128, 2048 + idx * 128, 2560 + idx * 128

                    def load_unit_w(ui):
                        qc, kc_, vc = unit_cols(units[ui])
                        load_w(wq[ui % 2], w_in, qc, 128, 8)
                        load_w(wk[ui % 2], w_in, kc_, 128, 8)
                        load_w(wv[ui % 2], w_in, vc, 128, 8)

                    load_unit_w(0)
                    gslot = 0
                    moba_rot = {"i": 0}
                    for ui, u in enumerate(units):
                        kind, idx = u
                        if ui + 1 < len(units):
                            load_unit_w(ui + 1)
                        Wq, Wk, Wv = wq[ui % 2], wk[ui % 2], wv[ui % 2]

                        if kind == "diff" and idx == 0:
                            memset(QTz[0][64:128, :], 0.0, [QTz[0]], eng="dve")
                            memset(QTz[1][0:64, :], 0.0, [QTz[1]], eng="dve")

                        def evac_k(tg, ps):
                            cp(KTz[0][0:64, tg * 512:(tg + 1) * 512], ps[0:64, :], [ps], [KTz[0]])
                            cp(KTz[1][64:128, tg * 512:(tg + 1) * 512], ps[64:128, :], [ps], [KTz[1]])
                            if kind == "moba":
                                P.op("dve", lambda e: e.reduce_sum(
                                    out=ksum[:, 2 * tg:2 * tg + 2],
                                    in_=ps[:, :].rearrange("p (b k) -> p b k", k=256), axis=AX.X), [ps], [ksum])
                        proj_fm(Wk, 0, evac_k)
                        ckpt("u%dk" % ui)
                        ckpt("u%dv" % ui)

                        def evac_q(tg, ps):
                            ts(QTz[0][0:64, tg * 512:(tg + 1) * 512], ps[0:64, :], 0.125, None, ALU.mult, None, [ps], [QTz[0]])
                            ts(QTz[1][64:128, tg * 512:(tg + 1) * 512], ps[64:128, :], 0.125, None, ALU.mult, None, [ps], [QTz[1]])
                            if kind == "moba":
                                qf = QTf[tg % 2]
                                ts(qf[:, :], ps[:, :], 0.125, None, ALU.mult, None, [ps], [qf])
                                for hh in range(2):
                                    for tt_ in range(4):
                                        t = tg * 4 + tt_
                                        col = (hh * 16 + t) * 8
                                        mm(PS[7][:, col:col + 8], qf[hh * 64:(hh + 1) * 64, tt_ * 128:(tt_ + 1) * 128],
                                           ksum[hh * 64:(hh + 1) * 64, 0:8], True, True, [qf, ksum], [PS[7]])
                        proj_fm(Wq, 0, evac_q)
                        ckpt("u%dq" % ui)
                        for tq in range(4):
                            ps = PS[3 + tq % 2]
                            for tt_ in range(4):
                                t = tq * 4 + tt_
                                for kc in range(8):
                                    mm(ps[:, tt_ * 128:(tt_ + 1) * 128], hnT[:, kc, t * 128:(t + 1) * 128], Wv[:, kc, :],
                                       kc == 0, kc == 7, [hnT, Wv], [ps])
                            psv = ps[:, :].rearrange("p (a b) -> p a b", b=128)
                            if kind == "moba":
                                act(VP[:, tq * 4:(tq + 1) * 4, 0, 0:64], psv[:, :, 0:64], AF.Copy, [ps], [VP])
                                act(VP[:, tq * 4:(tq + 1) * 4, 1, 64:128], psv[:, :, 64:128], AF.Copy, [ps], [VP])
                            else:
                                act(VP[:, tq * 4:(tq + 1) * 4, 0, :], psv, AF.Copy, [ps], [VP])


                        if kind == "moba":
                            tt(gm[:, :], PS[7][:, 0:256], gmask[:, :], ALU.add, [PS[7], gmask], [gm])
                            for q_ in range(32):
                                P.op("dve", lambda e, q_=q_: e.max(out=top[:, q_ * 8:(q_ + 1) * 8], in_=gm[:, q_ * 8:(q_ + 1) * 8]),
                                     [gm], [top])
                            thr = top[:, :].rearrange("p (a b) -> p a b", b=8)[:, :, 2:3].to_broadcast([128, 32, 8])
                            tt(pen[:, :].rearrange("p (a b) -> p a b", b=8), gm[:, :].rearrange("p (a b) -> p a b", b=8), thr,
                               ALU.is_ge, [gm, top], [pen])
                            ckpt("u%dsel" % ui)
                            ts(pen[:, :], pen[:, :], -1.0, -NEGBIG, ALU.add, ALU.mult, [pen], [pen])
                            tt(pen[:, :], pen[:, :], omask[:, :], ALU.mult, [pen, omask], [pen])
                            for hh in range(2):
                                for grp in range(4):
                                    ps = PS[5 + grp % 2]
                                    for tt_ in range(4):
                                        t = grp * 4 + tt_
                                        col = (hh * 16 + t) * 8
                                        trp(ps[0:8, tt_ * 128:(tt_ + 1) * 128], pen[:, col:col + 8], idf[:, :], [pen, idf], [ps])
                                    prow = 64 if hh == 0 else 0
                                    cp(QTz[hh][prow:prow + 8, grp * 512:(grp + 1) * 512], ps[0:8, :], [ps], [QTz[hh]])

                        ckpt("u%dproj" % ui)
                        utasks = []
                        if kind == "moba":
                            for hh in range(2):
                                head = idx * 2 + hh
                                G = Gt[gslot % 2]
                                gslot += 1
                                load_G(G, head)
                                lo, hi = hh * 64, (hh + 1) * 64
                                dlo, dhi = (1 - hh) * 64, (2 - hh) * 64
                                for g in range(4):
                                    numb = PS[4 + (moba_rot["i"] % 4)]
                                    moba_rot["i"] += 1

                                    def epi(g=g, numb=numb, lo=lo, hi=hi, dlo=dlo, dhi=dhi, idx=idx):
                                        r_ = rd[g % 2]
                                        recip(r_[lo:hi, :], numb[dlo:dhi, :], [numb], [r_])
                                        tt(mixT[lo:hi, idx, g * 512:(g + 1) * 512], numb[lo:hi, :], r_[lo:hi, :], ALU.mult,
                                           [numb, r_], [mixT])
                                    utasks += attn_tasks(
                                        g,
                                        lambda j, hh=hh: KTz[hh][:, j * 128:(j + 1) * 128],
                                        lambda g_, c0, N, hh=hh: QTz[hh][:, g_ * 512 + c0:g_ * 512 + c0 + N],
                                        None,
                                        G, lambda j, hh=hh: VP[:, j, hh, :], None, numb, None, pexp, ptb, 1.0,
                                        [KTz[hh], QTz[hh]], [VP], on_done=lambda epi=epi: pending.append(epi))
                        else:
                            head = 8 + idx
                            G = Gt[gslot % 2]
                            gslot += 1
                            load_G(G, head)
                            for g in range(4):
                                for c in range(2):
                                    numb, denb = (PS[4], PS[5]) if c == 0 else (PS[6], PS[7])

                                    def done(g=g, idx=idx):
                                        act(rd[0][:, :], PS[5][:, :], AF.Ln, [PS[5]], [rd[0]])
                                        cp(o0[:, :], PS[4][:, :], [PS[4]], [o0])
                                        act(rd[1][:, :], PS[7][:, :], AF.Ln, [PS[7]], [rd[1]])
                                        cp(o1[:, :], PS[6][:, :], [PS[6]], [o1])

                                        def epi(g=g, idx=idx):
                                            act(rd[0][:, :], rd[0][:, :], AF.Exp, [rd[0]], [rd[0]], scale=-1.0)
                                            act(rd[1][:, :], rd[1][:, :], AF.Exp, [rd[1]], [rd[1]], scale=-1.0)
                                            tt(o0[:, :], o0[:, :], rd[0][:, :], ALU.mult, [o0, rd[0]], [o0])
                                            tt(o1[:, :], o1[:, :], rd[1][:, :], ALU.mult, [o1, rd[1]], [o1])
                                            stt(o0[:, :], o1[:, :], small[:, 8 + i2:9 + i2], o0[:, :], ALU.mult, ALU.add, [o1, o0, small], [o0])
                                            act(sq[:, :], o0[:, :], AF.Square, [o0], [sq])
                                            ssb = PS[7]
                                            mm(ssb[:, :], ones_f[:, :], sq[:, :], True, True, [ones_f, sq], [ssb])
                                            act(sq[:, :], ssb[:, :], AF.Ln, [ssb], [sq], bias=EPS, scale=1.0 / 128)
                                            act(sq[:, :], sq[:, :], AF.Exp, [sq], [sq], scale=-0.5)
                                            stt(mixT[:, 4 + idx, g * 512:(g + 1) * 512], o0[:, :], small[:, 12 + i2:13 + i2], sq[:, :],
                                                ALU.mult, ALU.mult, [o0, small, sq], [mixT])
                                        pending.append(epi)
                                    utasks += attn_tasks(
                                        g,
                                        lambda j, c=c: KTz[c][:, j * 128:(j + 1) * 128],
                                        lambda g_, c0, N, c=c: QTz[c][:, g_ * 512 + c0:g_ * 512 + c0 + N],
                                        None, G, lambda j: VP[:, j, 0, :], ones_bf[:, :], numb, denb, pexp, ptb, 1.0,
                                        [KTz[c], QTz[c]], [VP], on_done=(done if c == 1 else None))
                        run_pipeline(utasks, 3)
                        flush_pending()
                        ckpt("u%d" % ui)
                P.barrier()
                with ExitStack() as sw:
                    w_out_residual(sw, mixT, ewout_d[i2])
                    norm_body(sw, 4 + li, list(range(NT)))
            P.barrier()

        def conv_ffn(li):
            w_in = fwin_d[li]
            w_out = fwout_d[li]
            with ExitStack() as sf:
                actT = sbt(sf, "actT", [128, NCC, 1024], BF16)
                wug = [sbt(sf, "wug%d" % i, [128, 8, 256], BF16) for i in range(3)]
                wo2 = [sbt(sf, "wo2_%d" % i, [128, NCC, 256], BF16) for i in range(2)]
                graw = [sbt(sf, "graw%d" % i, [128, 1040], F32) for i in range(2)]
                A = [sbt(sf, "A%d" % i, [128, 512], F32) for i in range(2)]
                Ag = [sbt(sf, "Ag%d" % i, [128, 512], F32) for i in range(2)]
                gtail = sbt(sf, "gtail", [128, NCC, 2], F32)

                def load_pair(slot, cc):
                    src_u = w_in[:, cc * 128:(cc + 1) * 128].rearrange("(kc p) n -> p kc n", p=128)
                    src_g = w_in[:, DFF + cc * 128:DFF + (cc + 1) * 128].rearrange("(kc p) n -> p kc n", p=128)
                    dma("pool", wug[slot][:, :, 0:128], src_u, [], [wug[slot]])
                    dma("pool", wug[slot][:, :, 128:256], src_g, [], [wug[slot]])

                k = 0
                ffn_tail = []
                for half in range(2):
                    tiles = list(range(half * 8, half * 8 + 8))
                    load_pair(0, 0)
                    load_pair(1, 1)
                    for cc in range(NCC):
                        if cc + 2 < NCC:
                            load_pair((cc + 2) % 3, cc + 2)
                        W = wug[cc % 3]
                        gr = graw[cc % 2]
                        if half == 0:
                            memset(gr[:, 14:16], 0.0, [gr], eng="dve")
                        else:
                            cp(gr[:, 14:16], gtail[:, cc, :], [gtail], [gr])
                        w0 = convw[:, (li * 3 + 0) * NCC + cc:(li * 3 + 0) * NCC + cc + 1]
                        w1 = convw[:, (li * 3 + 1) * NCC + cc:(li * 3 + 1) * NCC + cc + 1]
                        w2 = convw[:, (li * 3 + 2) * NCC + cc:(li * 3 + 2) * NCC + cc + 1]
                        bb = convw[:, (12 + li) * NCC + cc:(12 + li) * NCC + cc + 1]
                        for tgi in range(2):
                            tok0 = half * 1024 + tgi * 512
                            pu = PS[(k * 2) % 8]
                            pg = PS[(k * 2 + 1) % 8]
                            k += 1
                            for kc in range(8):
                                mm(pg[:, :], W[:, kc, 128:256], hnT[:, kc, tok0:tok0 + 512], kc == 0, kc == 7, [W, hnT], [pg])
                            for kc in range(8):
                                mm(pu[:, :], W[:, kc, 0:128], hnT[:, kc, tok0:tok0 + 512], kc == 0, kc == 7, [W, hnT], [pu])
                            a = A[tgi]
                            ag = Ag[tgi]
                            act(gr[:, 16 + tgi * 512:16 + (tgi + 1) * 512], pg[:, :], AF.Copy, [pg], [gr])
                            ts(a[:, :], gr[:, 16 + tgi * 512:16 + (tgi + 1) * 512], w2, bb, ALU.mult, ALU.add, [gr, convw], [a])
                            stt(a[:, :], gr[:, 15 + tgi * 512:15 + (tgi + 1) * 512], w1, a[:, :], ALU.mult, ALU.add, [gr, a, convw], [a], eng=FMA_ENG)
                            stt(a[:, :], gr[:, 14 + tgi * 512:14 + (tgi + 1) * 512], w0, a[:, :], ALU.mult, ALU.add, [gr, a, convw], [a], eng=FMA_ENG)
                            if ffn_tail:
                                ffn_tail.pop()()

                            def tail(a=a, ag=ag, pu=pu, cc=cc, tgi=tgi):
                                act(ag[:, :], a[:, :], AF.Gelu, [a], [ag])
                                tt(actT[:, cc, tgi * 512:(tgi + 1) * 512], ag[:, :], pu[:, :], ALU.mult, [ag, pu], [actT])
                            ffn_tail.append(tail)
                        if half == 0:
                            cp(gtail[:, cc, :], gr[:, 1038:1040], [gr], [gtail])
                    if ffn_tail:
                        ffn_tail.pop()()
                    load_cols = lambda slot, cq: dma(
                        "pool", wo2[slot][:, :, :],
                        w_out[:, cq * 256:(cq + 1) * 256].rearrange("(cc p) n -> p cc n", p=128), [], [wo2[slot]])
                    load_cols(0, 0)
                    for cq in range(4):
                        if cq + 1 < 4:
                            load_cols((cq + 1) % 2, cq + 1)
                        Wo = wo2[cq % 2]
                        for tl in range(8):
                            t = half * 8 + tl
                            ps = PS[k % 8]
                            k += 1
                            for cc in range(NCC):
                                mm(ps[:, 0:256], actT[:, cc, tl * 128:(tl + 1) * 128], Wo[:, cc, :], cc == 0, cc == NCC - 1,
                                   [actT, Wo], [ps])
                            tt(h[:, t, cq * 256:(cq + 1) * 256], ps[:, 0:256], h[:, t, cq * 256:(cq + 1) * 256], ALU.add,
                               [ps, hB[t]], [hB[t]])
            P.barrier()

        def odd_mixer(li):
            i2 = li // 2
            w_in = owin_d[i2]
            with ExitStack() as sm:
                mixT = sbt(sm, "mixTo", [128, 8, S], BF16)
                norm_transpose(2 + i2, list(range(NT)))
                with ExitStack() as sa:
                    QTz = [sbt(sa, "oQTz%d" % i, [128, S], BF16) for i in range(2)]
                    KTz = [sbt(sa, "oKTz%d" % i, [128, S], BF16) for i in range(2)]
                    for i_ in range(2):
                        memset(QTz[i_][:, :], 0.0, [QTz[i_]], eng="dve")
                        memset(KTz[i_][:, :], 0.0, [KTz[i_]], eng="dve")
                    VS = sbt(sa, "VS", [128, NT, 2, 128], BF16)
                    accn = sbt(sa, "accn", [128, S], F32)
                    accd = sbt(sa, "accd", [128, S], F32)
                    Gd = sbt(sa, "Gd", [128, 3, 2, 256], BF16)
                    wq = [sbt(sa, "owq%d" % i, [128, 8, 128], BF16) for i in range(2)]
                    wk = [sbt(sa, "owk%d" % i, [128, 8, 128], BF16) for i in range(2)]
                    wv = [sbt(sa, "owv%d" % i, [128, 8, 128], BF16) for i in range(2)]
                    pexp = [sbt(sa, "opexp%d" % i, [128, 512], BF16) for i in range(4)]
                    ptb = [sbt(sa, "optb%d" % i, [128, 512], BF16) for i in range(4)]
                    rd = [sbt(sa, "ord%d" % i, [128, 512], F32) for i in range(2)]

                    memset(VS[:, :, :, :].rearrange("p a b c -> p (a b c)"), 0.0, [VS], eng="dve")

                    def load_unit_w(ui):
                        load_w(wq[ui % 2], w_in, ui * 128, 128, 8)
                        load_w(wk[ui % 2], w_in, 512 + ui * 128, 128, 8)
                        load_w(wv[ui % 2], w_in, 1024 + ui * 128, 128, 8)

                    load_unit_w(0)
                    sr = 0
                    vr = {"i": 0}
                    for ui in range(4):
                        if ui + 1 < 4:
                            load_unit_w(ui + 1)
                        Wq, Wk, Wv = wq[ui % 2], wk[ui % 2], wv[ui % 2]
                        def evac_k(tg, ps):
                            cp(KTz[0][0:64, tg * 512:(tg + 1) * 512], ps[0:64, :], [ps], [KTz[0]])
                            cp(KTz[1][64:128, tg * 512:(tg + 1) * 512], ps[64:128, :], [ps], [KTz[1]])

                        def evac_q(tg, ps):
                            ts(QTz[0][0:64, tg * 512:(tg + 1) * 512], ps[0:64, :], 0.125, None, ALU.mult, None, [ps], [QTz[0]])
                            ts(QTz[1][64:128, tg * 512:(tg + 1) * 512], ps[64:128, :], 0.125, None, ALU.mult, None, [ps], [QTz[1]])
                        proj_fm(Wk, 0, evac_k)
                        proj_fm(Wq, 0, evac_q)
                        for hh in range(2):
                            for gi in range(3):
                                idx = (ui * 2 + hh) * 3 + gi
                                src = bass.AP(m_dil, idx * 128 * 512 + 127, [[511, 128], [1, 256]])
                                dma("sp", Gd[:, gi, hh, :], src, [B_mdil[idx]], [Gd])
                        first = True
                        for gi, dil in enumerate((1, 4, 16)):
                            L = S // dil
                            nblk = max(1, L // 128)
                            for r in range(dil):
                                for n0 in range(0, nblk, 4):
                                    nn = min(4, nblk - n0)
                                    ps = PS[(3, 0, 1, 2)[vr["i"] % 4]]
                                    vr["i"] += 1
                                    for n_ in range(nn):
                                        n = n0 + n_
                                        t0 = n * 128 * dil + r
                                        for kc in range(8):
                                            mm(ps[:, n_ * 128:(n_ + 1) * 128],
                                               hnT[:, kc, t0:t0 + 127 * dil + 1:dil], Wv[:, kc, :],
                                               kc == 0, kc == 7, [hnT, Wv], [ps])
                                    slot0 = r * nblk + n0
                                    psv = ps[:, 0:nn * 128].rearrange("p (a b) -> p a b", b=128)
                                    act(VS[:, slot0:slot0 + nn, 0, 0:64], psv[:, :, 0:64], AF.Copy, [ps], [VS])
                                    act(VS[:, slot0:slot0 + nn, 1, 64:128], psv[:, :, 64:128], AF.Copy, [ps], [VS])
                            ckpt("d%dv%d" % (ui, gi))
                            tasks = []
                            for r in range(dil):
                                for seg0 in range(0, nblk, 4):
                                    segn = min(4, nblk - seg0)
                                    W_ = segn * 128
                                    numb, denb = (PS[4], PS[5]) if sr % 2 == 0 else (PS[6], PS[7])
                                    sr += 1
                                    kbs = [kb for kb in range(seg0 - 1, seg0 + segn) if kb >= 0]
                                    for ki, kb in enumerate(kbs):
                                        q0 = max(kb, seg0)
                                        q1 = min(kb + 2, seg0 + segn)
                                        N = (q1 - q0) * 128
                                        xoff = (q0 - kb) * 128
                                        cpos = (q0 - seg0) * 128
                                        kt0 = kb * 128 * dil + r
                                        qt0 = q0 * 128 * dil + r
                                        for hh in range(2):
                                            st_ = {}

                                            def score_fn(N=N, kt0=kt0, qt0=qt0, st_=st_, hh=hh):
                                                sb = PS[rot["s"] % 4]
                                                rot["s"] += 1
                                                st_["sb"] = sb
                                                mm(sb[:, 0:N], KTz[hh][:, kt0:kt0 + 127 * dil + 1:dil],
                                                   QTz[hh][:, qt0:qt0 + (N - 1) * dil + 1:dil], True, True, [KTz[hh], QTz[hh]], [sb])

                                            def rest_fn(N=N, xoff=xoff, cpos=cpos, kb=kb, r=r, st_=st_, numb=numb, denb=denb, hh=hh,
                                                        first_k=(ki == 0 and hh == 0), last_k=(ki == len(kbs) - 1 and hh == 1),
                                                        W_=W_, seg0=seg0, first=first):
                                                sb = st_["sb"]
                                                pe_ = pexp[rot["p"] % 4]
                                                pt = ptb[rot["p"] % 4]
                                                rot["p"] += 1
                                                act(pe_[:, 0:N], sb[:, 0:N], AF.Exp, [sb], [pe_])
                                                tt(pt[:, 0:N], pe_[:, 0:N], Gd[:, gi, hh, xoff:xoff + N], ALU.mult, [pe_, Gd], [pt])
                                                mm(numb[:, cpos:cpos + N], VS[:, r * nblk + kb, hh, :], pt[:, 0:N], first_k, False,
                                                   [pt, VS], [numb])
                                                mm(denb[:, cpos:cpos + N], (onesA if hh == 0 else onesB)[:, :], pt[:, 0:N], first_k, False,
                                                   [pt, onesA, onesB], [denb])
                                                if last_k:
                                                    tq0 = seg0 * 128 * dil + r
                                                    dst_n = accn[:, tq0:tq0 + (W_ - 1) * dil + 1:dil]
                                                    dst_d = accd[:, tq0:tq0 + (W_ - 1) * dil + 1:dil]
                                                    if first:
                                                        cp(dst_n, numb[:, 0:W_], [numb], [accn])
                                                        cp(dst_d, denb[:, 0:W_], [denb], [accd])
                                                    else:
                                                        tt(dst_n, numb[:, 0:W_], dst_n, ALU.add, [numb, accn], [accn])
                                                        tt(dst_d, denb[:, 0:W_], dst_d, ALU.add, [denb, accd], [accd])
                                            tasks.append((score_fn, rest_fn))
                            run_pipeline(tasks, 3)
                            ckpt("d%db%d" % (ui, gi))
                            first = False
                        ckpt("d%dpre" % ui)
                        for g in range(4):
                            r_ = rd[g % 2]
                            recip(r_[:, :], accd[:, g * 512:(g + 1) * 512], [accd], [r_])
                            tt(mixT[:, ui, g * 512:(g + 1) * 512], accn[:, g * 512:(g + 1) * 512], r_[:, :], ALU.mult, [accn, r_], [mixT])
                P.barrier()
                with ExitStack() as sb_:
                    wuq = sbt(sb_, "wuq", [128, 2, 768], BF16)
                    wukv = sbt(sb_, "wukv", [128, 1, 1024], BF16)
                    cqT = sbt(sb_, "cqT", [128, 2, S], BF16)
                    ckvT = sbt(sb_, "ckvT", [128, 1, S], BF16)
                    krT = sbt(sb_, "krT", [128, S], BF16)
                    rcs = sbt(sb_, "rcs", [64, 2 * S], F32)
                    xs1 = [sbt(sb_, "xs1_%d" % i, [64, 512], F32) for i in range(2)]
                    xs2 = [sbt(sb_, "xs2_%d" % i, [64, 512], F32) for i in range(2)]
                    lst = sbt(sb_, "lst", [128, 96], F32)
                    rr = {"i": 0}

                    def rope(dst_ap, dst_T, ps, col0, tok0, n):
                        a = xs1[rr["i"] % 2]
                        b = xs2[rr["i"] % 2]
                        rr["i"] += 1
                        tt(b[0:32, 0:n], ps[32:64, col0:col0 + n], rcs[0:32, S + tok0:S + tok0 + n], ALU.mult, [ps, rcs], [b])
                        tt(b[32:64, 0:n], ps[0:32, col0:col0 + n], rcs[32:64, S + tok0:S + tok0 + n], ALU.mult, [ps, rcs], [b])
                        tt(a[0:64, 0:n], ps[0:64, col0:col0 + n], rcs[0:64, tok0:tok0 + n], ALU.mult, [ps, rcs], [a])
                        tt(dst_ap, a[0:64, 0:n], b[0:64, 0:n], ALU.add, [a, b], [dst_T])

                    memset(krT[64:128, :], 0.0, [krT], eng="dve")
                    load_w(wuq, wuq_d[i2], 0, 768, 2)
                    load_w(wukv, wukv_d[i2], 0, 1024, 1)
                    dma("sp", rcs[0:64, 0:S], c_ropec, [], [rcs])
                    dma("sp", rcs[0:64, S:2 * S], c_ropes, [], [rcs])
                    with ExitStack() as s1:
                        wl = sbt(s1, "wl", [128, 8, 448], BF16)
                        lat = [sbt(s1, "lat%d" % i, [128, 448], F32) for i in range(4)]
                        load_w(wl, w_in, 1536, 448, 8)
                        lB = [Buf("lst%d" % t) for t in range(NT)]
                        memset(lst[:, 0:32], 0.0, lB, eng="dve")
                        lat_tails = []
                        for t in range(NT):
                            ps = PS[t % 4]
                            la = lat[t % 4]
                            for kc in range(8):
                                mm(ps[:, 0:448], hnT[:, kc, t * 128:(t + 1) * 128], wl[:, kc, :], kc == 0, kc == 7, [hnT, wl], [ps])
                            act(la[:, 0:256], ps[:, 0:256], AF.Square, [ps], [la, lB[t]], accum=lst[:, t:t + 1])
                            act(la[:, 256:384], ps[:, 256:384], AF.Square, [ps], [la, lB[t]], accum=lst[:, 16 + t:17 + t])
                            act(lst[:, 32 + t:33 + t], lst[:, t:t + 1], AF.Ln, [lB[t]], [lB[t]], bias=EPS, scale=1.0 / 256)
                            act(lst[:, 32 + t:33 + t], lst[:, 32 + t:33 + t], AF.Exp, [lB[t]], [lB[t]], scale=-0.5)
                            act(lst[:, 48 + t:49 + t], lst[:, 16 + t:17 + t], AF.Ln, [lB[t]], [lB[t]], bias=EPS, scale=1.0 / 128)
                            act(lst[:, 48 + t:49 + t], lst[:, 48 + t:49 + t], AF.Exp, [lB[t]], [lB[t]], scale=-0.5)
                            act(la[:, 0:256], ps[:, 0:256], AF.Copy, [ps, lB[t]], [la], scale=lst[:, 32 + t:33 + t])
                            act(la[:, 256:384], ps[:, 256:384], AF.Copy, [ps, lB[t]], [la], scale=lst[:, 48 + t:49 + t])
                            act(la[:, 384:448], ps[:, 384:448], AF.Copy, [ps], [la])
                            def lat_tail(t=t, la=la):
                                pt_ = PS[4 + t % 4]
                                for c in range(3):
                                    trp(pt_[:, c * 128:(c + 1) * 128], la[:, c * 128:(c + 1) * 128], idf[:, :], [la, idf], [pt_])
                                trp(pt_[0:64, 384:512], la[:, 384:448], idf[:, :], [la, idf], [pt_])
                                for c in range(2):
                                    ts(cqT[:, c, t * 128:(t + 1) * 128], pt_[:, c * 128:(c + 1) * 128], small[:, 2 + 2 * i2 + c:3 + 2 * i2 + c],
                                       None, ALU.mult, None, [pt_, small], [cqT])
                                ts(ckvT[:, 0, t * 128:(t + 1) * 128], pt_[:, 256:384], small[:, 6 + i2:7 + i2], None, ALU.mult, None,
                                   [pt_, small], [ckvT])
                                rope_dst = krT[0:64, t * 128:(t + 1) * 128]
                                rope2(rope_dst, krT, pt_, 384, t * 128, 128, rope)
                            if lat_tails:
                                lat_tails.pop()()
                            lat_tails.append(lat_tail)
                        while lat_tails:
                            lat_tails.pop()()
                    P.barrier()
                    with ExitStack() as s2:
                        qnT = sbt(s2, "qnT", [128, S], BF16)
                        qrT = sbt(s2, "qrT", [128, S], BF16)
                        memset(qrT[64:128, :], 0.0, [qrT], eng="dve")
                        knT = sbt(s2, "knT", [128, S], BF16)
                        VM = sbt(s2, "VM", [128, NT, 128], BF16)
                        ptb = [sbt(s2, "mptb%d" % i, [128, 512], BF16) for i in range(4)]
                        rd = [sbt(s2, "mrd%d" % i, [128, 512], F32) for i in range(2)]
                        scale = 192.0 ** -0.5
                        for hd in range(4):
                            proj_fm(wuq, hd * 192, lambda tg, ps: cp(qnT[:, tg * 512:(tg + 1) * 512], ps[:, :], [ps], [qnT]),
                                    kchunks=2, src=cqT)
                            proj_fm(wuq, hd * 192 + 128,
                                    lambda tg, ps: rope2(qrT[0:64, tg * 512:(tg + 1) * 512], qrT, ps, 0, tg * 512, 512, rope),
                                    kchunks=2, src=cqT, M=64)
                            proj_fm(wukv, hd * 256, lambda tg, ps: cp(knT[:, tg * 512:(tg + 1) * 512], ps[:, :], [ps], [knT]),
                                    kchunks=1, src=ckvT)
                            for tq in range(4):
                                ps = PS[3 + tq % 2]
                                for tt_ in range(4):
                                    t = tq * 4 + tt_
                                    mm(ps[:, tt_ * 128:(tt_ + 1) * 128], ckvT[:, 0, t * 128:(t + 1) * 128],
                                       wukv[:, 0, hd * 256 + 128:hd * 256 + 256], True, True, [ckvT, wukv], [ps])
                                act(VM[:, tq * 4:(tq + 1) * 4, :], ps[:, :].rearrange("p (a b) -> p a b", b=128), AF.Copy, [ps], [VM])
                            mtasks = []
                            for g in range(4):
                                numb, denb = (PS[4], PS[5]) if g % 2 == 0 else (PS[6], PS[7])

                                def epi(g=g, numb=numb, denb=denb, hd=hd):
                                    r_ = rd[g % 2]
                                    recip(r_[:, :], denb[:, :], [denb], [r_])
                                    tt(mixT[:, 4 + hd, g * 512:(g + 1) * 512], numb[:, :], r_[:, :], ALU.mult, [numb, r_], [mixT])
                                mtasks += attn_tasks(
                                    g,
                                    lambda j: knT[:, j * 128:(j + 1) * 128],
                                    lambda g_, c0, N: qnT[:, g_ * 512 + c0:g_ * 512 + c0 + N],
                                    lambda j, g_, c0, N: (krT[:, j * 128:(j + 1) * 128],
                                                          qrT[:, g_ * 512 + c0:g_ * 512 + c0 + N]),
                                    None, lambda j: VM[:, j, :], ones_bf[:, :], numb, denb, None, ptb, scale,
                                    [knT, qnT, krT, qrT], [VM], mla=True, on_done=lambda epi=epi: pending.append(epi))
                            run_pipeline(mtasks, 3)
                            flush_pending()
                P.barrier()
                with ExitStack() as sw:
                    w_out_residual(sw, mixT, owout_d[i2])
                    norm_body(sw, 4 + li, list(range(NT)))
            P.barrier()

        try:
            ckpt("setup")
            for li in range(n_layers):
                if li % 2 == 0:
                    even_mixer(li)
                else:
                    odd_mixer(li)
                ckpt("mixer%d" % li)
                conv_ffn(li)
        except _Stop:
            pass
        P.stopped = False
        P.barrier()

        with ExitStack() as sfin:
            gfin = sbt(sfin, "gfin", [128, D], F32)
            yb = [sbt(sfin, "yb%d" % i, [128, D], F32) for i in range(2)]
            dma("sp", gfin[:, :], fin_d.partition_broadcast(128), [], [gfin])
            for t in range(NT):
                y = yb[t % 2]
                if debug_h:
                    cp(y[:, :], h[:, t, :], [hB[t]], [y])
                else:
                    act(y[:, :], h[:, t, :], AF.Square, [hB[t]], [y, stat], accum=stat[:, t:t + 1])
                    act(stat[:, 16 + t:17 + t], stat[:, t:t + 1], AF.Ln, [stat], [stat], bias=EPS, scale=1.0 / D)
                    act(stat[:, 32 + t:33 + t], stat[:, 16 + t:17 + t], AF.Exp, [stat], [stat], scale=-0.5)
                    stt(y[:, :], h[:, t, :], stat[:, 32 + t:33 + t], gfin[:, :], ALU.mult, ALU.mult, [hB[t], stat, gfin], [y])
                dma("sp", out_d[t * 128:(t + 1) * 128, :], y[:, :], [y], [B_out])
            P.wait_all("sp", [B_out])
            P.barrier()

        with nc.Block() as block:
            P.emit(block)
    return nc


_CACHE = {}


def kernel(**inputs):
    n = 8
    consts = _host_consts()
    if "nc" not in _CACHE:
        _CACHE["nc"] = build_program()
    nc = _CACHE["nc"]
    x = np.ascontiguousarray(np.asarray(inputs["x"], dtype=np.float32))
    shared = {k: np.ascontiguousarray(np.asarray(v, dtype=np.float32)) for k, v in inputs.items() if k != "x"}
    shared.update(consts)
    in_maps = []
    for c in range(n):
        m = dict(shared)
        m["x"] = x[c]
        in_maps.append(m)
    res = run_bass_kernel_spmd(nc, in_maps, core_ids=list(range(n)))
    out = np.stack([np.asarray(r["out"], dtype=np.float32) for r in res.results], axis=0)
    return out
```
